# Optimizing a Trainium2 kernel written in Bass

```python
import math
import jax
import jax.numpy as jnp
from jax import lax
import numpy as np

D_MODEL = 1024
BATCH = 4
SEQ = 8192
DEPTH = 4

GRID_W = 64
CTX_LEN = 256
N_MIXERS = 3
Q_BLOCK = 128
ROPE_THETA = 10000.0
NORM_EPS = 1e-6

A_HEAD_DIM = 64
A_HEADS = D_MODEL // (2 * A_HEAD_DIM)
B_HEADS = D_MODEL // 64
B_NOPE = 64
B_ROPE = 32
B_VDIM = 64
B_Q_LORA = D_MODEL // 4
B_KV_LORA = D_MODEL // 4
C_HEAD_DIM = 64
C_HEADS = D_MODEL // C_HEAD_DIM
NA_ROWS_MAX = 8
NA_COLS = 16
FFN_HIDDEN = ((8 * D_MODEL + 3 * 256 - 1) // (3 * 256)) * 256

kernel_name = 'hybrid_diff_mla_natten_dit_block'


def rmsnorm(x, g):
    xf = x.astype(jnp.float32)
    y = xf * lax.rsqrt(jnp.mean(xf * xf, axis=-1, keepdims=True) + NORM_EPS)
    return (y * g.astype(jnp.float32)).astype(x.dtype)


def swiglu(h, w_gu, w_down):
    g, u = jnp.split(h @ w_gu, 2, axis=-1)
    return (jax.nn.silu(g) * u) @ w_down


def lambda_init(layer_idx):
    return 0.8 - 0.6 * math.exp(-0.3 * layer_idx)


def axial_rope_tables(n_tokens, rot_dim):
    n_freq = rot_dim // 4
    inv = ROPE_THETA ** (-jnp.arange(n_freq, dtype=jnp.float32) / n_freq)
    t = jnp.arange(n_tokens, dtype=jnp.int32)
    row = (t // GRID_W).astype(jnp.float32)
    col = (t % GRID_W).astype(jnp.float32)
    ang = jnp.concatenate([row[:, None] * inv, col[:, None] * inv], axis=-1)
    return jnp.cos(ang), jnp.sin(ang)


def apply_rope(t, cos, sin):
    half = t.shape[-1] // 2
    tf = t.astype(jnp.float32)
    t1, t2 = tf[..., :half], tf[..., half:]
    cs, sn = cos[None, :, None, :], sin[None, :, None, :]
    return jnp.concatenate([t1 * cs - t2 * sn, t1 * sn + t2 * cs], axis=-1).astype(t.dtype)


def sweep_query_blocks(fn, qs):
    Bn, S = qs[0].shape[:2]
    nb = S // Q_BLOCK
    qb = tuple(q.reshape((Bn, nb, Q_BLOCK) + q.shape[2:]).swapaxes(0, 1) for q in qs)
    out = lax.map(lambda a: fn(*a), qb)
    return out.swapaxes(0, 1).reshape((Bn, S) + out.shape[3:])


def diff_attention_mixer(h_lat, h_ctx, w_qkv, w_o, lam_vecs, g_sub, lam_init, cos, sin, with_ctx_out):
    H, d = A_HEADS, A_HEAD_DIM
    scale = d ** -0.5

    def project(h):
        Bn, T, _ = h.shape
        q, k, v = jnp.split(h @ w_qkv, 3, axis=-1)
        q = q.reshape(Bn, T, H, 2, d)
        k = k.reshape(Bn, T, H, 2, d)
        v = v.reshape(Bn, T, H, 2 * d)
        return q[:, :, :, 0], q[:, :, :, 1], k[:, :, :, 0], k[:, :, :, 1], v

    lf = lam_vecs.astype(jnp.float32)
    lam = jnp.exp(jnp.sum(lf[0] * lf[1])) - jnp.exp(jnp.sum(lf[2] * lf[3])) + lam_init

    q1l, q2l, k1l, k2l, vl = project(h_lat)
    q1l, q2l, k1l, k2l = (apply_rope(t, cos, sin) for t in (q1l, q2l, k1l, k2l))
    q1c, q2c, k1c, k2c, vc = project(h_ctx)
    k1 = jnp.concatenate([k1c, k1l], axis=1)
    k2 = jnp.concatenate([k2c, k2l], axis=1)
    v = jnp.concatenate([vc, vl], axis=1)

    def attend(q1, q2, ka, kb, vv):
        s1 = jnp.einsum('bqhd,bkhd->bhqk', q1, ka).astype(jnp.float32) * scale
        s2 = jnp.einsum('bqhd,bkhd->bhqk', q2, kb).astype(jnp.float32) * scale
        p = jax.nn.softmax(s1, axis=-1) - lam * jax.nn.softmax(s2, axis=-1)
        return jnp.einsum('bhqk,bkhe->bqhe', p.astype(vv.dtype), vv)

    def finish(o):
        o = rmsnorm(o, g_sub) * (1.0 - lam_init)
        return o.reshape(o.shape[:2] + (H * 2 * d,)) @ w_o

    o_lat = sweep_query_blocks(lambda a, b: attend(a, b, k1, k2, v), (q1l, q2l))
    y_lat = finish(o_lat)
    y_ctx = finish(attend(q1c, q2c, k1c, k2c, vc)) if with_ctx_out else None
    return y_lat, y_ctx


def mla_mixer(h_lat, h_ctx, w_in, g_q, g_kv, w_uq, w_ukv, w_o, cos, sin, with_ctx_out):
    H = B_HEADS
    scale = (B_NOPE + B_ROPE) ** -0.5

    def project(h):
        Bn, T, _ = h.shape
        z = h @ w_in
        cq = rmsnorm(z[..., :B_Q_LORA], g_q)
        ckv = rmsnorm(z[..., B_Q_LORA:B_Q_LORA + B_KV_LORA], g_kv)
        k_rope = z[..., B_Q_LORA + B_KV_LORA:]
        q = (cq @ w_uq).reshape(Bn, T, H, B_NOPE + B_ROPE)
        kv = (ckv @ w_ukv).reshape(Bn, T, H, B_NOPE + B_VDIM)
        return q[..., :B_NOPE], q[..., B_NOPE:], kv[..., :B_NOPE], k_rope, kv[..., B_NOPE:]

    qn_l, qr_l, kn_l, kr_l, v_l = project(h_lat)
    qr_l = apply_rope(qr_l, cos, sin)
    kr_l = apply_rope(kr_l[:, :, None, :], cos, sin)[:, :, 0]
    qn_c, qr_c, kn_c, kr_c, v_c = project(h_ctx)
    kn = jnp.concatenate([kn_c, kn_l], axis=1)
    kr = jnp.concatenate([kr_c, kr_l], axis=1)
    v = jnp.concatenate([v_c, v_l], axis=1)

    def attend(qn, qr, kn_, kr_, vv):
        s = (jnp.einsum('bqhd,bkhd->bhqk', qn, kn_) + jnp.einsum('bqhr,bkr->bhqk', qr, kr_))
        p = jax.nn.softmax(s.astype(jnp.float32) * scale, axis=-1)
        return jnp.einsum('bhqk,bkhd->bqhd', p.astype(vv.dtype), vv)

    def finish(o):
        return o.reshape(o.shape[:2] + (H * B_VDIM,)) @ w_o

    o_lat = sweep_query_blocks(lambda a, b: attend(a, b, kn, kr, v), (qn_l, qr_l))
    y_lat = finish(o_lat)
    y_ctx = finish(attend(qn_c, qr_c, kn_c, kr_c, v_c)) if with_ctx_out else None
    return y_lat, y_ctx


def neighbourhood_mixer(h_lat, h_ctx, w_qkv, rpb, w_o, rows, with_ctx_out):
    H, d = C_HEADS, C_HEAD_DIM
    scale = d ** -0.5
    kr_win = min(NA_ROWS_MAX, rows)
    kc_win = NA_COLS

    def project(h):
        Bn, T, _ = h.shape
        q, k, v = jnp.split(h @ w_qkv, 3, axis=-1)
        return q.reshape(Bn, T, H, d), k.reshape(Bn, T, H, d), v.reshape(Bn, T, H, d)

    q_l, k_l, v_l = project(h_lat)
    q_c, k_c, v_c = project(h_ctx)
    Bn, S = q_l.shape[:2]
    L = k_c.shape[1]
    q_grid = q_l.reshape(Bn, rows, GRID_W, H, d)
    k_grid = k_l.reshape(Bn, rows, GRID_W, H, d)
    v_grid = v_l.reshape(Bn, rows, GRID_W, H, d)

    qcol = np.arange(GRID_W, dtype=np.int32)
    cstart = np.clip(qcol - kc_win // 2, 0, GRID_W - kc_win)
    col_idx = (cstart[:, None] + np.arange(kc_win, dtype=np.int32)[None, :]).astype(np.int32)
    dc_idx = (col_idx - qcol[:, None] + NA_COLS - 1).astype(np.int32)
    rpb_f = rpb.astype(jnp.float32)

    def row_fn(r):
        rstart = jnp.clip(r - kr_win // 2, 0, rows - kr_win)
        q_r = lax.dynamic_index_in_dim(q_grid, r, axis=1, keepdims=False)
        k_band = lax.dynamic_slice_in_dim(k_grid, rstart, kr_win, axis=1)
        v_band = lax.dynamic_slice_in_dim(v_grid, rstart, kr_win, axis=1)
        k_win = k_band[:, :, col_idx]
        v_win = v_band[:, :, col_idx]
        dr_idx = rstart + jnp.arange(kr_win, dtype=jnp.int32) - r + NA_ROWS_MAX - 1
        bias = rpb_f[:, dr_idx[:, None, None], dc_idx[None, :, :]]
        bias = jnp.transpose(bias, (0, 2, 1, 3))
        s_lat = jnp.einsum('bqhd,brqkhd->bhqrk', q_r, k_win).astype(jnp.float32) * scale + bias[None]
        s_ctx = jnp.einsum('bqhd,bkhd->bhqk', q_r, k_c).astype(jnp.float32) * scale
        s = jnp.concatenate([s_ctx, s_lat.reshape(Bn, H, GRID_W, kr_win * kc_win)], axis=-1)
        p = jax.nn.softmax(s, axis=-1).astype(v_win.dtype)
        p_lat = p[..., L:].reshape(Bn, H, GRID_W, kr_win, kc_win)
        return (jnp.einsum('bhqk,bkhd->bqhd', p[..., :L], v_c)
                + jnp.einsum('bhqrk,brqkhd->bqhd', p_lat, v_win))

    o = lax.map(row_fn, jnp.arange(rows, dtype=jnp.int32))
    o_lat = jnp.transpose(o, (1, 0, 2, 3, 4)).reshape(Bn, S, H * d)
    y_lat = o_lat @ w_o
    y_ctx = None
    if with_ctx_out:
        s = jnp.einsum('bqhd,bkhd->bhqk', q_c, k_c).astype(jnp.float32) * scale
        p = jax.nn.softmax(s, axis=-1).astype(v_c.dtype)
        o_c = jnp.einsum('bhqk,bkhd->bqhd', p, v_c)
        y_ctx = o_c.reshape(o_c.shape[:2] + (H * d,)) @ w_o
    return y_lat, y_ctx


def setup_inputs(seed: int = 0) -> dict:
    key = jax.random.key(seed)
    keys = iter(jax.random.split(key, 64))

    def nrm(shape, scale):
        return jax.random.normal(next(keys), shape, jnp.float32) * scale

    def gain(shape):
        return 1.0 + nrm(shape, 0.05)

    D = D_MODEL
    inp = {}
    inp['x'] = nrm((BATCH, SEQ, D), 1.0)
    inp['c'] = nrm((BATCH, D), 1.0)
    inp['ctx'] = nrm((BATCH, CTX_LEN, D), 1.0)
    inp['c_ctx'] = nrm((D,), 1.0)
    for i in range(DEPTH):
        p = 'l%d_' % i
        inp[p + 'w_mod'] = nrm((D, 6 * D), D ** -0.5)
        inp[p + 'b_mod'] = nrm((6 * D,), 0.02)
        inp[p + 'g_norm'] = gain((4, D))
        inp[p + 'w_gu'] = nrm((D, 2 * FFN_HIDDEN), D ** -0.5)
        inp[p + 'w_down'] = nrm((FFN_HIDDEN, D), FFN_HIDDEN ** -0.5)
        kind = i % N_MIXERS
        if kind == 0:
            wa = 2 * A_HEADS * A_HEAD_DIM
            inp[p + 'a_w_qkv'] = nrm((D, 3 * wa), D ** -0.5)
            inp[p + 'a_w_o'] = nrm((wa, D), wa ** -0.5)
            inp[p + 'a_lam'] = nrm((4, A_HEAD_DIM), 0.1)
            inp[p + 'a_g_sub'] = gain((2 * A_HEAD_DIM,))
        elif kind == 1:
            inp[p + 'b_w_in'] = nrm((D, B_Q_LORA + B_KV_LORA + B_ROPE), D ** -0.5)
            inp[p + 'b_g_q'] = gain((B_Q_LORA,))
            inp[p + 'b_g_kv'] = gain((B_KV_LORA,))
            inp[p + 'b_w_uq'] = nrm((B_Q_LORA, B_HEADS * (B_NOPE + B_ROPE)), B_Q_LORA ** -0.5)
            inp[p + 'b_w_ukv'] = nrm((B_KV_LORA, B_HEADS * (B_NOPE + B_VDIM)), B_KV_LORA ** -0.5)
            inp[p + 'b_w_o'] = nrm((B_HEADS * B_VDIM, D), (B_HEADS * B_VDIM) ** -0.5)
        else:
            wc = C_HEADS * C_HEAD_DIM
            inp[p + 'c_w_qkv'] = nrm((D, 3 * wc), D ** -0.5)
            inp[p + 'c_rpb'] = nrm((C_HEADS, 2 * NA_ROWS_MAX - 1, 2 * NA_COLS - 1), 0.5)
            inp[p + 'c_w_o'] = nrm((wc, D), wc ** -0.5)
    return inp


def reference(x, c, ctx, c_ctx,
              l0_w_mod, l0_b_mod, l0_g_norm, l0_w_gu, l0_w_down,
              l0_a_w_qkv, l0_a_w_o, l0_a_lam, l0_a_g_sub,
              l1_w_mod, l1_b_mod, l1_g_norm, l1_w_gu, l1_w_down,
              l1_b_w_in, l1_b_g_q, l1_b_g_kv, l1_b_w_uq, l1_b_w_ukv, l1_b_w_o,
              l2_w_mod, l2_b_mod, l2_g_norm, l2_w_gu, l2_w_down,
              l2_c_w_qkv, l2_c_rpb, l2_c_w_o,
              l3_w_mod, l3_b_mod, l3_g_norm, l3_w_gu, l3_w_down,
              l3_a_w_qkv, l3_a_w_o, l3_a_lam, l3_a_g_sub):
    S = x.shape[1]
    rows = S // GRID_W
    common = [
        (l0_w_mod, l0_b_mod, l0_g_norm, l0_w_gu, l0_w_down),
        (l1_w_mod, l1_b_mod, l1_g_norm, l1_w_gu, l1_w_down),
        (l2_w_mod, l2_b_mod, l2_g_norm, l2_w_gu, l2_w_down),
        (l3_w_mod, l3_b_mod, l3_g_norm, l3_w_gu, l3_w_down),
    ]
    mixer_params = [
        (l0_a_w_qkv, l0_a_w_o, l0_a_lam, l0_a_g_sub),
        (l1_b_w_in, l1_b_g_q, l1_b_g_kv, l1_b_w_uq, l1_b_w_ukv, l1_b_w_o),
        (l2_c_w_qkv, l2_c_rpb, l2_c_w_o),
        (l3_a_w_qkv, l3_a_w_o, l3_a_lam, l3_a_g_sub),
    ]
    cos_a, sin_a = axial_rope_tables(S, A_HEAD_DIM)
    cos_b, sin_b = axial_rope_tables(S, B_ROPE)

    for i in range(DEPTH):
        w_mod, b_mod, g_norm, w_gu, w_down = common[i]
        last = i == DEPTH - 1
        m_lat = (jax.nn.silu(c) @ w_mod + b_mod)[:, None, :]
        m_ctx = jax.nn.silu(c_ctx) @ w_mod + b_mod
        sh1, sc1, g1, sh2, sc2, g2 = jnp.split(m_lat, 6, axis=-1)
        csh1, csc1, cg1, csh2, csc2, cg2 = jnp.split(m_ctx, 6, axis=-1)

        h_lat = rmsnorm(x, g_norm[0]) * (1.0 + sc1) + sh1
        h_ctx = rmsnorm(ctx, g_norm[0]) * (1.0 + csc1) + csh1
        kind = i % N_MIXERS
        if kind == 0:
            y_lat, y_ctx = diff_attention_mixer(h_lat, h_ctx, *mixer_params[i], lambda_init(i),
                                                cos_a, sin_a, not last)
        elif kind == 1:
            y_lat, y_ctx = mla_mixer(h_lat, h_ctx, *mixer_params[i], cos_b, sin_b, not last)
        else:
            y_lat, y_ctx = neighbourhood_mixer(h_lat, h_ctx, *mixer_params[i], rows, not last)

        x = x + g1 * rmsnorm(y_lat, g_norm[1])
        f_lat = swiglu(rmsnorm(x, g_norm[2]) * (1.0 + sc2) + sh2, w_gu, w_down)
        x = x + g2 * rmsnorm(f_lat, g_norm[3])

        if not last:
            ctx = ctx + cg1 * rmsnorm(y_ctx, g_norm[1])
            f_ctx = swiglu(rmsnorm(ctx, g_norm[2]) * (1.0 + csc2) + csh2, w_gu, w_down)
            ctx = ctx + cg2 * rmsnorm(f_ctx, g_norm[3])
    return x
```

```python
import numpy as np
import concourse.bass as bass
import concourse.mybir as mybir

F32 = mybir.dt.float32
BF16 = mybir.dt.bfloat16
AF = mybir.ActivationFunctionType
ALU = mybir.AluOpType
AX = mybir.AxisListType

COMPUTE = ("pe", "act", "dve", "pool")
ENGINES = ("pe", "act", "dve", "pool", "sp")


class Buf:
    __slots__ = ("name", "last_w", "readers", "dsem", "is_dram")

    def __init__(self, name):
        self.name = name
        self.last_w = None
        self.readers = {}
        self.dsem = {}


class View:
    __slots__ = ("buf", "ap")

    def __init__(self, buf, ap):
        self.buf = buf
        self.ap = ap


class Tile:
    def __init__(self, handle, name, buf=None):
        self.h = handle
        self.buf = buf if buf is not None else Buf(name)

    def __getitem__(self, idx):
        return View(self.buf, self.h[idx])

    def v(self, ap):
        return View(self.buf, ap)

    def sub(self, name):
        return Tile(self.h, name)


class Op:
    __slots__ = ("eng", "fn", "reads", "writes", "is_dma", "deps", "needs_inc", "tick",
                 "dgroup", "dtarget", "idx", "done", "lhs_buf")


class Prog:
    def __init__(self, nc):
        self.nc = nc
        self.ops = []
        self.sem = {e: nc.alloc_semaphore("tick_" + e) for e in COMPUTE}
        self.tick = {e: 0 for e in COMPUTE}
        self.dsem_pool = {"hw": [], "sw": []}
        self.dsem_all = []
        self.known = {e: {} for e in ENGINES}
        self.n_emitted = 0
        self._phase_dsems = []
        self.same_engine_sync = True

    def _add(self, eng, fn, reads, writes, is_dma=False, dgroup=None):
        op = Op()
        op.eng = eng
        op.fn = fn
        op.reads = [v.buf for v in reads if v is not None]
        op.writes = [v.buf for v in writes if v is not None]
        op.is_dma = is_dma
        op.needs_inc = False
        op.tick = None
        op.dgroup = dgroup
        op.dtarget = None
        op.idx = len(self.ops)
        op.done = False
        op.lhs_buf = None
        deps = set()
        for b in op.reads:
            if b.last_w is not None:
                deps.add(b.last_w)
        for b in op.writes:
            lw = b.last_w
            if lw is not None:
                if is_dma and lw.is_dma and not b.readers and lw.dgroup is dgroup:
                    deps |= lw.deps
                else:
                    deps.add(lw)
            for r in b.readers.values():
                deps.add(r)
        deps.discard(op)
        deps = {d for d in deps if not d.done}
        op.deps = deps
        for b in op.reads:
            b.readers[("d", op.idx) if is_dma else eng] = op
        for b in op.writes:
            b.last_w = op
            b.readers = {}
        self.ops.append(op)
        return op

    def _dsem(self, buf, cls):
        if cls not in buf.dsem:
            pool = self.dsem_pool[cls]
            if pool:
                ent = pool.pop()
            else:
                ent = [self.nc.alloc_semaphore("dsem_%s%d" % (cls, len(self.dsem_all))), 0, cls]
                self.dsem_all.append(ent)
            buf.dsem[cls] = ent
            self._phase_dsems.append((buf, cls))
        return buf.dsem[cls]

    def flush(self, name=None):
        nc = self.nc
        ops = self.ops
        if not ops:
            return
        for op in ops:
            for d in op.deps:
                if not d.is_dma:
                    if d.eng == op.eng and (d.eng == "pe" or not self.same_engine_sync):
                        continue
                    d.needs_inc = True
        last = {}
        for op in ops:
            if not op.is_dma:
                last[op.eng] = op
        for e, op in last.items():
            op.needs_inc = True
        for op in ops:
            if op.is_dma:
                ent = op.dgroup
                ent[1] += 16
                op.dtarget = ent[1]
            elif op.needs_inc:
                self.tick[op.eng] += 1
                op.tick = self.tick[op.eng]
        final_tick = dict(self.tick)
        final_dsem = [(ent[0], ent[1]) for ent in self.dsem_all]

        per_eng = {e: [] for e in ENGINES}
        for op in ops:
            per_eng[op.eng].append(op)

        sem = self.sem
        known = self.known

        def emit_engine(ename, eng):
            kn = known[ename]

            def wait(s, val):
                key = id(s)
                if kn.get(key, 0) >= val:
                    return
                eng.wait_ge(s, val)
                kn[key] = val

            for op in per_eng[ename]:
                need = {}
                lhs_keys = set()
                lb = op.lhs_buf
                for d in op.deps:
                    if d.is_dma:
                        ent = d.dgroup
                        k = ("d", id(ent))
                        if k not in need or need[k][1] < d.dtarget:
                            need[k] = (ent[0], d.dtarget)
                    else:
                        if d.eng == ename and (ename == "pe" or not self.same_engine_sync):
                            continue
                        k = ("t", d.eng)
                        if k not in need or need[k][1] < d.tick:
                            need[k] = (sem[d.eng], d.tick)
                    if lb is not None and lb in d.writes:
                        lhs_keys.add(k)
                pend = [(k, s_, val) for k, (s_, val) in need.items() if kn.get(id(s_), 0) < val]
                attach = None
                if pend and not op.is_dma:
                    for i_, (k, s_, val) in enumerate(pend):
                        if ename != "pe" or k not in lhs_keys:
                            attach = (s_, val)
                            pend.pop(i_)
                            break
                for k, s_, val in pend:
                    wait(s_, val)
                ins = op.fn(eng)
                if attach is not None:
                    ins._wait_ge(attach[0], attach[1])
                    kn[id(attach[0])] = attach[1]
                if op.is_dma:
                    ins.then_inc(op.dgroup[0], 16)
                elif op.needs_inc:
                    ins.then_inc(sem[ename], 1)
                    if ename in kn and False:
                        pass
            for e2 in COMPUTE:
                if e2 != ename and final_tick[e2] > 0:
                    wait(sem[e2], final_tick[e2])
            for s, val in final_dsem:
                if val > 0:
                    wait(s, val)

        with nc.Block() as block:
            @block.sync
            def _(e):
                emit_engine("sp", e)

            @block.tensor
            def _(e):
                emit_engine("pe", e)

            @block.scalar
            def _(e):
                emit_engine("act", e)

            @block.vector
            def _(e):
                emit_engine("dve", e)

            @block.gpsimd
            def _(e):
                emit_engine("pool", e)

        self.n_emitted += len(ops)
        for op in ops:
            op.done = True
            op.fn = None
            op.deps = ()
        for b, cls in self._phase_dsems:
            self.dsem_pool[cls].append(b.dsem.pop(cls))
        self._phase_dsems = []
        self.ops = []

    def mm(self, out, lhsT, rhs, start=True, stop=True, **kw):
        op = self._add("pe", lambda e: e.matmul(out.ap, lhsT.ap, rhs.ap, start=start, stop=stop, **kw),
                       [lhsT, rhs] + ([] if start else [out]), [out])
        op.lhs_buf = lhsT.buf
        return op

    def transpose(self, out, in_, ident):
        op = self._add("pe", lambda e: e.transpose(out.ap, in_.ap, ident.ap), [in_, ident], [out])
        op.lhs_buf = in_.buf
        return op

    def act(self, out, in_, func, bias=None, scale=1.0, accum_out=None):
        rd = [in_]
        kw = {}
        if bias is not None:
            if isinstance(bias, View):
                rd.append(bias)
                kw["bias"] = bias.ap
            else:
                kw["bias"] = bias
        if isinstance(scale, View):
            rd.append(scale)
            kw["scale"] = scale.ap
        else:
            kw["scale"] = scale
        wr = [out]
        if accum_out is not None:
            wr.append(accum_out)
            kw["accum_out"] = accum_out.ap
        return self._add("act", lambda e: e.activation(out.ap, in_.ap, func, **kw), rd, wr)

    def tt(self, eng, out, in0, in1, op):
        return self._add(eng, lambda e: e.tensor_tensor(out.ap, in0.ap, in1.ap, op), [in0, in1], [out])

    def ts(self, eng, out, in0, s1, op0, s2=None, op1=None, accum_out=None):
        rd = [in0]
        a1 = s1
        a2 = s2
        if isinstance(s1, View):
            rd.append(s1)
            a1 = s1.ap
        if isinstance(s2, View):
            rd.append(s2)
            a2 = s2.ap
        wr = [out]
        kw = {}
        if op1 is not None:
            kw["op1"] = op1
        if accum_out is not None:
            wr.append(accum_out)
            kw["accum_out"] = accum_out.ap
        return self._add(eng, lambda e: e.tensor_scalar(out.ap, in0.ap, a1, a2, op0, **kw), rd, wr)

    def stt(self, eng, out, in0, scalar, in1, op0, op1):
        rd = [in0, in1]
        a = scalar
        if isinstance(scalar, View):
            rd.append(scalar)
            a = scalar.ap
        return self._add(eng, lambda e: e.scalar_tensor_tensor(out.ap, in0.ap, a, in1.ap, op0, op1), rd, [out])

    def copy(self, eng, out, in_):
        if eng == "act":
            return self._add("act", lambda e: e.copy(out.ap, in_.ap), [in_], [out])
        return self._add(eng, lambda e: e.tensor_copy(out.ap, in_.ap), [in_], [out])

    def memset(self, eng, out, val):
        return self._add(eng, lambda e: e.memset(out.ap, val), [], [out])

    def recip(self, out, in_):
        return self._add("dve", lambda e: e.reciprocal(out.ap, in_.ap), [in_], [out])

    def reduce(self, eng, out, in_, op, axis=AX.X):
        return self._add(eng, lambda e: e.tensor_reduce(out.ap, in_.ap, axis, op), [in_], [out])

    def dma(self, q, out, in_, **kw):
        cls = "sw" if q == "pool" else "hw"
        if isinstance(out, View):
            grp = self._dsem(out.buf, cls)
            return self._add(q, lambda e: e.dma_start(out.ap, in_, **kw), [], [out], is_dma=True, dgroup=grp)
        grp = self._dsem(in_.buf, cls)
        return self._add(q, lambda e: e.dma_start(out, in_.ap, **kw), [in_], [], is_dma=True, dgroup=grp)

import math
from contextlib import ExitStack
from concourse.bass_utils import run_bass_kernel_spmd

D = 1024
CTX = 256
FF = 2816
EPS = 1e-6
GRID_W = 64
NEG = -30000.0
KINDS = (0, 1, 2, 0)


def lambda_init(i):
    return 0.8 - 0.6 * math.exp(-0.3 * i)


class Ctx:
    pass


def rsqrt(P, out, in_, mul):
    P.ts("dve", out, in_, mul, ALU.mult, EPS, ALU.add)
    P.act(out, out, AF.Sqrt)
    P.recip(out, out)


_UID = [0]


def sbt(nc, es, name, shape, dtype):
    _UID[0] += 1
    name = "%s_u%d" % (name, _UID[0])
    return Tile(es.enter_context(nc.sbuf_tensor(name, list(shape), dtype)), name)


def build(S, kinds=KINDS):
    NL = len(kinds)
    KINDS_ = kinds
    TOK = CTX + S
    NK = TOK
    ROWS = S // GRID_W
    nc = bass.Bass("TRN2", target_bir_lowering=False)
    g = Ctx()
    g.nc = nc
    g.S, g.TOK, g.NK, g.ROWS, g.NL = S, TOK, NK, ROWS, NL

    def din(name, shape, dtype=F32):
        return nc.dram_tensor(name, list(shape), dtype, kind="ExternalInput").ap()

    def dscr(name, shape, dtype):
        return nc.dram_tensor(name, list(shape), dtype, kind="Internal").ap()

    g.xc = din("xc", [TOK, D])
    g.cc = din("cc", [2, D])
    g.ident_d = din("ident", [128, 128])
    g.ropeA = din("ropeA", [2, 128, TOK])
    g.ropeB = din("ropeB", [2, 128, TOK])
    g.W = []
    for l in range(NL):
        p = "l%d_" % l
        w = {}
        w["w_mod"] = din(p + "w_mod", [D, 6 * D])
        w["b_mod"] = din(p + "b_mod", [6 * D])
        w["g_norm"] = din(p + "g_norm", [4, D])
        w["w_gu"] = din(p + "w_gu", [D, 2 * FF])
        w["w_down"] = din(p + "w_down", [FF, D])
        k = KINDS_[l]
        if k == 0:
            w["w_qkv"] = din(p + "a_w_qkv", [D, 3 * D])
            w["w_o"] = din(p + "a_w_o", [D, D])
            w["lam"] = din(p + "a_lam", [4, 64])
            w["g_sub"] = din(p + "a_g_sub", [128])
        elif k == 1:
            w["w_in"] = din(p + "b_w_in", [D, 544])
            w["g_q"] = din(p + "b_g_q", [256])
            w["g_kv"] = din(p + "b_g_kv", [256])
            w["w_uq"] = din(p + "b_w_uq", [256, 1536])
            w["w_ukv"] = din(p + "b_w_ukv", [256, 2048])
            w["w_o"] = din(p + "b_w_o", [D, D])
        else:
            w["w_qkv"] = din(p + "c_w_qkv", [D, 3 * D])
            w["rpb_tab"] = din(p + "c_rpb_tab", [16, 15, 64, 64])
            w["w_o"] = din(p + "c_w_o", [D, D])
        g.W.append(w)
    g.y = nc.dram_tensor("y", [S, D], F32, kind="ExternalOutput").ap()

    g.xres = dscr("xres", [TOK, D], F32)
    g.hT = dscr("hT", [D, TOK], BF16)
    g.qT = dscr("qT", [1536, TOK], BF16)
    g.kT = dscr("kT", [1056, NK], BF16)
    g.vv = dscr("vv", [NK, 1040], BF16)
    g.oT = dscr("oT", [D, TOK], BF16)

    g.blocks = [(0, CTX, True)] + [(CTX + i * 512, 512, False) for i in range(S // 512)]

    P = Prog(nc)
    g.P = P
    g.ident_f = Tile(nc.alloc_sbuf_tensor("ident_f", [128, 128], F32), "ident_f")
    g.ident_b = Tile(nc.alloc_sbuf_tensor("ident_b", [128, 128], BF16), "ident_b")
    g.ident8 = Tile(nc.alloc_sbuf_tensor("ident8", [128, 128], BF16), "ident8")
    g.ones_f = Tile(nc.alloc_sbuf_tensor("ones_f", [128, 128], F32), "ones_f")
    g.sel65 = Tile(nc.alloc_sbuf_tensor("sel65", [65, 64], F32), "sel65")
    g.MV = [Tile(nc.alloc_sbuf_tensor("MV%d" % l, [128, 48, 2], F32), "MV%d" % l) for l in range(NL)]
    g.DV = [Tile(nc.alloc_sbuf_tensor("DV%d" % l, [128, 4, 8, 2], F32), "DV%d" % l) for l in range(NL)]
    g.gn = [Tile(nc.alloc_sbuf_tensor("gn%d" % l, [128, 4, 8], F32), "gn%d" % l) for l in range(NL)]
    g.psall = nc.alloc_psum_tensor("psall", [128, 8, 512], F32)

    with nc.allow_non_contiguous_dma(reason="small strided parameter loads"):
        phase_init(g)
        phase_mod(g)
        phase_norm(g, 0, first=True)
        for l in range(NL):
            k = KINDS_[l]
            last = (l == NL - 1)
            if k == 0:
                phase_projA(g, l)
                phase_attnA(g, l)
            elif k == 1:
                phase_projB(g, l)
                phase_attnB(g, l)
            else:
                phase_projC(g, l)
                phase_attnC(g, l)
            phase_post(g, l)
            phase_ffn(g, l, final=(l == NL - 1), last=last)
    import sys as _sys
    print("[build] S=%d kinds=%s ops=%d instructions=%d" % (S, str(kinds), P.n_emitted, nc.n_instructions()), file=_sys.stderr)
    return nc


def bank(g, i, name, dtype=None, n=1):
    ap = g.psall[:, i:i + n, :].rearrange("p a b -> p (a b)")
    if dtype is not None:
        ap = ap.bitcast(dtype)
    return Tile(ap, name)


def phase_init(g):
    P, nc = g.P, g.nc
    P.dma("sp", g.ident_f[:], g.ident_d)
    P.copy("dve", g.ident_b[:], g.ident_f[:])
    P.ts("dve", g.ident8[:], g.ident_f[:], 8.0, ALU.mult)
    P.memset("dve", g.ones_f[:], 1.0)
    P.memset("dve", g.sel65[:], 0.0)
    P.memset("dve", g.sel65[64:65, :], 1.0)
    P.flush()


def phase_mod(g):
    P, nc = g.P, g.nc
    with ExitStack() as es:
        cs = sbt(nc, es, "cs", [128, 8, 2], F32)
        sT = sbt(nc, es, "sT", [128, 8, 2], F32)
        wm = [sbt(nc, es, "wm%d" % i, [128, 8, 1024], F32) for i in range(2)]
        bT = [sbt(nc, es, "bT%d" % i, [128, 48], F32) for i in range(2)]
        tmp = sbt(nc, es, "tmpm", [128, 8], F32)
        pst = [bank(g, i, "psm%d" % i) for i in range(4)]
        for r in range(2):
            P.dma("sp", cs[:, :, r], g.cc[r, :].rearrange("(k p) -> p k", p=128))
        P.act(sT[:], cs[:], AF.Silu)
        n = 0
        for l in range(g.NL):
            w = g.W[l]
            b = bT[l % 2]
            P.dma("sp", b[:], w["b_mod"].rearrange("(j p) -> p j", p=128))
            for r in range(4):
                P.dma("sp", g.gn[l][:, r, :], w["g_norm"][r, :].rearrange("(c p) -> p c", p=128))
            for nb in range(6):
                wt = wm[n % 2]
                n += 1
                for k in range(8):
                    P.dma("sp" if k % 2 == 0 else "act", wt[:, k, :],
                          w["w_mod"][k * 128:(k + 1) * 128, nb * 1024:(nb + 1) * 1024])
                for j in range(8):
                    ps = pst[j % 4]
                    for k in range(8):
                        P.mm(ps[:, 0:2], wt[:, k, j * 128:(j + 1) * 128], sT[:, k, :],
                             start=(k == 0), stop=(k == 7))
                    P.ts("dve", g.MV[l][:, nb * 8 + j, :], ps[:, 0:2], b[:, nb * 8 + j:nb * 8 + j + 1], ALU.add)
            MV, DV, gn = g.MV[l], g.DV[l], g.gn[l]
            for r in range(2):
                P.stt("dve", DV[:, 0, :, r], MV[:, 8:16, r], 1.0, gn[:, 0, :], ALU.add, ALU.mult)
                P.stt("dve", DV[:, 1, :, r], MV[:, 32:40, r], 1.0, gn[:, 2, :], ALU.add, ALU.mult)
                P.tt("dve", DV[:, 2, :, r], MV[:, 16:24, r], gn[:, 1, :], ALU.mult)
                P.tt("dve", DV[:, 3, :, r], MV[:, 40:48, r], gn[:, 3, :], ALU.mult)
        P.flush()


class NormT:
    def __init__(self, g, es, bank_ids):
        nc = g.nc
        self.g = g
        self.junk = [sbt(nc, es, "nt_junk%d" % i, [128, 1024], BF16) for i in range(2)]
        self.ss = [sbt(nc, es, "nt_ss%d" % i, [128, 1], F32) for i in range(2)]
        self.rstd = [sbt(nc, es, "nt_rstd%d" % i, [128, 1], F32) for i in range(2)]
        self.xn = [sbt(nc, es, "nt_xn%d" % i, [128, 1024], BF16) for i in range(2)]
        self.pT = [bank(g, b, "nt_pT%d" % b, BF16) for b in bank_ids]
        self.n = 0

    def __call__(self, xt, l, which, r, hblk, col):
        g, P = self.g, self.g.P
        i = self.n % 2
        pT = self.pT[self.n % len(self.pT)]
        self.n += 1
        junk, ss, rstd, xn = self.junk[i], self.ss[i], self.rstd[i], self.xn[i]
        P.memset("dve", ss[:], 0.0)
        P.act(junk[:], xt, AF.Square, accum_out=ss[:])
        rsqrt(P, rstd[:], ss[:], 1.0 / D)
        P.ts("pool", xn[:], xt, rstd[:, 0:1], ALU.mult)
        for c in range(8):
            P.transpose(pT[:, c * 128:(c + 1) * 128], xn[:, c * 128:(c + 1) * 128], g.ident_b[:])
        A = g.DV[l]
        MV = g.MV[l]
        boff = 0 if which == 0 else 24
        for c in range(8):
            P.act(hblk[:, c, col:col + 128], pT[:, c * 128:(c + 1) * 128], AF.Identity,
                  bias=MV[:, boff + c, r:r + 1], scale=A[:, which, c, r:r + 1])


def phase_norm(g, l, first=False):
    P, nc = g.P, g.nc
    with ExitStack() as es:
        nt = NormT(g, es, [0, 1])
        xb = [sbt(nc, es, "pn_x%d" % i, [128, 1024], F32) for i in range(3)]
        hb = [sbt(nc, es, "pn_h%d" % i, [128, 8, 512], BF16) for i in range(2)]
        n = 0
        for bi, (t0, w, isctx) in enumerate(g.blocks):
            h = hb[bi % 2]
            for i in range(w // 128):
                xt = xb[n % 3]
                n += 1
                P.dma("sp", xt[:], g.xc[t0 + i * 128:t0 + (i + 1) * 128, :])
                nt(xt[:], l, 0, 1 if isctx else 0, h, i * 128)
            P.dma("pool", g.hT[:, t0:t0 + w].rearrange("(c p) t -> p c t", p=128), h[:, :, 0:w])
        P.flush()


def load_w_cast(P, tile, src, nk, c0, c1, q="pool"):
    for k in range(nk):
        P.dma(q, tile[:, k, 0:c1 - c0], src[k * 128:(k + 1) * 128, c0:c1], max_dma_last_dim=4096)


def load_w_cast_swapped(P, tile, src, nk, c0, ngroups, half, q="pool"):
    gw = 2 * half
    for k in range(nk):
        dst = tile.h[:, k, 0:ngroups * gw].rearrange("p (g two i) -> p g two i", two=2, i=half)
        s = src[k * 128:(k + 1) * 128, c0:c0 + ngroups * gw].rearrange("p (g two i) -> p g two i", two=2, i=half)
        P.dma(q, tile.v(dst[:, :, 0, :]), s[:, :, 1, :], max_dma_last_dim=4096)
        P.dma(q, tile.v(dst[:, :, 1, :]), s[:, :, 0, :], max_dma_last_dim=4096)


def phase_projA(g, l):
    P, nc, w = g.P, g.nc, g.W[l]
    with ExitStack() as es:
        wq = sbt(nc, es, "wq", [128, 8, 1024], BF16)
        wqp = sbt(nc, es, "wqp", [128, 8, 1024], BF16)
        wk = sbt(nc, es, "wk", [128, 8, 1024], BF16)
        wkp = sbt(nc, es, "wkp", [128, 8, 1024], BF16)
        wv = sbt(nc, es, "wv", [128, 8, 1024], BF16)
        hb = [sbt(nc, es, "hb%d" % i, [128, 8, 512], BF16) for i in range(2)]
        rp = [sbt(nc, es, "rp%d" % i, [128, 2, 512], F32) for i in range(2)]
        t1 = [sbt(nc, es, "t1_%d" % i, [128, 512], F32) for i in range(2)]
        t2 = [sbt(nc, es, "t2_%d" % i, [128, 512], F32) for i in range(2)]
        ro = [sbt(nc, es, "ro%d" % i, [128, 512], BF16) for i in range(2)]
        vt = [sbt(nc, es, "vt%d" % i, [128, 1024], BF16) for i in range(2)]
        psA = [bank(g, i, "psA%d" % i) for i in (0, 1)]
        psB = [bank(g, i, "psB%d" % i) for i in (2, 3)]
        psV = [bank(g, i, "psV%d" % i) for i in (4, 5)]
        load_w_cast(P, wq, w["w_qkv"], 8, 0, 1024)
        load_w_cast_swapped(P, wqp, w["w_qkv"], 8, 0, 16, 32)
        load_w_cast(P, wk, w["w_qkv"], 8, 1024, 2048)
        load_w_cast_swapped(P, wkp, w["w_qkv"], 8, 1024, 16, 32)
        load_w_cast(P, wv, w["w_qkv"], 8, 2048, 3072)
        n = 0
        m = 0
        for bi, (t0, wd, isctx) in enumerate(g.blocks):
            h = hb[bi % 2]
            r = rp[bi % 2]
            P.dma("sp", h[:, :, 0:wd], g.hT[:, t0:t0 + wd].rearrange("(c p) t -> p c t", p=128))
            P.dma("sp", r[:, :, 0:wd], g.ropeA[:, :, t0:t0 + wd].rearrange("a p t -> p a t"))
            for (wt, wtp, dst) in ((wq, wqp, g.qT), (wk, wkp, g.kT)):
                for fc in range(8):
                    pa, pb = psA[n % 2], psB[n % 2]
                    a1, a2, o = t1[n % 2], t2[n % 2], ro[n % 2]
                    n += 1
                    for kc in range(8):
                        P.mm(pa[:, 0:wd], wt[:, kc, fc * 128:(fc + 1) * 128], h[:, kc, 0:wd], start=(kc == 0), stop=(kc == 7))
                    for kc in range(8):
                        P.mm(pb[:, 0:wd], wtp[:, kc, fc * 128:(fc + 1) * 128], h[:, kc, 0:wd], start=(kc == 0), stop=(kc == 7))
                    P.tt("dve", a1[:, 0:wd], pa[:, 0:wd], r[:, 0, 0:wd], ALU.mult)
                    P.tt("dve", a2[:, 0:wd], pb[:, 0:wd], r[:, 1, 0:wd], ALU.mult)
                    P.tt("pool", o[:, 0:wd], a1[:, 0:wd], a2[:, 0:wd], ALU.add)
                    P.dma("pool", dst[fc * 128:(fc + 1) * 128, t0:t0 + wd], o[:, 0:wd])
            for i in range(wd // 128):
                v = vt[m % 2]
                for half in range(2):
                    pv = psV[(2 * m + half) % 2]
                    for kc in range(8):
                        P.mm(pv[:, :], h[:, kc, i * 128:(i + 1) * 128], wv[:, kc, half * 512:(half + 1) * 512],
                             start=(kc == 0), stop=(kc == 7))
                    P.copy("act", v[:, half * 512:(half + 1) * 512], pv[:, :])
                m += 1
                P.dma("pool", g.vv[t0 + i * 128:t0 + (i + 1) * 128, 0:1024], v[:])
        P.flush()


def phase_attnA(g, l):
    P, nc, w = g.P, g.nc, g.W[l]
    NK = g.NK
    NKC = NK // 128
    scale = 64 ** -0.5
    li = lambda_init(l)
    with ExitStack() as es:
        kh = [sbt(nc, es, "kh%d" % i, [64, 2, NK], BF16) for i in range(2)]
        vh = [sbt(nc, es, "vh%d" % i, [128, NKC, 128], BF16) for i in range(2)]
        qh = [sbt(nc, es, "qh%d" % i, [64, 2, 512], BF16) for i in range(2)]
        ee = [sbt(nc, es, "ee_%d" % i, [128, 2, 512], BF16) for i in range(3)]
        accD = [sbt(nc, es, "accD_%d" % i, [128, 2, 512], F32) for i in range(2)]
        accP = [sbt(nc, es, "accP_%d" % i, [128, 2, 512], F32) for i in range(2)]
        rc1 = sbt(nc, es, "rc1", [128, 512], F32)
        rc2 = sbt(nc, es, "rc2", [128, 512], F32)
        u1 = sbt(nc, es, "u1", [128, 512], F32)
        u2 = sbt(nc, es, "u2", [128, 512], F32)
        oo = sbt(nc, es, "oo", [128, 512], F32)
        sq = sbt(nc, es, "sq", [128, 512], F32)
        rs = sbt(nc, es, "rs", [128, 512], F32)
        ob = [sbt(nc, es, "ob%d" % i, [128, 512], BF16) for i in range(2)]
        lt = sbt(nc, es, "lt", [1, 4, 64], F32)
        pr = sbt(nc, es, "pr", [1, 2, 64], F32)
        sm = sbt(nc, es, "sm", [1, 2], F32)
        ex = sbt(nc, es, "ex", [1, 2], F32)
        l1 = sbt(nc, es, "l1", [1, 2], F32)
        neglam = sbt(nc, es, "neglam", [128, 2], F32)
        gs = sbt(nc, es, "gs", [128, 1], F32)
        ps_s = [Tile(g.psall[:, 0:2, :], "ps_sA0"), Tile(g.psall[:, 2:4, :], "ps_sA1")]
        ps_o1 = bank(g, 4, "ps_o1")
        ps_o2 = bank(g, 5, "ps_o2")
        ps_r = [bank(g, 6, "ps_r0"), bank(g, 7, "ps_r1")]
        P.dma("sp", lt[:], w["lam"].rearrange("(o a) d -> o a d", o=1))
        P.dma("sp", gs[:], w["g_sub"].rearrange("(p o) -> p o", o=1))
        P.tt("dve", pr[:, 0, :], lt[:, 0, :], lt[:, 1, :], ALU.mult)
        P.tt("dve", pr[:, 1, :], lt[:, 2, :], lt[:, 3, :], ALU.mult)
        P.reduce("dve", sm[:], pr[:], ALU.add, AX.X)
        P.act(ex[:], sm[:], AF.Exp)
        P.tt("dve", l1[:, 0:1], ex[:, 0:1], ex[:, 1:2], ALU.subtract)
        P.ts("dve", l1[:, 0:1], l1[:, 0:1], li, ALU.add, -1.0, ALU.mult)
        P.copy("dve", l1[:, 1:2], l1[:, 0:1])
        P.mm(ps_r[0][:, 0:2], g.ones_f[0:1, :], l1[0:1, 0:2])
        P.copy("dve", neglam[:], ps_r[0][:, 0:2])
        P.ts("dve", gs[:], gs[:], 1.0 - li, ALU.mult)
        nq = 0
        ne = 0
        for h in range(8):
            k_t, v_t = kh[h % 2], vh[h % 2]
            for j in range(2):
                P.dma("sp", k_t[:, j, :], g.kT[h * 128 + j * 64:h * 128 + (j + 1) * 64, :])
            P.dma("sp", v_t[:], g.vv[:, h * 128:(h + 1) * 128].rearrange("(c p) d -> p c d", p=128))
            for bi, (t0, wd, isctx) in enumerate(g.blocks):
                q = qh[nq % 2]
                aD, aP = accD[nq % 2], accP[nq % 2]
                nq += 1
                for j in range(2):
                    P.dma("sp", q[:, j, 0:wd], g.qT[h * 128 + j * 64:h * 128 + (j + 1) * 64, t0:t0 + wd])
                chunks = [0, 1] if isctx else list(range(NKC))
                for ci, kc in enumerate(chunks):
                    s = ps_s[ne % 2]
                    x = ee[ne % 3]
                    ne += 1
                    P.mm(s[:, 0, 0:wd], k_t[:, 0, kc * 128:(kc + 1) * 128], q[:, 0, 0:wd])
                    P.mm(s[:, 1, 0:wd], k_t[:, 1, kc * 128:(kc + 1) * 128], q[:, 1, 0:wd])
                    P.act(x[:, :, 0:wd], s[:, :, 0:wd], AF.Exp, scale=scale)
                    first, lastc = (ci == 0), (ci == len(chunks) - 1)
                    P.mm(ps_o1[:, 0:wd], v_t[:, kc, :], x[:, 0, 0:wd], start=first, stop=lastc)
                    P.mm(ps_o2[:, 0:wd], v_t[:, kc, :], x[:, 1, 0:wd], start=first, stop=lastc)
                    eng, a = ("dve", aD) if ci % 2 == 0 else ("pool", aP)
                    if ci < 2:
                        P.copy(eng, a[:, :, 0:wd], x[:, :, 0:wd])
                    else:
                        P.tt(eng, a[:, :, 0:wd], a[:, :, 0:wd], x[:, :, 0:wd], ALU.add)
                P.tt("dve", aD[:, :, 0:wd], aD[:, :, 0:wd], aP[:, :, 0:wd], ALU.add)
                a1 = Tile(aD.h[:, 0, :], "a1v", buf=aD.buf)
                a2 = Tile(aD.h[:, 1, :], "a2v", buf=aD.buf)
                P.mm(ps_r[0][:, 0:wd], g.ones_f[:], a1[:, 0:wd])
                P.mm(ps_r[1][:, 0:wd], g.ones_f[:], a2[:, 0:wd])
                P.recip(rc1[:, 0:wd], ps_r[0][:, 0:wd])
                P.recip(rc2[:, 0:wd], ps_r[1][:, 0:wd])
                P.tt("dve", u1[:, 0:wd], ps_o1[:, 0:wd], rc1[:, 0:wd], ALU.mult)
                P.tt("dve", u2[:, 0:wd], ps_o2[:, 0:wd], rc2[:, 0:wd], ALU.mult)
                P.stt("dve", oo[:, 0:wd], u2[:, 0:wd], neglam[:, 0:1], u1[:, 0:wd], ALU.mult, ALU.add)
                P.tt("pool", sq[:, 0:wd], oo[:, 0:wd], oo[:, 0:wd], ALU.mult)
                P.mm(ps_r[0][:, 0:wd], g.ones_f[:], sq[:, 0:wd])
                rsqrt(P, rs[:, 0:wd], ps_r[0][:, 0:wd], 1.0 / 128)
                o_b = ob[nq % 2]
                P.stt("dve", o_b[:, 0:wd], oo[:, 0:wd], gs[:, 0:1], rs[:, 0:wd], ALU.mult, ALU.mult)
                P.dma("pool", g.oT[h * 128:(h + 1) * 128, t0:t0 + wd], o_b[:, 0:wd])
        P.flush()


def phase_post(g, l):
    P, nc, w = g.P, g.nc, g.W[l]
    src = g.xc if l == 0 else g.xres
    with ExitStack() as es:
        wo = sbt(nc, es, "wo", [128, 8, 1024], BF16)
        gbc = [sbt(nc, es, "gbc%d" % r, [128, 1024], F32) for r in range(2)]
        dg = sbt(nc, es, "dg", [128, 128], F32)
        ob = [sbt(nc, es, "pob%d" % i, [128, 8, 512], BF16) for i in range(2)]
        hb = [sbt(nc, es, "phb%d" % i, [128, 8, 512], BF16) for i in range(2)]
        xb = [sbt(nc, es, "pxb%d" % i, [128, 1024], F32) for i in range(2)]
        xo = [sbt(nc, es, "pxo%d" % i, [128, 1024], F32) for i in range(2)]
        tm = [sbt(nc, es, "ptm%d" % i, [128, 1024], F32) for i in range(2)]
        junk = sbt(nc, es, "pjunk", [128, 1024], BF16)
        ss = [sbt(nc, es, "pss%d" % i, [128, 1], F32) for i in range(2)]
        rstd = [sbt(nc, es, "prstd%d" % i, [128, 1], F32) for i in range(2)]
        nt = NormT(g, es, [4, 5])
        psy = [bank(g, 0, "psy0", n=2), bank(g, 2, "psy1", n=2)]
        psg = bank(g, 6, "psg")
        load_w_cast(P, wo, w["w_o"], 8, 0, 1024)
        make_gbc(g, l, 2, gbc, dg, psg)
        n = 0
        for bi, (t0, wd, isctx) in enumerate(g.blocks):
            o = ob[bi % 2]
            h = hb[bi % 2]
            r = 1 if isctx else 0
            P.dma("sp", o[:, :, 0:wd], g.oT[:, t0:t0 + wd].rearrange("(c p) t -> p c t", p=128))
            for i in range(wd // 128):
                py = psy[n % 2]
                xt, xn_, t_, s_, r_ = xb[n % 2], xo[n % 2], tm[n % 2], ss[n % 2], rstd[n % 2]
                n += 1
                rows = slice(t0 + i * 128, t0 + (i + 1) * 128)
                P.dma("sp", xt[:], src[rows, :])
                for half in range(2):
                    for kc in range(8):
                        P.mm(py[:, half * 512:(half + 1) * 512], o[:, kc, i * 128:(i + 1) * 128],
                             wo[:, kc, half * 512:(half + 1) * 512], start=(kc == 0), stop=(kc == 7))
                residual_update(P, py, xt, xn_, t_, s_, r_, junk, gbc[r])
                P.dma("pool", g.xres[rows, :], xn_[:])
                nt(xn_[:], l, 1, r, h, i * 128)
            P.dma("pool", g.hT[:, t0:t0 + wd].rearrange("(c p) t -> p c t", p=128), h[:, :, 0:wd])
        P.flush()


def residual_update(P, py, xt, xnew, tmp, ss, rstd, junk, gbc):
    P.memset("dve", ss[:], 0.0)
    P.act(junk[:], py[:, :], AF.Square, accum_out=ss[:])
    rsqrt(P, rstd[:], ss[:], 1.0 / D)
    P.stt("dve", tmp[:], py[:, :], rstd[:, 0:1], gbc[:], ALU.mult, ALU.mult)
    P.tt("pool", xnew[:], tmp[:], xt[:], ALU.add)


def make_gbc(g, l, which, gbc, dg, psg):
    P = g.P
    for r in range(2):
        for c in range(8):
            P.ts("dve", dg[:], g.ident_f[:], g.DV[l][:, which, c, r:r + 1], ALU.mult)
            P.mm(psg[:, 0:128], g.ones_f[:], dg[:])
            P.copy("dve", gbc[r][:, c * 128:(c + 1) * 128], psg[:, 0:128])


def phase_ffn(g, l, final, last):
    P, nc, w = g.P, g.nc, g.W[l]
    TB = 256
    NJ = FF // 128
    with ExitStack() as es:
        wgu = sbt(nc, es, "wgu", [128, 8, 2 * FF], BF16)
        wd_ = sbt(nc, es, "wdn", [128, NJ, 1024], BF16)
        gbc = [sbt(nc, es, "fgbc%d" % r, [128, 1024], F32) for r in range(2)]
        dg = sbt(nc, es, "fdg", [128, 128], F32)
        hb = [sbt(nc, es, "fhb%d" % i, [128, 8, TB], BF16) for i in range(2)]
        ho = [sbt(nc, es, "fho%d" % i, [128, 8, TB], BF16) for i in range(2)]
        at = sbt(nc, es, "fat", [128, NJ, TB], BF16)
        sg = [sbt(nc, es, "fsg%d" % i, [128, TB], F32) for i in range(2)]
        xb = [sbt(nc, es, "fxb%d" % i, [128, 1024], F32) for i in range(2)]
        xo = [sbt(nc, es, "fxo%d" % i, [128, 1024], F32) for i in range(2)]
        tm = sbt(nc, es, "ftm", [128, 1024], F32)
        junk = sbt(nc, es, "fjunk", [128, 1024], BF16)
        ss = [sbt(nc, es, "fss%d" % i, [128, 1], F32) for i in range(2)]
        rstd = [sbt(nc, es, "frstd%d" % i, [128, 1], F32) for i in range(2)]
        nt = NormT(g, es, [6]) if not final else None
        psgu = [bank(g, i, "psgu%d" % i) for i in (0, 1, 2, 3)]
        psf = bank(g, 4, "psf", n=2)
        psg = bank(g, 7, "fpsg")
        for k in range(8):
            for c in range(0, 2 * FF, 1408):
                P.dma("pool", wgu[:, k, c:c + 1408], w["w_gu"][k * 128:(k + 1) * 128, c:c + 1408], max_dma_last_dim=4096)
        for j in range(NJ):
            P.dma("pool", wd_[:, j, :], w["w_down"][j * 128:(j + 1) * 128, :], max_dma_last_dim=4096)
        make_gbc(g, l, 3, gbc, dg, psg)
        n = 0
        ng = 0
        nblk = g.TOK // TB
        for bi in range(nblk):
            t0 = bi * TB
            isctx = t0 < CTX
            r = 1 if isctx else 0
            if last and isctx:
                continue
            h = hb[bi % 2]
            hn = ho[bi % 2]
            P.dma("sp", h[:], g.hT[:, t0:t0 + TB].rearrange("(c p) t -> p c t", p=128))
            for j in range(NJ):
                pg, pu = psgu[(2 * ng) % 4], psgu[(2 * ng + 1) % 4]
                s_ = sg[ng % 2]
                ng += 1
                for kc in range(8):
                    P.mm(pg[:, 0:TB], wgu[:, kc, j * 128:(j + 1) * 128], h[:, kc, :], start=(kc == 0), stop=(kc == 7))
                for kc in range(8):
                    P.mm(pu[:, 0:TB], wgu[:, kc, FF + j * 128:FF + (j + 1) * 128], h[:, kc, :], start=(kc == 0), stop=(kc == 7))
                P.act(s_[:], pg[:, 0:TB], AF.Silu)
                P.tt("dve", at[:, j, :], s_[:], pu[:, 0:TB], ALU.mult)
            for i in range(TB // 128):
                xt, xn_, s2, r2 = xb[n % 2], xo[n % 2], ss[n % 2], rstd[n % 2]
                n += 1
                rows = slice(t0 + i * 128, t0 + (i + 1) * 128)
                P.dma("sp", xt[:], g.xres[rows, :])
                for half in range(2):
                    for j in range(NJ):
                        P.mm(psf[:, half * 512:(half + 1) * 512], at[:, j, i * 128:(i + 1) * 128],
                             wd_[:, j, half * 512:(half + 1) * 512], start=(j == 0), stop=(j == NJ - 1))
                residual_update(P, psf, xt, xn_, tm, s2, r2, junk, gbc[r])
                if final:
                    if not isctx:
                        P.dma("pool", g.y[t0 - CTX + i * 128:t0 - CTX + (i + 1) * 128, :], xn_[:])
                else:
                    P.dma("pool", g.xres[rows, :], xn_[:])
                    nt(xn_[:], l + 1, 0, r, hn, i * 128)
            if not final:
                P.dma("pool", g.hT[:, t0:t0 + TB].rearrange("(c p) t -> p c t", p=128), hn[:])
        P.flush()


def rot_store(P, pa, pb, rt, rows, wd, a1, a2, o, dst_ap):
    P.tt("dve", a1[0:rows, 0:wd], pa[0:rows, 0:wd], rt[0:rows, 0, 0:wd], ALU.mult)
    P.tt("dve", a2[0:rows, 0:wd], pb[0:rows, 0:wd], rt[0:rows, 1, 0:wd], ALU.mult)
    P.tt("pool", o[0:rows, 0:wd], a1[0:rows, 0:wd], a2[0:rows, 0:wd], ALU.add)
    P.dma("pool", dst_ap, o[0:rows, 0:wd])


def phase_projB(g, l):
    P, nc, w = g.P, g.nc, g.W[l]
    with ExitStack() as es:
        win = sbt(nc, es, "win", [128, 8, 544], BF16)
        winp = sbt(nc, es, "winp", [128, 8, 32], BF16)
        wuq = sbt(nc, es, "wuq", [128, 2, 1536], BF16)
        wuqp = sbt(nc, es, "wuqp", [128, 2, 1536], BF16)
        wkk = sbt(nc, es, "wkk", [128, 2, 1024], BF16)
        wkv = sbt(nc, es, "wkv", [128, 2, 1024], BF16)
        gq = sbt(nc, es, "gq", [128, 4], F32)
        hb = [sbt(nc, es, "bhb%d" % i, [128, 8, 512], BF16) for i in range(2)]
        rB = [sbt(nc, es, "rB%d" % i, [128, 2, 512], F32) for i in range(2)]
        rK = [sbt(nc, es, "rK%d" % i, [32, 2, 512], F32) for i in range(2)]
        sqt = [sbt(nc, es, "sqt%d" % i, [128, 512], F32) for i in range(2)]
        rsd = sbt(nc, es, "rsd", [128, 512], F32)
        cn = [sbt(nc, es, "cn%d" % i, [128, 2, 512], BF16) for i in range(2)]
        a1 = [sbt(nc, es, "ba1_%d" % i, [128, 512], F32) for i in range(2)]
        a2 = [sbt(nc, es, "ba2_%d" % i, [128, 512], F32) for i in range(2)]
        ro = [sbt(nc, es, "bro%d" % i, [128, 512], BF16) for i in range(2)]
        va = [sbt(nc, es, "bva%d" % i, [128, 16, 65], BF16) for i in range(2)]
        pz = [bank(g, 0, "pz0"), bank(g, 1, "pz1")]
        pss = bank(g, 2, "pss")
        pzr, pzrp = bank(g, 3, "pzr"), bank(g, 4, "pzrp")
        pq, pqp = bank(g, 5, "pq"), bank(g, 6, "pqp")
        pk = bank(g, 7, "pk")
        load_w_cast(P, win, w["w_in"], 8, 0, 544)
        load_w_cast_swapped(P, winp, w["w_in"], 8, 512, 1, 16)
        load_w_cast(P, wuq, w["w_uq"], 2, 0, 1536)
        for k in range(2):
            dst = wuqp.h[:, k, :].rearrange("p (h c) -> p h c", c=96)
            s = w["w_uq"][k * 128:(k + 1) * 128, :].rearrange("p (h c) -> p h c", c=96)
            P.dma("pool", wuqp.v(dst[:, :, 0:64]), s[:, :, 0:64], max_dma_last_dim=4096)
            P.dma("pool", wuqp.v(dst[:, :, 64:80]), s[:, :, 80:96], max_dma_last_dim=4096)
            P.dma("pool", wuqp.v(dst[:, :, 80:96]), s[:, :, 64:80], max_dma_last_dim=4096)
            s2 = w["w_ukv"][k * 128:(k + 1) * 128, :].rearrange("p (h c) -> p h c", c=128)
            P.dma("pool", wkk.v(wkk.h[:, k, :].rearrange("p (h c) -> p h c", c=64)), s2[:, :, 0:64], max_dma_last_dim=4096)
            P.dma("pool", wkv.v(wkv.h[:, k, :].rearrange("p (h c) -> p h c", c=64)), s2[:, :, 64:128], max_dma_last_dim=4096)
        P.dma("sp", gq[:, 0:2], w["g_q"].rearrange("(c p) -> p c", p=128))
        P.dma("sp", gq[:, 2:4], w["g_kv"].rearrange("(c p) -> p c", p=128))
        for v in va:
            P.memset("dve", v[:], 1.0)
        n = 0
        m = 0
        for bi, (t0, wd, isctx) in enumerate(g.blocks):
            h = hb[bi % 2]
            rb, rk = rB[bi % 2], rK[bi % 2]
            P.dma("sp", h[:, :, 0:wd], g.hT[:, t0:t0 + wd].rearrange("(c p) t -> p c t", p=128))
            P.dma("sp", rb[:, :, 0:wd], g.ropeB[:, :, t0:t0 + wd].rearrange("a p t -> p a t"))
            P.dma("sp", rk[:, :, 0:wd], g.ropeB[:, 64:96, t0:t0 + wd].rearrange("a p t -> p a t"))
            for which in range(2):
                c_ = cn[which]
                for c in range(2):
                    col = which * 256 + c * 128
                    for kc in range(8):
                        P.mm(pz[c][:, 0:wd], win[:, kc, col:col + 128], h[:, kc, 0:wd], start=(kc == 0), stop=(kc == 7))
                    P.act(sqt[c][:, 0:wd], pz[c][:, 0:wd], AF.Square)
                P.mm(pss[:, 0:wd], g.ones_f[:], sqt[0][:, 0:wd], start=True, stop=False)
                P.mm(pss[:, 0:wd], g.ones_f[:], sqt[1][:, 0:wd], start=False, stop=True)
                rsqrt(P, rsd[:, 0:wd], pss[:, 0:wd], 1.0 / 256)
                for c in range(2):
                    P.stt("dve", c_[:, c, 0:wd], pz[c][:, 0:wd], gq[:, which * 2 + c:which * 2 + c + 1], rsd[:, 0:wd],
                          ALU.mult, ALU.mult)
            for kc in range(8):
                P.mm(pzr[0:32, 0:wd], win[:, kc, 512:544], h[:, kc, 0:wd], start=(kc == 0), stop=(kc == 7))
            for kc in range(8):
                P.mm(pzrp[0:32, 0:wd], winp[:, kc, 0:32], h[:, kc, 0:wd], start=(kc == 0), stop=(kc == 7))
            rot_store(P, pzr, pzrp, rk, 32, wd, a1[n % 2], a2[n % 2], ro[n % 2], g.kT[1024:1056, t0:t0 + wd])
            n += 1
            cq, ckv = cn[0], cn[1]
            for hh in range(16):
                for kc in range(2):
                    P.mm(pq[0:96, 0:wd], wuq[:, kc, hh * 96:(hh + 1) * 96], cq[:, kc, 0:wd], start=(kc == 0), stop=(kc == 1))
                for kc in range(2):
                    P.mm(pqp[0:96, 0:wd], wuqp[:, kc, hh * 96:(hh + 1) * 96], cq[:, kc, 0:wd], start=(kc == 0), stop=(kc == 1))
                rot_store(P, pq, pqp, rb, 96, wd, a1[n % 2], a2[n % 2], ro[n % 2], g.qT[hh * 96:(hh + 1) * 96, t0:t0 + wd])
                n += 1
            for fc in range(8):
                for kc in range(2):
                    P.mm(pk[:, 0:wd], wkk[:, kc, fc * 128:(fc + 1) * 128], ckv[:, kc, 0:wd], start=(kc == 0), stop=(kc == 1))
                o = ro[n % 2]
                n += 1
                P.copy("act", o[:, 0:wd], pk[:, 0:wd])
                P.dma("pool", g.kT[fc * 128:(fc + 1) * 128, t0:t0 + wd], o[:, 0:wd])
            for i in range(wd // 128):
                v = va[m % 2]
                m += 1
                for half in range(2):
                    for kc in range(2):
                        P.mm(pk[:, :], ckv[:, kc, i * 128:(i + 1) * 128], wkv[:, kc, half * 512:(half + 1) * 512],
                             start=(kc == 0), stop=(kc == 1))
                    P.copy("act", v[:, half * 8:(half + 1) * 8, 0:64], pk.v(pk.h[:, :].rearrange("p (h d) -> p h d", d=64)))
                P.dma("pool", g.vv[t0 + i * 128:t0 + (i + 1) * 128, 0:1040], v.v(v.h[:, :, :].rearrange("p h d -> p (h d)")))
        P.flush()


def attn_single(g, l, nheads, qrows, krow_loader, scale):
    P, nc = g.P, g.nc
    NK = g.NK
    NKC = NK // 128
    with ExitStack() as es:
        kh = [sbt(nc, es, "bkh%d" % i, [qrows, NK], BF16) for i in range(2)]
        vh = [sbt(nc, es, "bvh%d" % i, [128, NKC, 65], BF16) for i in range(2)]
        qh = [sbt(nc, es, "bqh%d" % i, [qrows, 512], BF16) for i in range(2)]
        ee = [sbt(nc, es, "bee%d" % i, [128, 512], BF16) for i in range(3)]
        osb = [sbt(nc, es, "bosb%d" % i, [65, 512], F32) for i in range(2)]
        rc = sbt(nc, es, "brc", [64, 512], F32)
        ob = [sbt(nc, es, "bob%d" % i, [64, 512], BF16) for i in range(2)]
        ps_s = [bank(g, i, "bps_s%d" % i) for i in (0, 1)]
        ps_o = [bank(g, i, "bps_o%d" % i) for i in (2, 3)]
        ps_b = bank(g, 4, "bps_b")
        nq = 0
        ne = 0
        for h in range(nheads):
            k_t, v_t = kh[h % 2], vh[h % 2]
            krow_loader(P, k_t, h)
            P.dma("sp", v_t[:], g.vv[:, h * 65:(h + 1) * 65].rearrange("(c p) d -> p c d", p=128))
            for bi, (t0, wd, isctx) in enumerate(g.blocks):
                q = qh[nq % 2]
                po = ps_o[nq % 2]
                os_ = osb[nq % 2]
                o_b = ob[nq % 2]
                nq += 1
                P.dma("sp", q[:, 0:wd], g.qT[h * qrows:(h + 1) * qrows, t0:t0 + wd])
                chunks = [0, 1] if isctx else list(range(NKC))
                for ci, kc in enumerate(chunks):
                    s = ps_s[ne % 2]
                    x = ee[ne % 3]
                    ne += 1
                    P.mm(s[:, 0:wd], k_t[:, kc * 128:(kc + 1) * 128], q[:, 0:wd])
                    P.act(x[:, 0:wd], s[:, 0:wd], AF.Exp, scale=scale)
                    P.mm(po[0:65, 0:wd], v_t[:, kc, :], x[:, 0:wd], start=(ci == 0), stop=(ci == len(chunks) - 1))
                P.copy("dve", os_[:, 0:wd], po[0:65, 0:wd])
                P.mm(ps_b[0:64, 0:wd], g.sel65[:, :], os_[:, 0:wd])
                P.recip(rc[:, 0:wd], ps_b[0:64, 0:wd])
                P.tt("dve", o_b[:, 0:wd], os_[0:64, 0:wd], rc[:, 0:wd], ALU.mult)
                P.dma("pool", g.oT[h * 64:(h + 1) * 64, t0:t0 + wd], o_b[:, 0:wd])
        P.flush()


def phase_attnB(g, l):
    def loader(P, k_t, h):
        P.dma("sp", k_t[0:64, :], g.kT[h * 64:(h + 1) * 64, :])
        P.dma("sp", k_t[64:96, :], g.kT[1024:1056, :])
    attn_single(g, l, 16, 96, loader, 96 ** -0.5)


def phase_projC(g, l):
    P, nc, w = g.P, g.nc, g.W[l]
    with ExitStack() as es:
        wq = sbt(nc, es, "cwq", [128, 8, 1024], BF16)
        wk = sbt(nc, es, "cwk", [128, 8, 1024], BF16)
        wv = sbt(nc, es, "cwv", [128, 8, 1024], BF16)
        hb = [sbt(nc, es, "chb%d" % i, [128, 8, 512], BF16) for i in range(2)]
        ro = [sbt(nc, es, "cro%d" % i, [128, 512], BF16) for i in range(2)]
        va = [sbt(nc, es, "cva%d" % i, [128, 16, 65], BF16) for i in range(2)]
        psA = [bank(g, i, "cpsA%d" % i) for i in (0, 1)]
        psV = [bank(g, i, "cpsV%d" % i) for i in (2, 3)]
        load_w_cast(P, wq, w["w_qkv"], 8, 0, 1024)
        load_w_cast(P, wk, w["w_qkv"], 8, 1024, 2048)
        load_w_cast(P, wv, w["w_qkv"], 8, 2048, 3072)
        for v in va:
            P.memset("dve", v[:], 1.0)
        n = 0
        m = 0
        for bi, (t0, wd, isctx) in enumerate(g.blocks):
            h = hb[bi % 2]
            P.dma("sp", h[:, :, 0:wd], g.hT[:, t0:t0 + wd].rearrange("(c p) t -> p c t", p=128))
            for (wt, dst) in ((wq, g.qT), (wk, g.kT)):
                for fc in range(8):
                    pa = psA[n % 2]
                    o = ro[n % 2]
                    n += 1
                    for kc in range(8):
                        P.mm(pa[:, 0:wd], wt[:, kc, fc * 128:(fc + 1) * 128], h[:, kc, 0:wd], start=(kc == 0), stop=(kc == 7))
                    if n % 2:
                        P.copy("act", o[:, 0:wd], pa[:, 0:wd])
                    else:
                        P.copy("dve", o[:, 0:wd], pa[:, 0:wd])
                    P.dma("pool", dst[fc * 128:(fc + 1) * 128, t0:t0 + wd], o[:, 0:wd])
            for i in range(wd // 128):
                v = va[m % 2]
                for half in range(2):
                    pv = psV[(2 * m + half) % 2]
                    for kc in range(8):
                        P.mm(pv[:, :], h[:, kc, i * 128:(i + 1) * 128], wv[:, kc, half * 512:(half + 1) * 512],
                             start=(kc == 0), stop=(kc == 7))
                    P.copy("act", v[:, half * 8:(half + 1) * 8, 0:64], pv.v(pv.h[:, :].rearrange("p (h d) -> p h d", d=64)))
                m += 1
                P.dma("pool", g.vv[t0 + i * 128:t0 + (i + 1) * 128, 0:1040], v.v(v.h[:, :, :].rearrange("p h d -> p (h d)")))
        P.flush()


def phase_attnC(g, l):
    P, nc, w = g.P, g.nc, g.W[l]
    ROWS = g.ROWS
    scale = 64 ** -0.5
    with ExitStack() as es:
        tb = sbt(nc, es, "tb", [128, 16, 14, 64], BF16)
        kc_t = sbt(nc, es, "kc_t", [64, 16, 256], BF16)
        vc_t = sbt(nc, es, "vc_t", [128, 2, 1040], BF16)
        kb = [sbt(nc, es, "kb%d" % i, [64, 16, 512], BF16) for i in range(2)]
        vb = [sbt(nc, es, "vb%d" % i, [128, 4, 1040], BF16) for i in range(2)]
        qr = [sbt(nc, es, "qr%d" % i, [64, 16, 512], BF16) for i in range(2)]
        ee = [sbt(nc, es, "cee%d" % i, [128, 512], BF16) for i in range(3)]
        osb = [sbt(nc, es, "cosb%d" % i, [65, 512], F32) for i in range(2)]
        rc = sbt(nc, es, "crc", [64, 512], F32)
        obuf = [sbt(nc, es, "cobuf%d" % i, [64, 16, 512], BF16) for i in range(2)]
        ps_s = [bank(g, i, "cps_s%d" % i) for i in (0, 1)]
        ps_o = [bank(g, i, "cps_o%d" % i) for i in (2, 3)]
        ps_b = bank(g, 4, "cps_b")
        tab = w["rpb_tab"]
        for h in range(16):
            P.dma("pool", tb[0:64, h, :, :], tab[h, 0:14, :, :].rearrange("e k q -> k e q"), max_dma_last_dim=4096)
            P.dma("pool", tb[64:128, h, :, :], tab[h, 1:15, :, :].rearrange("e k q -> k e q"), max_dma_last_dim=4096)
        P.dma("sp", kc_t[:], g.kT[0:1024, 0:CTX].rearrange("(h d) t -> d h t", d=64))
        P.dma("sp", vc_t[:], g.vv[0:CTX, :].rearrange("(c p) d -> p c d", p=128))
        prow = [("c", i) for i in range(CTX // 64)] + [("l", r) for r in range(ROWS)]
        ne = 0
        ng = 0
        for pi, (kind, r) in enumerate(prow):
            if kind == "c":
                grp0, gi = 0, r
                tq0 = 0
            else:
                grp0, gi = CTX + (r // 8) * 512, r % 8
                tq0 = grp0
            gidx = 0 if kind == "c" else 1 + r // 8
            q_t = qr[gidx % 2]
            o_t = obuf[gidx % 2]
            gw = CTX if kind == "c" else 512
            if gi == 0:
                P.dma("sp", q_t[:, :, 0:gw], g.qT[0:1024, tq0:tq0 + gw].rearrange("(h d) t -> d h t", d=64))
            if kind == "l":
                rs_ = min(max(r - 4, 0), ROWS - 8)
                k_b, v_b = kb[r % 2], vb[r % 2]
                tk0 = CTX + rs_ * 64
                P.dma("sp", k_b[:], g.kT[0:1024, tk0:tk0 + 512].rearrange("(h d) t -> d h t", d=64))
                P.dma("sp", v_b[:], g.vv[tk0:tk0 + 512, :].rearrange("(c p) d -> p c d", p=128))
                nch = 6
            else:
                nch = 2
            for hg in range(2):
                po = ps_o[ng % 2]
                os_ = osb[ng % 2]
                ng += 1
                for j in range(nch):
                    s = ps_s[ne % 2]
                    x = ee[ne % 3]
                    ne += 1
                    for hh in range(8):
                        h = hg * 8 + hh
                        if j < 2:
                            kl = kc_t[:, h, j * 128:(j + 1) * 128]
                        else:
                            kl = k_b[:, h, (j - 2) * 128:(j - 1) * 128]
                        P.mm(s[:, hh * 64:(hh + 1) * 64], kl, q_t[:, h, gi * 64:(gi + 1) * 64],
                             start=(hh == 0), stop=(hh == 7 and j < 2))
                    if j >= 2:
                        e = rs_ + 2 * (j - 2) - r + 7
                        assert 0 <= e <= 13
                        P.mm(s[:, :], g.ident8[:], tb[:, hg * 8:(hg + 1) * 8, e, :], start=False, stop=True)
                    P.act(x[:], s[:], AF.Exp, scale=scale)
                    for hh in range(8):
                        h = hg * 8 + hh
                        if j < 2:
                            vl = vc_t[:, j, h * 65:(h + 1) * 65]
                        else:
                            vl = v_b[:, j - 2, h * 65:(h + 1) * 65]
                        P.mm(po[0:65, hh * 64:(hh + 1) * 64], vl, x[:, hh * 64:(hh + 1) * 64],
                             start=(j == 0 and hh == 0), stop=(j == nch - 1 and hh == 7))
                P.copy("dve", os_[:], po[0:65, :])
                P.mm(ps_b[0:64, :], g.sel65[:, :], os_[:, :])
                P.recip(rc[:], ps_b[0:64, :])
                P.tt("dve", o_t[:, hg * 8:(hg + 1) * 8, gi * 64:(gi + 1) * 64],
                     os_.v(os_.h[0:64, :].rearrange("p (h q) -> p h q", q=64)),
                     rc.v(rc.h[:, :].rearrange("p (h q) -> p h q", q=64)), ALU.mult)
            last_in_group = (kind == "c" and r == CTX // 64 - 1) or (kind == "l" and (gi == 7 or r == ROWS - 1))
            if last_in_group:
                P.dma("pool", g.oT[0:1024, tq0:tq0 + gw].rearrange("(h d) t -> d h t", d=64), o_t[:, :, 0:gw])
        P.flush()


def _rope_np(S, rot_dim):
    n_freq = rot_dim // 4
    inv = (np.float32(10000.0) ** (-(np.arange(n_freq, dtype=np.float32) / np.float32(n_freq)))).astype(np.float32)
    t = np.arange(S, dtype=np.int64)
    row = (t // GRID_W).astype(np.float32)
    col = (t % GRID_W).astype(np.float32)
    ang = np.concatenate([row[:, None] * inv, col[:, None] * inv], axis=-1).astype(np.float32)
    return np.cos(ang).astype(np.float32), np.sin(ang).astype(np.float32)


def _tables(S):
    TOK = CTX + S
    ca, sa = _rope_np(S, 64)
    cb, sb_ = _rope_np(S, 32)
    ropeA = np.zeros((2, 128, TOK), np.float32)
    ropeA[0] = 1.0
    ropeB = np.zeros((2, 128, TOK), np.float32)
    ropeB[0] = 1.0
    for p in range(128):
        d = p % 64
        i = d % 32
        ropeA[0, p, CTX:] = ca[:, i]
        ropeA[1, p, CTX:] = -sa[:, i] if d < 32 else sa[:, i]
    for p in range(64, 96):
        d = p - 64
        i = d % 16
        ropeB[0, p, CTX:] = cb[:, i]
        ropeB[1, p, CTX:] = -sb_[:, i] if d < 16 else sb_[:, i]
    return ropeA, ropeB


def _rpb_tab(rpb):
    qc = np.arange(64)
    cstart = np.clip(qc - 8, 0, 64 - 16)
    kc = np.arange(64)
    idx = kc[:, None] - qc[None, :] + 15
    inwin = (kc[:, None] >= cstart[None, :]) & (kc[:, None] < cstart[None, :] + 16)
    idxc = np.clip(idx, 0, 30)
    tab = rpb[:, :, idxc]
    tab = np.where(inwin[None, None], tab, np.float32(NEG)).astype(np.float32)
    return np.ascontiguousarray(tab)


_CACHE = {}


def run_model(inputs, S, kinds, n_cores=8):
    key = (S, tuple(kinds))
    if key not in _CACHE:
        _CACHE[key] = build(S, tuple(kinds))
    nc = _CACHE[key]
    ropeA, ropeB = _tables(S)
    ident = np.eye(128, dtype=np.float32)
    B = inputs["x"].shape[0]
    shared = {"ident": ident, "ropeA": ropeA, "ropeB": ropeB}
    for l, k in enumerate(kinds):
        p = "l%d_" % l
        for nm in ("w_mod", "b_mod", "g_norm", "w_gu", "w_down"):
            shared[p + nm] = np.ascontiguousarray(inputs[p + nm], dtype=np.float32)
        if k == 0:
            for nm in ("a_w_qkv", "a_w_o", "a_lam", "a_g_sub"):
                shared[p + nm] = np.ascontiguousarray(inputs[p + nm], dtype=np.float32)
        elif k == 1:
            for nm in ("b_w_in", "b_g_q", "b_g_kv", "b_w_uq", "b_w_ukv", "b_w_o"):
                shared[p + nm] = np.ascontiguousarray(inputs[p + nm], dtype=np.float32)
        else:
            for nm in ("c_w_qkv", "c_w_o"):
                shared[p + nm] = np.ascontiguousarray(inputs[p + nm], dtype=np.float32)
            shared[p + "c_rpb_tab"] = _rpb_tab(np.asarray(inputs[p + "c_rpb"], dtype=np.float32))
    in_maps = []
    for core in range(n_cores):
        b = core % B
        m = dict(shared)
        m["xc"] = np.ascontiguousarray(np.concatenate([inputs["ctx"][b], inputs["x"][b]], axis=0), dtype=np.float32)
        m["cc"] = np.ascontiguousarray(np.stack([inputs["c"][b], inputs["c_ctx"]], axis=0), dtype=np.float32)
        in_maps.append(m)
    res = run_bass_kernel_spmd(nc, in_maps, core_ids=list(range(n_cores)))
    out = np.stack([np.asarray(res.results[b]["y"]) for b in range(B)], axis=0)
    return out.astype(np.float32)


def kernel(**inputs):
    inputs = {k: np.asarray(v) for k, v in inputs.items()}
    return run_model(inputs, inputs["x"].shape[1], KINDS)
```

```python
import numpy as np
import concourse.bass as bass
import concourse.mybir as mybir

F32 = mybir.dt.float32
BF16 = mybir.dt.bfloat16
AF = mybir.ActivationFunctionType
ALU = mybir.AluOpType
AX = mybir.AxisListType

COMPUTE = ("pe", "act", "dve", "pool")
ENGINES = ("pe", "act", "dve", "pool", "sp")


class Buf:
    __slots__ = ("name", "last_w", "readers", "dsem", "is_dram")

    def __init__(self, name):
        self.name = name
        self.last_w = None
        self.readers = {}
        self.dsem = {}


class View:
    __slots__ = ("buf", "ap")

    def __init__(self, buf, ap):
        self.buf = buf
        self.ap = ap


class Tile:
    def __init__(self, handle, name, buf=None):
        self.h = handle
        self.buf = buf if buf is not None else Buf(name)

    def __getitem__(self, idx):
        return View(self.buf, self.h[idx])

    def v(self, ap):
        return View(self.buf, ap)

    def sub(self, name):
        return Tile(self.h, name)


class Op:
    __slots__ = ("eng", "fn", "reads", "writes", "is_dma", "deps", "needs_inc", "tick",
                 "dgroup", "dtarget", "idx", "done", "lhs_buf")


class Prog:
    def __init__(self, nc):
        self.nc = nc
        self.ops = []
        self.sem = {e: nc.alloc_semaphore("tick_" + e) for e in COMPUTE}
        self.tick = {e: 0 for e in COMPUTE}
        self.dsem_pool = {"hw": [], "sw": []}
        self.dsem_all = []
        self.known = {e: {} for e in ENGINES}
        self.n_emitted = 0
        self._phase_dsems = []
        self.same_engine_sync = True

    def _add(self, eng, fn, reads, writes, is_dma=False, dgroup=None):
        op = Op()
        op.eng = eng
        op.fn = fn
        op.reads = [v.buf for v in reads if v is not None]
        op.writes = [v.buf for v in writes if v is not None]
        op.is_dma = is_dma
        op.needs_inc = False
        op.tick = None
        op.dgroup = dgroup
        op.dtarget = None
        op.idx = len(self.ops)
        op.done = False
        op.lhs_buf = None
        deps = set()
        for b in op.reads:
            if b.last_w is not None:
                deps.add(b.last_w)
        for b in op.writes:
            lw = b.last_w
            if lw is not None:
                if is_dma and lw.is_dma and not b.readers and lw.dgroup is dgroup:
                    deps |= lw.deps
                else:
                    deps.add(lw)
            for r in b.readers.values():
                deps.add(r)
        deps.discard(op)
        deps = {d for d in deps if not d.done}
        op.deps = deps
        for b in op.reads:
            b.readers[("d", op.idx) if is_dma else eng] = op
        for b in op.writes:
            b.last_w = op
            b.readers = {}
        self.ops.append(op)
        return op

    def _dsem(self, buf, cls):
        if cls not in buf.dsem:
            pool = self.dsem_pool[cls]
            if pool:
                ent = pool.pop()
            else:
                ent = [self.nc.alloc_semaphore("dsem_%s%d" % (cls, len(self.dsem_all))), 0, cls]
                self.dsem_all.append(ent)
            buf.dsem[cls] = ent
            self._phase_dsems.append((buf, cls))
        return buf.dsem[cls]

    def flush(self, name=None):
        nc = self.nc
        ops = self.ops
        if not ops:
            return
        for op in ops:
            for d in op.deps:
                if not d.is_dma:
                    if d.eng == op.eng and (d.eng == "pe" or not self.same_engine_sync):
                        continue
                    d.needs_inc = True
        last = {}
        for op in ops:
            if not op.is_dma:
                last[op.eng] = op
        for e, op in last.items():
            op.needs_inc = True
        for op in ops:
            if op.is_dma:
                ent = op.dgroup
                ent[1] += 16
                op.dtarget = ent[1]
            elif op.needs_inc:
                self.tick[op.eng] += 1
                op.tick = self.tick[op.eng]
        final_tick = dict(self.tick)
        final_dsem = [(ent[0], ent[1]) for ent in self.dsem_all]

        per_eng = {e: [] for e in ENGINES}
        for op in ops:
            per_eng[op.eng].append(op)

        sem = self.sem
        known = self.known

        def emit_engine(ename, eng):
            kn = known[ename]

            def wait(s, val):
                key = id(s)
                if kn.get(key, 0) >= val:
                    return
                eng.wait_ge(s, val)
                kn[key] = val

            for op in per_eng[ename]:
                need = {}
                lhs_keys = set()
                lb = op.lhs_buf
                for d in op.deps:
                    if d.is_dma:
                        ent = d.dgroup
                        k = ("d", id(ent))
                        if k not in need or need[k][1] < d.dtarget:
                            need[k] = (ent[0], d.dtarget)
                    else:
                        if d.eng == ename and (ename == "pe" or not self.same_engine_sync):
                            continue
                        k = ("t", d.eng)
                        if k not in need or need[k][1] < d.tick:
                            need[k] = (sem[d.eng], d.tick)
                    if lb is not None and lb in d.writes:
                        lhs_keys.add(k)
                pend = [(k, s_, val) for k, (s_, val) in need.items() if kn.get(id(s_), 0) < val]
                attach = None
                if pend and not op.is_dma:
                    for i_, (k, s_, val) in enumerate(pend):
                        if ename != "pe" or k not in lhs_keys:
                            attach = (s_, val)
                            pend.pop(i_)
                            break
                for k, s_, val in pend:
                    wait(s_, val)
                ins = op.fn(eng)
                if attach is not None:
                    ins._wait_ge(attach[0], attach[1])
                    kn[id(attach[0])] = attach[1]
                if op.is_dma:
                    ins.then_inc(op.dgroup[0], 16)
                elif op.needs_inc:
                    ins.then_inc(sem[ename], 1)
                    if ename in kn and False:
                        pass
            for e2 in COMPUTE:
                if e2 != ename and final_tick[e2] > 0:
                    wait(sem[e2], final_tick[e2])
            for s, val in final_dsem:
                if val > 0:
                    wait(s, val)

        with nc.Block() as block:
            @block.sync
            def _(e):
                emit_engine("sp", e)

            @block.tensor
            def _(e):
                emit_engine("pe", e)

            @block.scalar
            def _(e):
                emit_engine("act", e)

            @block.vector
            def _(e):
                emit_engine("dve", e)

            @block.gpsimd
            def _(e):
                emit_engine("pool", e)

        self.n_emitted += len(ops)
        for op in ops:
            op.done = True
            op.fn = None
            op.deps = ()
        for b, cls in self._phase_dsems:
            self.dsem_pool[cls].append(b.dsem.pop(cls))
        self._phase_dsems = []
        self.ops = []

    def mm(self, out, lhsT, rhs, start=True, stop=True, **kw):
        op = self._add("pe", lambda e: e.matmul(out.ap, lhsT.ap, rhs.ap, start=start, stop=stop, **kw),
                       [lhsT, rhs] + ([] if start else [out]), [out])
        op.lhs_buf = lhsT.buf
        return op

    def transpose(self, out, in_, ident):
        op = self._add("pe", lambda e: e.transpose(out.ap, in_.ap, ident.ap), [in_, ident], [out])
        op.lhs_buf = in_.buf
        return op

    def act(self, out, in_, func, bias=None, scale=1.0, accum_out=None):
        rd = [in_]
        kw = {}
        if bias is not None:
            if isinstance(bias, View):
                rd.append(bias)
                kw["bias"] = bias.ap
            else:
                kw["bias"] = bias
        if isinstance(scale, View):
            rd.append(scale)
            kw["scale"] = scale.ap
        else:
            kw["scale"] = scale
        wr = [out]
        if accum_out is not None:
            wr.append(accum_out)
            kw["accum_out"] = accum_out.ap
        return self._add("act", lambda e: e.activation(out.ap, in_.ap, func, **kw), rd, wr)

    def tt(self, eng, out, in0, in1, op):
        return self._add(eng, lambda e: e.tensor_tensor(out.ap, in0.ap, in1.ap, op), [in0, in1], [out])

    def ts(self, eng, out, in0, s1, op0, s2=None, op1=None, accum_out=None):
        rd = [in0]
        a1 = s1
        a2 = s2
        if isinstance(s1, View):
            rd.append(s1)
            a1 = s1.ap
        if isinstance(s2, View):
            rd.append(s2)
            a2 = s2.ap
        wr = [out]
        kw = {}
        if op1 is not None:
            kw["op1"] = op1
        if accum_out is not None:
            wr.append(accum_out)
            kw["accum_out"] = accum_out.ap
        return self._add(eng, lambda e: e.tensor_scalar(out.ap, in0.ap, a1, a2, op0, **kw), rd, wr)

    def stt(self, eng, out, in0, scalar, in1, op0, op1):
        rd = [in0, in1]
        a = scalar
        if isinstance(scalar, View):
            rd.append(scalar)
            a = scalar.ap
        return self._add(eng, lambda e: e.scalar_tensor_tensor(out.ap, in0.ap, a, in1.ap, op0, op1), rd, [out])

    def copy(self, eng, out, in_):
        if eng == "act":
            return self._add("act", lambda e: e.copy(out.ap, in_.ap), [in_], [out])
        return self._add(eng, lambda e: e.tensor_copy(out.ap, in_.ap), [in_], [out])

    def memset(self, eng, out, val):
        return self._add(eng, lambda e: e.memset(out.ap, val), [], [out])

    def recip(self, out, in_):
        return self._add("dve", lambda e: e.reciprocal(out.ap, in_.ap), [in_], [out])

    def reduce(self, eng, out, in_, op, axis=AX.X):
        return self._add(eng, lambda e: e.tensor_reduce(out.ap, in_.ap, axis, op), [in_], [out])

    def cc_allgather(self, out_ap, in_ap, groups):
        if not hasattr(self, "cc_ent"):
            self.cc_ent = [self.nc.alloc_semaphore("ccsem"), 0, "cc"]
            self.dsem_all.append(self.cc_ent)
        return self._add("pool", lambda e: e.collective_compute("AllGather", op=ALU.bypass, replica_groups=groups,
                                                                ins=[in_ap], outs=[out_ap]),
                         [], [], is_dma=True, dgroup=self.cc_ent)

    def dma(self, q, out, in_, **kw):
        cls = "sw" if q == "pool" else "hw"
        if isinstance(out, View):
            grp = self._dsem(out.buf, cls)
            return self._add(q, lambda e: e.dma_start(out.ap, in_, **kw), [], [out], is_dma=True, dgroup=grp)
        grp = self._dsem(in_.buf, cls)
        return self._add(q, lambda e: e.dma_start(out, in_.ap, **kw), [in_], [], is_dma=True, dgroup=grp)

import math
from contextlib import ExitStack
from concourse.bass_utils import run_bass_kernel_spmd

D = 1024
CTX = 256
FF = 2816
EPS = 1e-6
GRID_W = 64
NEG = -30000.0
KINDS = (0, 1, 2, 0)


def lambda_init(i):
    return 0.8 - 0.6 * math.exp(-0.3 * i)


class Ctx:
    pass


def pipeline(n, st_qk, st_mid, st_pv):
    if n == 0:
        return
    st_qk(0)
    for i in range(n):
        if i + 1 < n:
            st_qk(i + 1)
        st_mid(i)
        st_pv(i)


def rsqrt(P, out, in_, mul):
    P.ts("dve", out, in_, mul, ALU.mult, EPS, ALU.add)
    P.act(out, out, AF.Sqrt)
    P.recip(out, out)


_UID = [0]


def sbt(nc, es, name, shape, dtype):
    _UID[0] += 1
    name = "%s_u%d" % (name, _UID[0])
    return Tile(es.enter_context(nc.sbuf_tensor(name, list(shape), dtype)), name)


def build(S, kinds=KINDS):
    NL = len(kinds)
    KINDS_ = kinds
    TOK = CTX + S
    NK = TOK
    ROWS = S // GRID_W
    nc = bass.Bass("TRN2", target_bir_lowering=False)
    g = Ctx()
    g.nc = nc
    g.S, g.TOK, g.NK, g.ROWS, g.NL = S, TOK, NK, ROWS, NL

    def din(name, shape, dtype=F32):
        return nc.dram_tensor(name, list(shape), dtype, kind="ExternalInput").ap()

    def dscr(name, shape, dtype):
        return nc.dram_tensor(name, list(shape), dtype, kind="Internal").ap()

    g.xc = din("xc", [TOK, D])
    g.cc = din("cc", [2, D])
    g.ident_d = din("ident", [128, 128])
    g.ropeA = din("ropeA", [2, 128, TOK])
    g.ropeB = din("ropeB", [2, 128, TOK])
    g.W = []
    for l in range(NL):
        p = "l%d_" % l
        w = {}
        w["w_mod"] = din(p + "w_mod", [D, 6 * D])
        w["b_mod"] = din(p + "b_mod", [6 * D])
        w["g_norm"] = din(p + "g_norm", [4, D])
        w["w_gu"] = din(p + "w_gu", [D, 2 * FF])
        w["w_down"] = din(p + "w_down", [FF, D])
        k = KINDS_[l]
        if k == 0:
            w["w_qkv"] = din(p + "a_w_qkv", [D, 3 * D])
            w["w_o"] = din(p + "a_w_o", [D, D])
            w["lam"] = din(p + "a_lam", [4, 64])
            w["g_sub"] = din(p + "a_g_sub", [128])
        elif k == 1:
            w["w_in"] = din(p + "b_w_in", [D, 544])
            w["g_q"] = din(p + "b_g_q", [256])
            w["g_kv"] = din(p + "b_g_kv", [256])
            w["w_uq"] = din(p + "b_w_uq", [256, 1536])
            w["w_ukv"] = din(p + "b_w_ukv", [256, 2048])
            w["w_o"] = din(p + "b_w_o", [D, D])
        else:
            w["w_qkv"] = din(p + "c_w_qkv", [D, 3 * D])
            w["rpb_tab"] = din(p + "c_rpb_tab", [16, 15, 64, 64])
            w["w_o"] = din(p + "c_w_o", [D, D])
        g.W.append(w)
    g.y = nc.dram_tensor("y", [S, D], F32, kind="ExternalOutput").ap()

    g.xres = dscr("xres", [TOK, D], F32)
    g.hT = dscr("hT", [D, TOK], BF16)
    g.qT = dscr("qT", [1536, TOK], BF16)
    g.kT = dscr("kT", [1056, NK], BF16)
    g.vv = dscr("vv", [NK, 1040], BF16)
    g.oT = dscr("oT", [D, TOK], BF16)

    g.blocks = [(0, CTX, True)] + [(CTX + i * 512, 512, False) for i in range(S // 512)]

    P = Prog(nc)
    g.P = P
    g.ident_f = Tile(nc.alloc_sbuf_tensor("ident_f", [128, 128], F32), "ident_f")
    g.ident_b = Tile(nc.alloc_sbuf_tensor("ident_b", [128, 128], BF16), "ident_b")
    g.ident8 = Tile(nc.alloc_sbuf_tensor("ident8", [128, 128], BF16), "ident8")
    g.ones_f = Tile(nc.alloc_sbuf_tensor("ones_f", [128, 128], F32), "ones_f")
    g.sel65 = Tile(nc.alloc_sbuf_tensor("sel65", [65, 64], F32), "sel65")
    g.MV = [Tile(nc.alloc_sbuf_tensor("MV%d" % l, [128, 48, 2], F32), "MV%d" % l) for l in range(NL)]
    g.DV = [Tile(nc.alloc_sbuf_tensor("DV%d" % l, [128, 4, 8, 2], F32), "DV%d" % l) for l in range(NL)]
    g.gn = [Tile(nc.alloc_sbuf_tensor("gn%d" % l, [128, 4, 8], F32), "gn%d" % l) for l in range(NL)]
    g.psall = nc.alloc_psum_tensor("psall", [128, 8, 512], F32)

    with nc.allow_non_contiguous_dma(reason="small strided parameter loads"):
        phase_init(g)
        phase_mod(g)
        phase_norm(g, 0, first=True)
        for l in range(NL):
            k = KINDS_[l]
            last = (l == NL - 1)
            if k == 0:
                phase_projA(g, l)
                phase_attnA(g, l)
            elif k == 1:
                phase_projB(g, l)
                phase_attnB(g, l)
            else:
                phase_projC(g, l)
                phase_attnC(g, l)
            phase_post(g, l)
            phase_ffn(g, l, final=(l == NL - 1), last=last)
    import sys as _sys
    print("[build] S=%d kinds=%s ops=%d instructions=%d" % (S, str(kinds), P.n_emitted, nc.n_instructions()), file=_sys.stderr)
    return nc


def bank(g, i, name, dtype=None, n=1):
    ap = g.psall[:, i:i + n, :].rearrange("p a b -> p (a b)")
    if dtype is not None:
        ap = ap.bitcast(dtype)
    return Tile(ap, name)


def phase_init(g):
    P, nc = g.P, g.nc
    P.dma("sp", g.ident_f[:], g.ident_d)
    P.copy("dve", g.ident_b[:], g.ident_f[:])
    P.ts("dve", g.ident8[:], g.ident_f[:], 8.0, ALU.mult)
    P.memset("dve", g.ones_f[:], 1.0)
    P.memset("dve", g.sel65[:], 0.0)
    P.memset("dve", g.sel65[64:65, :], 1.0)
    P.flush()


def phase_mod(g):
    P, nc = g.P, g.nc
    with ExitStack() as es:
        cs = sbt(nc, es, "cs", [128, 8, 2], F32)
        sT = sbt(nc, es, "sT", [128, 8, 2], F32)
        wm = [sbt(nc, es, "wm%d" % i, [128, 8, 1024], F32) for i in range(2)]
        bT = [sbt(nc, es, "bT%d" % i, [128, 48], F32) for i in range(2)]
        tmp = sbt(nc, es, "tmpm", [128, 8], F32)
        pst = [bank(g, i, "psm%d" % i) for i in range(4)]
        for r in range(2):
            P.dma("sp", cs[:, :, r], g.cc[r, :].rearrange("(k p) -> p k", p=128))
        P.act(sT[:], cs[:], AF.Silu)
        n = 0
        for l in range(g.NL):
            w = g.W[l]
            b = bT[l % 2]
            P.dma("sp", b[:], w["b_mod"].rearrange("(j p) -> p j", p=128))
            for r in range(4):
                P.dma("sp", g.gn[l][:, r, :], w["g_norm"][r, :].rearrange("(c p) -> p c", p=128))
            for nb in range(6):
                wt = wm[n % 2]
                n += 1
                for k in range(8):
                    P.dma("sp" if k % 2 == 0 else "act", wt[:, k, :],
                          w["w_mod"][k * 128:(k + 1) * 128, nb * 1024:(nb + 1) * 1024])
                for j in range(8):
                    ps = pst[j % 4]
                    for k in range(8):
                        P.mm(ps[:, 0:2], wt[:, k, j * 128:(j + 1) * 128], sT[:, k, :],
                             start=(k == 0), stop=(k == 7))
                    P.ts("dve", g.MV[l][:, nb * 8 + j, :], ps[:, 0:2], b[:, nb * 8 + j:nb * 8 + j + 1], ALU.add)
            MV, DV, gn = g.MV[l], g.DV[l], g.gn[l]
            for r in range(2):
                P.stt("dve", DV[:, 0, :, r], MV[:, 8:16, r], 1.0, gn[:, 0, :], ALU.add, ALU.mult)
                P.stt("dve", DV[:, 1, :, r], MV[:, 32:40, r], 1.0, gn[:, 2, :], ALU.add, ALU.mult)
                P.tt("dve", DV[:, 2, :, r], MV[:, 16:24, r], gn[:, 1, :], ALU.mult)
                P.tt("dve", DV[:, 3, :, r], MV[:, 40:48, r], gn[:, 3, :], ALU.mult)
        P.flush()


class NormT:
    def __init__(self, g, es, bank_ids):
        nc = g.nc
        self.g = g
        self.junk = [sbt(nc, es, "nt_junk%d" % i, [128, 1024], BF16) for i in range(2)]
        self.ss = [sbt(nc, es, "nt_ss%d" % i, [128, 1], F32) for i in range(2)]
        self.rstd = [sbt(nc, es, "nt_rstd%d" % i, [128, 1], F32) for i in range(2)]
        self.xn = [sbt(nc, es, "nt_xn%d" % i, [128, 1024], BF16) for i in range(2)]
        self.pT = [bank(g, b, "nt_pT%d" % b, BF16) for b in bank_ids]
        self.n = 0

    def __call__(self, xt, l, which, r, hblk, col):
        g, P = self.g, self.g.P
        i = self.n % 2
        pT = self.pT[self.n % len(self.pT)]
        self.n += 1
        junk, ss, rstd, xn = self.junk[i], self.ss[i], self.rstd[i], self.xn[i]
        P.memset("dve", ss[:], 0.0)
        P.act(junk[:], xt, AF.Square, accum_out=ss[:])
        rsqrt(P, rstd[:], ss[:], 1.0 / D)
        P.ts("pool", xn[:], xt, rstd[:, 0:1], ALU.mult)
        for c in range(8):
            P.transpose(pT[:, c * 128:(c + 1) * 128], xn[:, c * 128:(c + 1) * 128], g.ident_b[:])
        A = g.DV[l]
        MV = g.MV[l]
        boff = 0 if which == 0 else 24
        for c in range(8):
            P.act(hblk[:, c, col:col + 128], pT[:, c * 128:(c + 1) * 128], AF.Identity,
                  bias=MV[:, boff + c, r:r + 1], scale=A[:, which, c, r:r + 1])


def phase_norm(g, l, first=False):
    P, nc = g.P, g.nc
    with ExitStack() as es:
        nt = NormT(g, es, [0, 1])
        xb = [sbt(nc, es, "pn_x%d" % i, [128, 1024], F32) for i in range(3)]
        hb = [sbt(nc, es, "pn_h%d" % i, [128, 8, 512], BF16) for i in range(2)]
        n = 0
        for bi, (t0, w, isctx) in enumerate(g.blocks):
            h = hb[bi % 2]
            for i in range(w // 128):
                xt = xb[n % 3]
                n += 1
                P.dma("sp", xt[:], g.xc[t0 + i * 128:t0 + (i + 1) * 128, :])
                nt(xt[:], l, 0, 1 if isctx else 0, h, i * 128)
            P.dma("pool", g.hT[:, t0:t0 + w].rearrange("(c p) t -> p c t", p=128), h[:, :, 0:w])
        P.flush()


def load_w_cast(P, tile, src, nk, c0, c1, q="pool"):
    for k in range(nk):
        P.dma(q, tile[:, k, 0:c1 - c0], src[k * 128:(k + 1) * 128, c0:c1], max_dma_last_dim=4096)


def load_w_cast_swapped(P, tile, src, nk, c0, ngroups, half, q="pool"):
    gw = 2 * half
    for k in range(nk):
        dst = tile.h[:, k, 0:ngroups * gw].rearrange("p (g two i) -> p g two i", two=2, i=half)
        s = src[k * 128:(k + 1) * 128, c0:c0 + ngroups * gw].rearrange("p (g two i) -> p g two i", two=2, i=half)
        P.dma(q, tile.v(dst[:, :, 0, :]), s[:, :, 1, :], max_dma_last_dim=4096)
        P.dma(q, tile.v(dst[:, :, 1, :]), s[:, :, 0, :], max_dma_last_dim=4096)


def phase_projA(g, l):
    P, nc, w = g.P, g.nc, g.W[l]
    with ExitStack() as es:
        wq = sbt(nc, es, "wq", [128, 8, 1024], BF16)
        wqp = sbt(nc, es, "wqp", [128, 8, 1024], BF16)
        wk = sbt(nc, es, "wk", [128, 8, 1024], BF16)
        wkp = sbt(nc, es, "wkp", [128, 8, 1024], BF16)
        wv = sbt(nc, es, "wv", [128, 8, 1024], BF16)
        hb = [sbt(nc, es, "hb%d" % i, [128, 8, 512], BF16) for i in range(2)]
        rp = [sbt(nc, es, "rp%d" % i, [128, 2, 512], F32) for i in range(2)]
        t1 = [sbt(nc, es, "t1_%d" % i, [128, 512], F32) for i in range(2)]
        t2 = [sbt(nc, es, "t2_%d" % i, [128, 512], F32) for i in range(2)]
        ro = [sbt(nc, es, "ro%d" % i, [128, 512], BF16) for i in range(2)]
        vt = [sbt(nc, es, "vt%d" % i, [128, 1024], BF16) for i in range(2)]
        psA = [bank(g, i, "psA%d" % i) for i in (0, 1)]
        psB = [bank(g, i, "psB%d" % i) for i in (2, 3)]
        psV = [bank(g, i, "psV%d" % i) for i in (4, 5)]
        load_w_cast(P, wq, w["w_qkv"], 8, 0, 1024)
        load_w_cast_swapped(P, wqp, w["w_qkv"], 8, 0, 16, 32)
        load_w_cast(P, wk, w["w_qkv"], 8, 1024, 2048)
        load_w_cast_swapped(P, wkp, w["w_qkv"], 8, 1024, 16, 32)
        load_w_cast(P, wv, w["w_qkv"], 8, 2048, 3072)
        n = 0
        m = 0
        for bi, (t0, wd, isctx) in enumerate(g.blocks):
            h = hb[bi % 2]
            r = rp[bi % 2]
            P.dma("sp", h[:, :, 0:wd], g.hT[:, t0:t0 + wd].rearrange("(c p) t -> p c t", p=128))
            P.dma("sp", r[:, :, 0:wd], g.ropeA[:, :, t0:t0 + wd].rearrange("a p t -> p a t"))
            for (wt, wtp, dst) in ((wq, wqp, g.qT), (wk, wkp, g.kT)):
                for fc in range(8):
                    pa, pb = psA[n % 2], psB[n % 2]
                    a1, a2, o = t1[n % 2], t2[n % 2], ro[n % 2]
                    n += 1
                    for kc in range(8):
                        P.mm(pa[:, 0:wd], wt[:, kc, fc * 128:(fc + 1) * 128], h[:, kc, 0:wd], start=(kc == 0), stop=(kc == 7))
                    for kc in range(8):
                        P.mm(pb[:, 0:wd], wtp[:, kc, fc * 128:(fc + 1) * 128], h[:, kc, 0:wd], start=(kc == 0), stop=(kc == 7))
                    P.tt("dve", a1[:, 0:wd], pa[:, 0:wd], r[:, 0, 0:wd], ALU.mult)
                    P.tt("dve", a2[:, 0:wd], pb[:, 0:wd], r[:, 1, 0:wd], ALU.mult)
                    P.tt("pool", o[:, 0:wd], a1[:, 0:wd], a2[:, 0:wd], ALU.add)
                    P.dma("pool", dst[fc * 128:(fc + 1) * 128, t0:t0 + wd], o[:, 0:wd])
            for i in range(wd // 128):
                v = vt[m % 2]
                for half in range(2):
                    pv = psV[(2 * m + half) % 2]
                    for kc in range(8):
                        P.mm(pv[:, :], h[:, kc, i * 128:(i + 1) * 128], wv[:, kc, half * 512:(half + 1) * 512],
                             start=(kc == 0), stop=(kc == 7))
                    P.copy("act", v[:, half * 512:(half + 1) * 512], pv[:, :])
                m += 1
                P.dma("pool", g.vv[t0 + i * 128:t0 + (i + 1) * 128, 0:1024], v[:])
        P.flush()


def phase_attnA(g, l):
    P, nc, w = g.P, g.nc, g.W[l]
    NK = g.NK
    NKC = NK // 128
    scale = 64 ** -0.5
    li = lambda_init(l)
    with ExitStack() as es:
        kh = [sbt(nc, es, "kh%d" % i, [128, NK], BF16) for i in range(2)]
        vh = [sbt(nc, es, "vh%d" % i, [128, NKC, 128], BF16) for i in range(2)]
        qh = [sbt(nc, es, "qh%d" % i, [128, 512], BF16) for i in range(2)]
        ee = [sbt(nc, es, "ee_%d" % i, [128, 2, 512], BF16) for i in range(3)]
        accD = [sbt(nc, es, "accD_%d" % i, [128, 2, 512], F32) for i in range(2)]
        accP = [sbt(nc, es, "accP_%d" % i, [128, 2, 512], F32) for i in range(2)]
        rc1 = sbt(nc, es, "rc1", [128, 512], F32)
        rc2 = sbt(nc, es, "rc2", [128, 512], F32)
        u1 = sbt(nc, es, "u1", [128, 512], F32)
        u2 = sbt(nc, es, "u2", [128, 512], F32)
        oo = sbt(nc, es, "oo", [128, 512], F32)
        sq = sbt(nc, es, "sq", [128, 512], F32)
        rs = sbt(nc, es, "rs", [128, 512], F32)
        ob = [sbt(nc, es, "ob%d" % i, [128, 512], BF16) for i in range(2)]
        lt = sbt(nc, es, "lt", [1, 4, 64], F32)
        pr = sbt(nc, es, "pr", [1, 2, 64], F32)
        sm = sbt(nc, es, "sm", [1, 2], F32)
        ex = sbt(nc, es, "ex", [1, 2], F32)
        l1 = sbt(nc, es, "l1", [1, 2], F32)
        neglam = sbt(nc, es, "neglam", [128, 2], F32)
        gs = sbt(nc, es, "gs", [128, 1], F32)
        ps_s = [Tile(g.psall[:, 0:2, :], "ps_sA0"), Tile(g.psall[:, 2:4, :], "ps_sA1")]
        ps_o1 = bank(g, 4, "ps_o1")
        ps_o2 = bank(g, 5, "ps_o2")
        ps_r = [bank(g, 6, "ps_r0"), bank(g, 7, "ps_r1")]
        P.dma("sp", lt[:], w["lam"].rearrange("(o a) d -> o a d", o=1))
        P.dma("sp", gs[:], w["g_sub"].rearrange("(p o) -> p o", o=1))
        P.tt("dve", pr[:, 0, :], lt[:, 0, :], lt[:, 1, :], ALU.mult)
        P.tt("dve", pr[:, 1, :], lt[:, 2, :], lt[:, 3, :], ALU.mult)
        P.reduce("dve", sm[:], pr[:], ALU.add, AX.X)
        P.act(ex[:], sm[:], AF.Exp)
        P.tt("dve", l1[:, 0:1], ex[:, 0:1], ex[:, 1:2], ALU.subtract)
        P.ts("dve", l1[:, 0:1], l1[:, 0:1], li, ALU.add, -1.0, ALU.mult)
        P.copy("dve", l1[:, 1:2], l1[:, 0:1])
        P.mm(ps_r[0][:, 0:2], g.ones_f[0:1, :], l1[0:1, 0:2])
        P.copy("dve", neglam[:], ps_r[0][:, 0:2])
        P.ts("dve", gs[:], gs[:], 1.0 - li, ALU.mult)
        nblk = len(g.blocks)
        items = []
        for h in range(8):
            for bi, (t0, wd, isctx) in enumerate(g.blocks):
                chunks = [0, 1] if isctx else list(range(NKC))
                for ci, kc in enumerate(chunks):
                    items.append((h, bi, t0, wd, ci, kc, len(chunks)))

        def bufs(i):
            h, bi, t0, wd, ci, kc, n = items[i]
            qi = h * nblk + bi
            return kh[h % 2], vh[h % 2], qh[qi % 2], accD[qi % 2], accP[qi % 2], ps_s[i % 2], ee[i % 3], ob[qi % 2]

        def st_qk(i):
            h, bi, t0, wd, ci, kc, n = items[i]
            k_t, v_t, q, aD, aP, s, x, o_b = bufs(i)
            if ci == 0 and bi == 0:
                P.dma("sp", k_t[:, :], g.kT[h * 128:(h + 1) * 128, :])
                P.dma("sp", v_t[:], g.vv[:, h * 128:(h + 1) * 128].rearrange("(c p) d -> p c d", p=128))
            if ci == 0:
                P.dma("sp", q[:, 0:wd], g.qT[h * 128:(h + 1) * 128, t0:t0 + wd])
            P.mm(s[:, 0, 0:wd], k_t[0:64, kc * 128:(kc + 1) * 128], q[0:64, 0:wd])
            P.mm(s[:, 1, 0:wd], k_t[64:128, kc * 128:(kc + 1) * 128], q[64:128, 0:wd])

        def st_exp(i):
            h, bi, t0, wd, ci, kc, n = items[i]
            k_t, v_t, q, aD, aP, s, x, o_b = bufs(i)
            P.act(x[:, :, 0:wd], s[:, :, 0:wd], AF.Exp, scale=scale)

        def st_pv(i):
            h, bi, t0, wd, ci, kc, n = items[i]
            k_t, v_t, q, aD, aP, s, x, o_b = bufs(i)
            first, lastc = (ci == 0), (ci == n - 1)
            P.mm(ps_o1[:, 0:wd], v_t[:, kc, :], x[:, 0, 0:wd], start=first, stop=lastc)
            P.mm(ps_o2[:, 0:wd], v_t[:, kc, :], x[:, 1, 0:wd], start=first, stop=lastc)
            eng, a = ("dve", aD) if ci % 2 == 0 else ("pool", aP)
            if ci < 2:
                P.copy(eng, a[:, :, 0:wd], x[:, :, 0:wd])
            else:
                P.tt(eng, a[:, :, 0:wd], a[:, :, 0:wd], x[:, :, 0:wd], ALU.add)
            if not lastc:
                return
            P.tt("dve", aD[:, :, 0:wd], aD[:, :, 0:wd], aP[:, :, 0:wd], ALU.add)
            a1 = Tile(aD.h[:, 0, :], "a1v", buf=aD.buf)
            a2 = Tile(aD.h[:, 1, :], "a2v", buf=aD.buf)
            P.mm(ps_r[0][:, 0:wd], g.ones_f[:], a1[:, 0:wd])
            P.mm(ps_r[1][:, 0:wd], g.ones_f[:], a2[:, 0:wd])
            P.recip(rc1[:, 0:wd], ps_r[0][:, 0:wd])
            P.recip(rc2[:, 0:wd], ps_r[1][:, 0:wd])
            P.tt("dve", u1[:, 0:wd], ps_o1[:, 0:wd], rc1[:, 0:wd], ALU.mult)
            P.tt("dve", u2[:, 0:wd], ps_o2[:, 0:wd], rc2[:, 0:wd], ALU.mult)
            P.stt("dve", oo[:, 0:wd], u2[:, 0:wd], neglam[:, 0:1], u1[:, 0:wd], ALU.mult, ALU.add)
            P.tt("pool", sq[:, 0:wd], oo[:, 0:wd], oo[:, 0:wd], ALU.mult)
            P.mm(ps_r[0][:, 0:wd], g.ones_f[:], sq[:, 0:wd])
            rsqrt(P, rs[:, 0:wd], ps_r[0][:, 0:wd], 1.0 / 128)
            P.stt("dve", o_b[:, 0:wd], oo[:, 0:wd], gs[:, 0:1], rs[:, 0:wd], ALU.mult, ALU.mult)
            P.dma("pool", g.oT[h * 128:(h + 1) * 128, t0:t0 + wd], o_b[:, 0:wd])

        pipeline(len(items), st_qk, st_exp, st_pv)
        P.flush()


def phase_post(g, l):
    P, nc, w = g.P, g.nc, g.W[l]
    src = g.xc if l == 0 else g.xres
    with ExitStack() as es:
        wo = sbt(nc, es, "wo", [128, 8, 1024], BF16)
        gbc = [sbt(nc, es, "gbc%d" % r, [128, 1024], F32) for r in range(2)]
        dg = sbt(nc, es, "dg", [128, 128], F32)
        ob = [sbt(nc, es, "pob%d" % i, [128, 8, 512], BF16) for i in range(2)]
        hb = [sbt(nc, es, "phb%d" % i, [128, 8, 512], BF16) for i in range(2)]
        xb = [sbt(nc, es, "pxb%d" % i, [128, 1024], F32) for i in range(2)]
        xo = [sbt(nc, es, "pxo%d" % i, [128, 1024], F32) for i in range(2)]
        tm = [sbt(nc, es, "ptm%d" % i, [128, 1024], F32) for i in range(2)]
        junk = sbt(nc, es, "pjunk", [128, 1024], BF16)
        ss = [sbt(nc, es, "pss%d" % i, [128, 1], F32) for i in range(2)]
        rstd = [sbt(nc, es, "prstd%d" % i, [128, 1], F32) for i in range(2)]
        nt = NormT(g, es, [4, 5])
        psy = [bank(g, 0, "psy0", n=2), bank(g, 2, "psy1", n=2)]
        psg = bank(g, 6, "psg")
        load_w_cast(P, wo, w["w_o"], 8, 0, 1024)
        make_gbc(g, l, 2, gbc, dg, psg)
        n = 0
        for bi, (t0, wd, isctx) in enumerate(g.blocks):
            o = ob[bi % 2]
            h = hb[bi % 2]
            r = 1 if isctx else 0
            P.dma("sp", o[:, :, 0:wd], g.oT[:, t0:t0 + wd].rearrange("(c p) t -> p c t", p=128))
            for i in range(wd // 128):
                py = psy[n % 2]
                xt, xn_, t_, s_, r_ = xb[n % 2], xo[n % 2], tm[n % 2], ss[n % 2], rstd[n % 2]
                n += 1
                rows = slice(t0 + i * 128, t0 + (i + 1) * 128)
                P.dma("sp", xt[:], src[rows, :])
                for half in range(2):
                    for kc in range(8):
                        P.mm(py[:, half * 512:(half + 1) * 512], o[:, kc, i * 128:(i + 1) * 128],
                             wo[:, kc, half * 512:(half + 1) * 512], start=(kc == 0), stop=(kc == 7))
                residual_update(P, py, xt, xn_, t_, s_, r_, junk, gbc[r])
                P.dma("pool", g.xres[rows, :], xn_[:])
                nt(xn_[:], l, 1, r, h, i * 128)
            P.dma("pool", g.hT[:, t0:t0 + wd].rearrange("(c p) t -> p c t", p=128), h[:, :, 0:wd])
        P.flush()


def residual_update(P, py, xt, xnew, tmp, ss, rstd, junk, gbc):
    P.memset("dve", ss[:], 0.0)
    P.act(junk[:], py[:, :], AF.Square, accum_out=ss[:])
    rsqrt(P, rstd[:], ss[:], 1.0 / D)
    P.stt("dve", tmp[:], py[:, :], rstd[:, 0:1], gbc[:], ALU.mult, ALU.mult)
    P.tt("pool", xnew[:], tmp[:], xt[:], ALU.add)


def make_gbc(g, l, which, gbc, dg, psg):
    P = g.P
    for r in range(2):
        for c in range(8):
            P.ts("dve", dg[:], g.ident_f[:], g.DV[l][:, which, c, r:r + 1], ALU.mult)
            P.mm(psg[:, 0:128], g.ones_f[:], dg[:])
            P.copy("dve", gbc[r][:, c * 128:(c + 1) * 128], psg[:, 0:128])


def phase_ffn(g, l, final, last):
    P, nc, w = g.P, g.nc, g.W[l]
    TB = 256
    NJ = FF // 128
    with ExitStack() as es:
        wgu = sbt(nc, es, "wgu", [128, 8, 2 * FF], BF16)
        wd_ = sbt(nc, es, "wdn", [128, NJ, 1024], BF16)
        gbc = [sbt(nc, es, "fgbc%d" % r, [128, 1024], F32) for r in range(2)]
        dg = sbt(nc, es, "fdg", [128, 128], F32)
        hb = [sbt(nc, es, "fhb%d" % i, [128, 8, TB], BF16) for i in range(2)]
        ho = [sbt(nc, es, "fho%d" % i, [128, 8, TB], BF16) for i in range(2)]
        at = sbt(nc, es, "fat", [128, NJ, TB], BF16)
        sg = [sbt(nc, es, "fsg%d" % i, [128, TB], F32) for i in range(2)]
        xb = [sbt(nc, es, "fxb%d" % i, [128, 1024], F32) for i in range(2)]
        xo = [sbt(nc, es, "fxo%d" % i, [128, 1024], F32) for i in range(2)]
        tm = sbt(nc, es, "ftm", [128, 1024], F32)
        junk = sbt(nc, es, "fjunk", [128, 1024], BF16)
        ss = [sbt(nc, es, "fss%d" % i, [128, 1], F32) for i in range(2)]
        rstd = [sbt(nc, es, "frstd%d" % i, [128, 1], F32) for i in range(2)]
        nt = NormT(g, es, [6]) if not final else None
        psgu = [bank(g, i, "psgu%d" % i) for i in (0, 1, 2, 3)]
        psf = bank(g, 4, "psf", n=2)
        psg = bank(g, 7, "fpsg")
        for k in range(8):
            for c in range(0, 2 * FF, 1408):
                P.dma("pool", wgu[:, k, c:c + 1408], w["w_gu"][k * 128:(k + 1) * 128, c:c + 1408], max_dma_last_dim=4096)
        for j in range(NJ):
            P.dma("pool", wd_[:, j, :], w["w_down"][j * 128:(j + 1) * 128, :], max_dma_last_dim=4096)
        make_gbc(g, l, 3, gbc, dg, psg)
        n = 0
        ng = 0
        nblk = g.TOK // TB
        for bi in range(nblk):
            t0 = bi * TB
            isctx = t0 < CTX
            r = 1 if isctx else 0
            if last and isctx:
                continue
            h = hb[bi % 2]
            hn = ho[bi % 2]
            P.dma("sp", h[:], g.hT[:, t0:t0 + TB].rearrange("(c p) t -> p c t", p=128))
            for j in range(NJ):
                pg, pu = psgu[(2 * ng) % 4], psgu[(2 * ng + 1) % 4]
                s_ = sg[ng % 2]
                ng += 1
                for kc in range(8):
                    P.mm(pg[:, 0:TB], wgu[:, kc, j * 128:(j + 1) * 128], h[:, kc, :], start=(kc == 0), stop=(kc == 7))
                for kc in range(8):
                    P.mm(pu[:, 0:TB], wgu[:, kc, FF + j * 128:FF + (j + 1) * 128], h[:, kc, :], start=(kc == 0), stop=(kc == 7))
                P.act(s_[:], pg[:, 0:TB], AF.Silu)
                P.tt("dve", at[:, j, :], s_[:], pu[:, 0:TB], ALU.mult)
            for i in range(TB // 128):
                xt, xn_, s2, r2 = xb[n % 2], xo[n % 2], ss[n % 2], rstd[n % 2]
                n += 1
                rows = slice(t0 + i * 128, t0 + (i + 1) * 128)
                P.dma("sp", xt[:], g.xres[rows, :])
                for half in range(2):
                    for j in range(NJ):
                        P.mm(psf[:, half * 512:(half + 1) * 512], at[:, j, i * 128:(i + 1) * 128],
                             wd_[:, j, half * 512:(half + 1) * 512], start=(j == 0), stop=(j == NJ - 1))
                residual_update(P, psf, xt, xn_, tm, s2, r2, junk, gbc[r])
                if final:
                    if not isctx:
                        P.dma("pool", g.y[t0 - CTX + i * 128:t0 - CTX + (i + 1) * 128, :], xn_[:])
                else:
                    P.dma("pool", g.xres[rows, :], xn_[:])
                    nt(xn_[:], l + 1, 0, r, hn, i * 128)
            if not final:
                P.dma("pool", g.hT[:, t0:t0 + TB].rearrange("(c p) t -> p c t", p=128), hn[:])
        P.flush()


def rot_store(P, pa, pb, rt, rows, wd, a1, a2, o, dst_ap):
    P.tt("dve", a1[0:rows, 0:wd], pa[0:rows, 0:wd], rt[0:rows, 0, 0:wd], ALU.mult)
    P.tt("dve", a2[0:rows, 0:wd], pb[0:rows, 0:wd], rt[0:rows, 1, 0:wd], ALU.mult)
    P.tt("pool", o[0:rows, 0:wd], a1[0:rows, 0:wd], a2[0:rows, 0:wd], ALU.add)
    P.dma("pool", dst_ap, o[0:rows, 0:wd])


def phase_projB(g, l):
    P, nc, w = g.P, g.nc, g.W[l]
    with ExitStack() as es:
        win = sbt(nc, es, "win", [128, 8, 544], BF16)
        winp = sbt(nc, es, "winp", [128, 8, 32], BF16)
        wuq = sbt(nc, es, "wuq", [128, 2, 1536], BF16)
        wuqp = sbt(nc, es, "wuqp", [128, 2, 1536], BF16)
        wkk = sbt(nc, es, "wkk", [128, 2, 1024], BF16)
        wkv = sbt(nc, es, "wkv", [128, 2, 1024], BF16)
        gq = sbt(nc, es, "gq", [128, 4], F32)
        hb = [sbt(nc, es, "bhb%d" % i, [128, 8, 512], BF16) for i in range(2)]
        rB = [sbt(nc, es, "rB%d" % i, [128, 2, 512], F32) for i in range(2)]
        rK = [sbt(nc, es, "rK%d" % i, [32, 2, 512], F32) for i in range(2)]
        sqt = [sbt(nc, es, "sqt%d" % i, [128, 512], F32) for i in range(2)]
        rsd = sbt(nc, es, "rsd", [128, 512], F32)
        cn = [sbt(nc, es, "cn%d" % i, [128, 2, 512], BF16) for i in range(2)]
        a1 = [sbt(nc, es, "ba1_%d" % i, [128, 512], F32) for i in range(2)]
        a2 = [sbt(nc, es, "ba2_%d" % i, [128, 512], F32) for i in range(2)]
        ro = [sbt(nc, es, "bro%d" % i, [128, 512], BF16) for i in range(2)]
        va = [sbt(nc, es, "bva%d" % i, [128, 16, 65], BF16) for i in range(2)]
        pz = [bank(g, 0, "pz0"), bank(g, 1, "pz1")]
        pss = bank(g, 2, "pss")
        pzr, pzrp = bank(g, 3, "pzr"), bank(g, 4, "pzrp")
        pq, pqp = bank(g, 5, "pq"), bank(g, 6, "pqp")
        pk = bank(g, 7, "pk")
        load_w_cast(P, win, w["w_in"], 8, 0, 544)
        load_w_cast_swapped(P, winp, w["w_in"], 8, 512, 1, 16)
        load_w_cast(P, wuq, w["w_uq"], 2, 0, 1536)
        for k in range(2):
            dst = wuqp.h[:, k, :].rearrange("p (h c) -> p h c", c=96)
            s = w["w_uq"][k * 128:(k + 1) * 128, :].rearrange("p (h c) -> p h c", c=96)
            P.dma("pool", wuqp.v(dst[:, :, 0:64]), s[:, :, 0:64], max_dma_last_dim=4096)
            P.dma("pool", wuqp.v(dst[:, :, 64:80]), s[:, :, 80:96], max_dma_last_dim=4096)
            P.dma("pool", wuqp.v(dst[:, :, 80:96]), s[:, :, 64:80], max_dma_last_dim=4096)
            s2 = w["w_ukv"][k * 128:(k + 1) * 128, :].rearrange("p (h c) -> p h c", c=128)
            P.dma("pool", wkk.v(wkk.h[:, k, :].rearrange("p (h c) -> p h c", c=64)), s2[:, :, 0:64], max_dma_last_dim=4096)
            P.dma("pool", wkv.v(wkv.h[:, k, :].rearrange("p (h c) -> p h c", c=64)), s2[:, :, 64:128], max_dma_last_dim=4096)
        P.dma("sp", gq[:, 0:2], w["g_q"].rearrange("(c p) -> p c", p=128))
        P.dma("sp", gq[:, 2:4], w["g_kv"].rearrange("(c p) -> p c", p=128))
        for v in va:
            P.memset("dve", v[:], 1.0)
        n = 0
        m = 0
        for bi, (t0, wd, isctx) in enumerate(g.blocks):
            h = hb[bi % 2]
            rb, rk = rB[bi % 2], rK[bi % 2]
            P.dma("sp", h[:, :, 0:wd], g.hT[:, t0:t0 + wd].rearrange("(c p) t -> p c t", p=128))
            P.dma("sp", rb[:, :, 0:wd], g.ropeB[:, :, t0:t0 + wd].rearrange("a p t -> p a t"))
            P.dma("sp", rk[:, :, 0:wd], g.ropeB[:, 64:96, t0:t0 + wd].rearrange("a p t -> p a t"))
            for which in range(2):
                c_ = cn[which]
                for c in range(2):
                    col = which * 256 + c * 128
                    for kc in range(8):
                        P.mm(pz[c][:, 0:wd], win[:, kc, col:col + 128], h[:, kc, 0:wd], start=(kc == 0), stop=(kc == 7))
                    P.act(sqt[c][:, 0:wd], pz[c][:, 0:wd], AF.Square)
                P.mm(pss[:, 0:wd], g.ones_f[:], sqt[0][:, 0:wd], start=True, stop=False)
                P.mm(pss[:, 0:wd], g.ones_f[:], sqt[1][:, 0:wd], start=False, stop=True)
                rsqrt(P, rsd[:, 0:wd], pss[:, 0:wd], 1.0 / 256)
                for c in range(2):
                    P.stt("dve", c_[:, c, 0:wd], pz[c][:, 0:wd], gq[:, which * 2 + c:which * 2 + c + 1], rsd[:, 0:wd],
                          ALU.mult, ALU.mult)
            for kc in range(8):
                P.mm(pzr[0:32, 0:wd], win[:, kc, 512:544], h[:, kc, 0:wd], start=(kc == 0), stop=(kc == 7))
            for kc in range(8):
                P.mm(pzrp[0:32, 0:wd], winp[:, kc, 0:32], h[:, kc, 0:wd], start=(kc == 0), stop=(kc == 7))
            rot_store(P, pzr, pzrp, rk, 32, wd, a1[n % 2], a2[n % 2], ro[n % 2], g.kT[1024:1056, t0:t0 + wd])
            n += 1
            cq, ckv = cn[0], cn[1]
            for hh in range(16):
                for kc in range(2):
                    P.mm(pq[0:96, 0:wd], wuq[:, kc, hh * 96:(hh + 1) * 96], cq[:, kc, 0:wd], start=(kc == 0), stop=(kc == 1))
                for kc in range(2):
                    P.mm(pqp[0:96, 0:wd], wuqp[:, kc, hh * 96:(hh + 1) * 96], cq[:, kc, 0:wd], start=(kc == 0), stop=(kc == 1))
                rot_store(P, pq, pqp, rb, 96, wd, a1[n % 2], a2[n % 2], ro[n % 2], g.qT[hh * 96:(hh + 1) * 96, t0:t0 + wd])
                n += 1
            for fc in range(8):
                for kc in range(2):
                    P.mm(pk[:, 0:wd], wkk[:, kc, fc * 128:(fc + 1) * 128], ckv[:, kc, 0:wd], start=(kc == 0), stop=(kc == 1))
                o = ro[n % 2]
                n += 1
                P.copy("act", o[:, 0:wd], pk[:, 0:wd])
                P.dma("pool", g.kT[fc * 128:(fc + 1) * 128, t0:t0 + wd], o[:, 0:wd])
            for i in range(wd // 128):
                v = va[m % 2]
                m += 1
                for half in range(2):
                    for kc in range(2):
                        P.mm(pk[:, :], ckv[:, kc, i * 128:(i + 1) * 128], wkv[:, kc, half * 512:(half + 1) * 512],
                             start=(kc == 0), stop=(kc == 1))
                    P.copy("act", v[:, half * 8:(half + 1) * 8, 0:64], pk.v(pk.h[:, :].rearrange("p (h d) -> p h d", d=64)))
                P.dma("pool", g.vv[t0 + i * 128:t0 + (i + 1) * 128, 0:1040], v.v(v.h[:, :, :].rearrange("p h d -> p (h d)")))
        P.flush()


def attn_single(g, l, nheads, qrows, krow_loader, scale):
    P, nc = g.P, g.nc
    NK = g.NK
    NKC = NK // 128
    with ExitStack() as es:
        kh = [sbt(nc, es, "bkh%d" % i, [qrows, NK], BF16) for i in range(2)]
        vh = [sbt(nc, es, "bvh%d" % i, [128, NKC, 65], BF16) for i in range(2)]
        qh = [sbt(nc, es, "bqh%d" % i, [qrows, 512], BF16) for i in range(2)]
        ee = [sbt(nc, es, "bee%d" % i, [128, 512], BF16) for i in range(3)]
        osb = [sbt(nc, es, "bosb%d" % i, [65, 512], F32) for i in range(2)]
        rc = sbt(nc, es, "brc", [64, 512], F32)
        ob = [sbt(nc, es, "bob%d" % i, [64, 512], BF16) for i in range(2)]
        ps_s = [bank(g, i, "bps_s%d" % i) for i in (0, 1)]
        ps_o = [bank(g, i, "bps_o%d" % i) for i in (2, 3)]
        ps_b = bank(g, 4, "bps_b")
        nblk = len(g.blocks)
        items = []
        for h in range(nheads):
            for bi, (t0, wd, isctx) in enumerate(g.blocks):
                chunks = [0, 1] if isctx else list(range(NKC))
                for ci, kc in enumerate(chunks):
                    items.append((h, bi, t0, wd, ci, kc, len(chunks)))

        def bufs(i):
            h, bi, t0, wd, ci, kc, n = items[i]
            qi = h * nblk + bi
            return kh[h % 2], vh[h % 2], qh[qi % 2], ps_o[qi % 2], osb[qi % 2], ob[qi % 2], ps_s[i % 2], ee[i % 3]

        def st_qk(i):
            h, bi, t0, wd, ci, kc, n = items[i]
            k_t, v_t, q, po, os_, o_b, s, x = bufs(i)
            if ci == 0 and bi == 0:
                krow_loader(P, k_t, h)
                P.dma("sp", v_t[:], g.vv[:, h * 65:(h + 1) * 65].rearrange("(c p) d -> p c d", p=128))
            if ci == 0:
                P.dma("sp", q[:, 0:wd], g.qT[h * qrows:(h + 1) * qrows, t0:t0 + wd])
            P.mm(s[:, 0:wd], k_t[:, kc * 128:(kc + 1) * 128], q[:, 0:wd])

        def st_exp(i):
            h, bi, t0, wd, ci, kc, n = items[i]
            k_t, v_t, q, po, os_, o_b, s, x = bufs(i)
            P.act(x[:, 0:wd], s[:, 0:wd], AF.Exp, scale=scale)

        def st_pv(i):
            h, bi, t0, wd, ci, kc, n = items[i]
            k_t, v_t, q, po, os_, o_b, s, x = bufs(i)
            P.mm(po[0:65, 0:wd], v_t[:, kc, :], x[:, 0:wd], start=(ci == 0), stop=(ci == n - 1))
            if ci != n - 1:
                return
            P.copy("dve", os_[:, 0:wd], po[0:65, 0:wd])
            P.mm(ps_b[0:64, 0:wd], g.sel65[:, :], os_[:, 0:wd])
            P.recip(rc[:, 0:wd], ps_b[0:64, 0:wd])
            P.tt("dve", o_b[:, 0:wd], os_[0:64, 0:wd], rc[:, 0:wd], ALU.mult)
            P.dma("pool", g.oT[h * 64:(h + 1) * 64, t0:t0 + wd], o_b[:, 0:wd])

        pipeline(len(items), st_qk, st_exp, st_pv)
        P.flush()


def phase_attnB(g, l):
    def loader(P, k_t, h):
        P.dma("sp", k_t[0:64, :], g.kT[h * 64:(h + 1) * 64, :])
        P.dma("sp", k_t[64:96, :], g.kT[1024:1056, :])
    attn_single(g, l, 16, 96, loader, 96 ** -0.5)


def phase_projC(g, l):
    P, nc, w = g.P, g.nc, g.W[l]
    with ExitStack() as es:
        wq = sbt(nc, es, "cwq", [128, 8, 1024], BF16)
        wk = sbt(nc, es, "cwk", [128, 8, 1024], BF16)
        wv = sbt(nc, es, "cwv", [128, 8, 1024], BF16)
        hb = [sbt(nc, es, "chb%d" % i, [128, 8, 512], BF16) for i in range(2)]
        ro = [sbt(nc, es, "cro%d" % i, [128, 512], BF16) for i in range(2)]
        va = [sbt(nc, es, "cva%d" % i, [128, 16, 65], BF16) for i in range(2)]
        psA = [bank(g, i, "cpsA%d" % i) for i in (0, 1)]
        psV = [bank(g, i, "cpsV%d" % i) for i in (2, 3)]
        load_w_cast(P, wq, w["w_qkv"], 8, 0, 1024)
        load_w_cast(P, wk, w["w_qkv"], 8, 1024, 2048)
        load_w_cast(P, wv, w["w_qkv"], 8, 2048, 3072)
        for v in va:
            P.memset("dve", v[:], 1.0)
        n = 0
        m = 0
        for bi, (t0, wd, isctx) in enumerate(g.blocks):
            h = hb[bi % 2]
            P.dma("sp", h[:, :, 0:wd], g.hT[:, t0:t0 + wd].rearrange("(c p) t -> p c t", p=128))
            for (wt, dst) in ((wq, g.qT), (wk, g.kT)):
                for fc in range(8):
                    pa = psA[n % 2]
                    o = ro[n % 2]
                    n += 1
                    for kc in range(8):
                        P.mm(pa[:, 0:wd], wt[:, kc, fc * 128:(fc + 1) * 128], h[:, kc, 0:wd], start=(kc == 0), stop=(kc == 7))
                    if n % 2:
                        P.copy("act", o[:, 0:wd], pa[:, 0:wd])
                    else:
                        P.copy("dve", o[:, 0:wd], pa[:, 0:wd])
                    P.dma("pool", dst[fc * 128:(fc + 1) * 128, t0:t0 + wd], o[:, 0:wd])
            for i in range(wd // 128):
                v = va[m % 2]
                for half in range(2):
                    pv = psV[(2 * m + half) % 2]
                    for kc in range(8):
                        P.mm(pv[:, :], h[:, kc, i * 128:(i + 1) * 128], wv[:, kc, half * 512:(half + 1) * 512],
                             start=(kc == 0), stop=(kc == 7))
                    P.copy("act", v[:, half * 8:(half + 1) * 8, 0:64], pv.v(pv.h[:, :].rearrange("p (h d) -> p h d", d=64)))
                m += 1
                P.dma("pool", g.vv[t0 + i * 128:t0 + (i + 1) * 128, 0:1040], v.v(v.h[:, :, :].rearrange("p h d -> p (h d)")))
        P.flush()


def phase_attnC(g, l):
    P, nc, w = g.P, g.nc, g.W[l]
    ROWS = g.ROWS
    scale = 64 ** -0.5
    with ExitStack() as es:
        tb = sbt(nc, es, "tb", [128, 16, 14, 64], BF16)
        kc_t = sbt(nc, es, "kc_t", [64, 16, 256], BF16)
        vc_t = sbt(nc, es, "vc_t", [128, 2, 1040], BF16)
        kb = [sbt(nc, es, "kb%d" % i, [64, 16, 512], BF16) for i in range(2)]
        vb = [sbt(nc, es, "vb%d" % i, [128, 4, 1040], BF16) for i in range(2)]
        qr = [sbt(nc, es, "qr%d" % i, [64, 16, 512], BF16) for i in range(2)]
        ee = [sbt(nc, es, "cee%d" % i, [128, 512], BF16) for i in range(3)]
        osb = [sbt(nc, es, "cosb%d" % i, [65, 512], F32) for i in range(2)]
        rc = sbt(nc, es, "crc", [64, 512], F32)
        obuf = [sbt(nc, es, "cobuf%d" % i, [64, 16, 512], BF16) for i in range(2)]
        ps_s = [bank(g, i, "cps_s%d" % i) for i in (0, 1)]
        ps_o = [bank(g, i, "cps_o%d" % i) for i in (2, 3)]
        ps_b = bank(g, 4, "cps_b")
        tab = w["rpb_tab"]
        for h in range(16):
            P.dma("pool", tb[0:64, h, :, :], tab[h, 0:14, :, :].rearrange("e k q -> k e q"), max_dma_last_dim=4096)
            P.dma("pool", tb[64:128, h, :, :], tab[h, 1:15, :, :].rearrange("e k q -> k e q"), max_dma_last_dim=4096)
        P.dma("sp", kc_t[:], g.kT[0:1024, 0:CTX].rearrange("(h d) t -> d h t", d=64))
        P.dma("sp", vc_t[:], g.vv[0:CTX, :].rearrange("(c p) d -> p c d", p=128))
        prow = [("c", i) for i in range(CTX // 64)] + [("l", r) for r in range(ROWS)]
        info = []
        for pi, (kind, r) in enumerate(prow):
            if kind == "c":
                gi, tq0, gidx, gw, nch, rs_ = r, 0, 0, CTX, 2, None
                lastg = (r == CTX // 64 - 1)
            else:
                gi, tq0, gidx, gw, nch = r % 8, CTX + (r // 8) * 512, 1 + r // 8, 512, 6
                rs_ = min(max(r - 4, 0), ROWS - 8)
                lastg = (gi == 7 or r == ROWS - 1)
            info.append((kind, r, gi, tq0, gidx, gw, nch, rs_, lastg))
        items = []
        for pi in range(len(prow)):
            for hg in range(2):
                for j in range(info[pi][6]):
                    items.append((pi, hg, j))

        def bufs(i):
            pi, hg, j = items[i]
            kind, r, gi, tq0, gidx, gw, nch, rs_, lastg = info[pi]
            gg = pi * 2 + hg
            return (qr[gidx % 2], obuf[gidx % 2], kb[pi % 2], vb[pi % 2], ps_s[i % 2], ee[i % 3], ps_o[gg % 2], osb[gg % 2])

        def st_qk(i):
            pi, hg, j = items[i]
            kind, r, gi, tq0, gidx, gw, nch, rs_, lastg = info[pi]
            q_t, o_t, k_b, v_b, s, x, po, os_ = bufs(i)
            if hg == 0 and j == 0:
                if gi == 0:
                    P.dma("sp", q_t[:, :, 0:gw], g.qT[0:1024, tq0:tq0 + gw].rearrange("(h d) t -> d h t", d=64))
                if kind == "l":
                    tk0 = CTX + rs_ * 64
                    P.dma("sp", k_b[:], g.kT[0:1024, tk0:tk0 + 512].rearrange("(h d) t -> d h t", d=64))
                    P.dma("sp", v_b[:], g.vv[tk0:tk0 + 512, :].rearrange("(c p) d -> p c d", p=128))
            for hh in range(8):
                h = hg * 8 + hh
                if j < 2:
                    kl = kc_t[:, h, j * 128:(j + 1) * 128]
                else:
                    kl = k_b[:, h, (j - 2) * 128:(j - 1) * 128]
                P.mm(s[:, hh * 64:(hh + 1) * 64], kl, q_t[:, h, gi * 64:(gi + 1) * 64],
                     start=(hh == 0), stop=(hh == 7 and j < 2))
            if j >= 2:
                e = rs_ + 2 * (j - 2) - r + 7
                assert 0 <= e <= 13
                P.mm(s[:, :], g.ident8[:], tb[:, hg * 8:(hg + 1) * 8, e, :], start=False, stop=True)

        def st_exp(i):
            q_t, o_t, k_b, v_b, s, x, po, os_ = bufs(i)
            P.act(x[:], s[:], AF.Exp, scale=scale)

        def st_pv(i):
            pi, hg, j = items[i]
            kind, r, gi, tq0, gidx, gw, nch, rs_, lastg = info[pi]
            q_t, o_t, k_b, v_b, s, x, po, os_ = bufs(i)
            for hh in range(8):
                h = hg * 8 + hh
                if j < 2:
                    vl = vc_t[:, j, h * 65:(h + 1) * 65]
                else:
                    vl = v_b[:, j - 2, h * 65:(h + 1) * 65]
                P.mm(po[0:65, hh * 64:(hh + 1) * 64], vl, x[:, hh * 64:(hh + 1) * 64],
                     start=(j == 0 and hh == 0), stop=(j == nch - 1 and hh == 7))
            if j != nch - 1:
                return
            P.copy("dve", os_[:], po[0:65, :])
            P.mm(ps_b[0:64, :], g.sel65[:, :], os_[:, :])
            P.recip(rc[:], ps_b[0:64, :])
            P.tt("dve", o_t[:, hg * 8:(hg + 1) * 8, gi * 64:(gi + 1) * 64],
                 os_.v(os_.h[0:64, :].rearrange("p (h q) -> p h q", q=64)),
                 rc.v(rc.h[:, :].rearrange("p (h q) -> p h q", q=64)), ALU.mult)
            if hg == 1 and lastg:
                P.dma("pool", g.oT[0:1024, tq0:tq0 + gw].rearrange("(h d) t -> d h t", d=64), o_t[:, :, 0:gw])

        pipeline(len(items), st_qk, st_exp, st_pv)
        P.flush()


def _rope_np(S, rot_dim):
    n_freq = rot_dim // 4
    inv = (np.float32(10000.0) ** (-(np.arange(n_freq, dtype=np.float32) / np.float32(n_freq)))).astype(np.float32)
    t = np.arange(S, dtype=np.int64)
    row = (t // GRID_W).astype(np.float32)
    col = (t % GRID_W).astype(np.float32)
    ang = np.concatenate([row[:, None] * inv, col[:, None] * inv], axis=-1).astype(np.float32)
    return np.cos(ang).astype(np.float32), np.sin(ang).astype(np.float32)


def _tables(S):
    TOK = CTX + S
    ca, sa = _rope_np(S, 64)
    cb, sb_ = _rope_np(S, 32)
    ropeA = np.zeros((2, 128, TOK), np.float32)
    ropeA[0] = 1.0
    ropeB = np.zeros((2, 128, TOK), np.float32)
    ropeB[0] = 1.0
    for p in range(128):
        d = p % 64
        i = d % 32
        ropeA[0, p, CTX:] = ca[:, i]
        ropeA[1, p, CTX:] = -sa[:, i] if d < 32 else sa[:, i]
    for p in range(64, 96):
        d = p - 64
        i = d % 16
        ropeB[0, p, CTX:] = cb[:, i]
        ropeB[1, p, CTX:] = -sb_[:, i] if d < 16 else sb_[:, i]
    return ropeA, ropeB


def _rpb_tab(rpb):
    qc = np.arange(64)
    cstart = np.clip(qc - 8, 0, 64 - 16)
    kc = np.arange(64)
    idx = kc[:, None] - qc[None, :] + 15
    inwin = (kc[:, None] >= cstart[None, :]) & (kc[:, None] < cstart[None, :] + 16)
    idxc = np.clip(idx, 0, 30)
    tab = rpb[:, :, idxc]
    tab = np.where(inwin[None, None], tab, np.float32(NEG)).astype(np.float32)
    return np.ascontiguousarray(tab)


_CACHE = {}


def run_model(inputs, S, kinds, n_cores=8):
    key = (S, tuple(kinds))
    if key not in _CACHE:
        _CACHE[key] = build(S, tuple(kinds))
    nc = _CACHE[key]
    ropeA, ropeB = _tables(S)
    ident = np.eye(128, dtype=np.float32)
    B = inputs["x"].shape[0]
    shared = {"ident": ident, "ropeA": ropeA, "ropeB": ropeB}
    for l, k in enumerate(kinds):
        p = "l%d_" % l
        for nm in ("w_mod", "b_mod", "g_norm", "w_gu", "w_down"):
            shared[p + nm] = np.ascontiguousarray(inputs[p + nm], dtype=np.float32)
        if k == 0:
            for nm in ("a_w_qkv", "a_w_o", "a_lam", "a_g_sub"):
                shared[p + nm] = np.ascontiguousarray(inputs[p + nm], dtype=np.float32)
        elif k == 1:
            for nm in ("b_w_in", "b_g_q", "b_g_kv", "b_w_uq", "b_w_ukv", "b_w_o"):
                shared[p + nm] = np.ascontiguousarray(inputs[p + nm], dtype=np.float32)
        else:
            for nm in ("c_w_qkv", "c_w_o"):
                shared[p + nm] = np.ascontiguousarray(inputs[p + nm], dtype=np.float32)
            shared[p + "c_rpb_tab"] = _rpb_tab(np.asarray(inputs[p + "c_rpb"], dtype=np.float32))
    in_maps = []
    for core in range(n_cores):
        b = core % B
        m = dict(shared)
        m["xc"] = np.ascontiguousarray(np.concatenate([inputs["ctx"][b], inputs["x"][b]], axis=0), dtype=np.float32)
        m["cc"] = np.ascontiguousarray(np.stack([inputs["c"][b], inputs["c_ctx"]], axis=0), dtype=np.float32)
        in_maps.append(m)
    res = run_bass_kernel_spmd(nc, in_maps, core_ids=list(range(n_cores)))
    out = np.stack([np.asarray(res.results[b]["y"]) for b in range(B)], axis=0)
    return out.astype(np.float32)


_INPUT_NAMES = (
    "x", "c", "ctx", "c_ctx",
    "l0_w_mod", "l0_b_mod", "l0_g_norm", "l0_w_gu", "l0_w_down", "l0_a_w_qkv", "l0_a_w_o", "l0_a_lam", "l0_a_g_sub",
    "l1_w_mod", "l1_b_mod", "l1_g_norm", "l1_w_gu", "l1_w_down", "l1_b_w_in", "l1_b_g_q", "l1_b_g_kv", "l1_b_w_uq",
    "l1_b_w_ukv", "l1_b_w_o",
    "l2_w_mod", "l2_b_mod", "l2_g_norm", "l2_w_gu", "l2_w_down", "l2_c_w_qkv", "l2_c_rpb", "l2_c_w_o",
    "l3_w_mod", "l3_b_mod", "l3_g_norm", "l3_w_gu", "l3_w_down", "l3_a_w_qkv", "l3_a_w_o", "l3_a_lam", "l3_a_g_sub",
)


def kernel(**inputs):
    assert all(n in inputs for n in _INPUT_NAMES)
    inputs = {k: np.asarray(v) for k, v in inputs.items()}
    return run_model(inputs, inputs["x"].shape[1], KINDS)
```

```python
import numpy as np
import concourse.bass as bass
import concourse.mybir as mybir

F32 = mybir.dt.float32
BF16 = mybir.dt.bfloat16
AF = mybir.ActivationFunctionType
ALU = mybir.AluOpType
AX = mybir.AxisListType

COMPUTE = ("pe", "act", "dve", "pool")
ENGINES = ("pe", "act", "dve", "pool", "sp")


class Buf:
    __slots__ = ("name", "last_w", "readers", "dsem", "is_dram")

    def __init__(self, name):
        self.name = name
        self.last_w = None
        self.readers = {}
        self.dsem = {}


class View:
    __slots__ = ("buf", "ap")

    def __init__(self, buf, ap):
        self.buf = buf
        self.ap = ap


class Tile:
    def __init__(self, handle, name, buf=None):
        self.h = handle
        self.buf = buf if buf is not None else Buf(name)

    def __getitem__(self, idx):
        return View(self.buf, self.h[idx])

    def v(self, ap):
        return View(self.buf, ap)

    def sub(self, name):
        return Tile(self.h, name)


class Op:
    __slots__ = ("eng", "fn", "reads", "writes", "is_dma", "deps", "needs_inc", "tick",
                 "dgroup", "dtarget", "idx", "done", "lhs_buf")


class Prog:
    def __init__(self, nc):
        self.nc = nc
        self.ops = []
        self.sem = {e: nc.alloc_semaphore("tick_" + e) for e in COMPUTE}
        self.tick = {e: 0 for e in COMPUTE}
        self.dsem_pool = {"hw": [], "sw": []}
        self.dsem_all = []
        self.known = {e: {} for e in ENGINES}
        self.n_emitted = 0
        self._phase_dsems = []
        self.same_engine_sync = True

    def _add(self, eng, fn, reads, writes, is_dma=False, dgroup=None):
        op = Op()
        op.eng = eng
        op.fn = fn
        op.reads = [v.buf for v in reads if v is not None]
        op.writes = [v.buf for v in writes if v is not None]
        op.is_dma = is_dma
        op.needs_inc = False
        op.tick = None
        op.dgroup = dgroup
        op.dtarget = None
        op.idx = len(self.ops)
        op.done = False
        op.lhs_buf = None
        deps = set()
        for b in op.reads:
            if b.last_w is not None:
                deps.add(b.last_w)
        for b in op.writes:
            lw = b.last_w
            if lw is not None:
                if is_dma and lw.is_dma and not b.readers and lw.dgroup is dgroup:
                    deps |= lw.deps
                else:
                    deps.add(lw)
            for r in b.readers.values():
                deps.add(r)
        deps.discard(op)
        deps = {d for d in deps if not d.done}
        op.deps = deps
        for b in op.reads:
            b.readers[("d", op.idx) if is_dma else eng] = op
        for b in op.writes:
            b.last_w = op
            b.readers = {}
        self.ops.append(op)
        return op

    def _dsem(self, buf, cls):
        if cls not in buf.dsem:
            pool = self.dsem_pool[cls]
            if pool:
                ent = pool.pop()
            else:
                ent = [self.nc.alloc_semaphore("dsem_%s%d" % (cls, len(self.dsem_all))), 0, cls]
                self.dsem_all.append(ent)
            buf.dsem[cls] = ent
            self._phase_dsems.append((buf, cls))
        return buf.dsem[cls]

    def flush(self, name=None):
        nc = self.nc
        ops = self.ops
        if not ops:
            return
        for op in ops:
            for d in op.deps:
                if not d.is_dma:
                    if d.eng == op.eng and (d.eng == "pe" or not self.same_engine_sync):
                        continue
                    d.needs_inc = True
        last = {}
        for op in ops:
            if not op.is_dma:
                last[op.eng] = op
        for e, op in last.items():
            op.needs_inc = True
        for op in ops:
            if op.is_dma:
                ent = op.dgroup
                ent[1] += 16
                op.dtarget = ent[1]
            elif op.needs_inc:
                self.tick[op.eng] += 1
                op.tick = self.tick[op.eng]
        final_tick = dict(self.tick)
        final_dsem = [(ent[0], ent[1]) for ent in self.dsem_all]

        per_eng = {e: [] for e in ENGINES}
        for op in ops:
            per_eng[op.eng].append(op)

        sem = self.sem
        known = self.known

        def emit_engine(ename, eng):
            kn = known[ename]

            def wait(s, val):
                key = id(s)
                if kn.get(key, 0) >= val:
                    return
                eng.wait_ge(s, val)
                kn[key] = val

            for op in per_eng[ename]:
                need = {}
                lhs_keys = set()
                lb = op.lhs_buf
                for d in op.deps:
                    if d.is_dma:
                        ent = d.dgroup
                        k = ("d", id(ent))
                        if k not in need or need[k][1] < d.dtarget:
                            need[k] = (ent[0], d.dtarget)
                    else:
                        if d.eng == ename and (ename == "pe" or not self.same_engine_sync):
                            continue
                        k = ("t", d.eng)
                        if k not in need or need[k][1] < d.tick:
                            need[k] = (sem[d.eng], d.tick)
                    if lb is not None and lb in d.writes:
                        lhs_keys.add(k)
                pend = [(k, s_, val) for k, (s_, val) in need.items() if kn.get(id(s_), 0) < val]
                attach = None
                if pend and not op.is_dma:
                    for i_, (k, s_, val) in enumerate(pend):
                        if ename != "pe" or k not in lhs_keys:
                            attach = (s_, val)
                            pend.pop(i_)
                            break
                for k, s_, val in pend:
                    wait(s_, val)
                ins = op.fn(eng)
                if attach is not None:
                    ins._wait_ge(attach[0], attach[1])
                    kn[id(attach[0])] = attach[1]
                if op.is_dma:
                    ins.then_inc(op.dgroup[0], 16)
                elif op.needs_inc:
                    ins.then_inc(sem[ename], 1)
                    if ename in kn and False:
                        pass
            for e2 in COMPUTE:
                if e2 != ename and final_tick[e2] > 0:
                    wait(sem[e2], final_tick[e2])
            for s, val in final_dsem:
                if val > 0:
                    wait(s, val)

        with nc.Block() as block:
            @block.sync
            def _(e):
                emit_engine("sp", e)

            @block.tensor
            def _(e):
                emit_engine("pe", e)

            @block.scalar
            def _(e):
                emit_engine("act", e)

            @block.vector
            def _(e):
                emit_engine("dve", e)

            @block.gpsimd
            def _(e):
                emit_engine("pool", e)

        self.n_emitted += len(ops)
        for op in ops:
            op.done = True
            op.fn = None
            op.deps = ()
        for b, cls in self._phase_dsems:
            self.dsem_pool[cls].append(b.dsem.pop(cls))
        self._phase_dsems = []
        self.ops = []

    def mm(self, out, lhsT, rhs, start=True, stop=True, **kw):
        op = self._add("pe", lambda e: e.matmul(out.ap, lhsT.ap, rhs.ap, start=start, stop=stop, **kw),
                       [lhsT, rhs] + ([] if start else [out]), [out])
        op.lhs_buf = lhsT.buf
        return op

    def transpose(self, out, in_, ident):
        op = self._add("pe", lambda e: e.transpose(out.ap, in_.ap, ident.ap), [in_, ident], [out])
        op.lhs_buf = in_.buf
        return op

    def act(self, out, in_, func, bias=None, scale=1.0, accum_out=None):
        rd = [in_]
        kw = {}
        if bias is not None:
            if isinstance(bias, View):
                rd.append(bias)
                kw["bias"] = bias.ap
            else:
                kw["bias"] = bias
        if isinstance(scale, View):
            rd.append(scale)
            kw["scale"] = scale.ap
        else:
            kw["scale"] = scale
        wr = [out]
        if accum_out is not None:
            wr.append(accum_out)
            kw["accum_out"] = accum_out.ap
        return self._add("act", lambda e: e.activation(out.ap, in_.ap, func, **kw), rd, wr)

    def tt(self, eng, out, in0, in1, op):
        return self._add(eng, lambda e: e.tensor_tensor(out.ap, in0.ap, in1.ap, op), [in0, in1], [out])

    def ts(self, eng, out, in0, s1, op0, s2=None, op1=None, accum_out=None):
        rd = [in0]
        a1 = s1
        a2 = s2
        if isinstance(s1, View):
            rd.append(s1)
            a1 = s1.ap
        if isinstance(s2, View):
            rd.append(s2)
            a2 = s2.ap
        wr = [out]
        kw = {}
        if op1 is not None:
            kw["op1"] = op1
        if accum_out is not None:
            wr.append(accum_out)
            kw["accum_out"] = accum_out.ap
        return self._add(eng, lambda e: e.tensor_scalar(out.ap, in0.ap, a1, a2, op0, **kw), rd, wr)

    def stt(self, eng, out, in0, scalar, in1, op0, op1):
        rd = [in0, in1]
        a = scalar
        if isinstance(scalar, View):
            rd.append(scalar)
            a = scalar.ap
        return self._add(eng, lambda e: e.scalar_tensor_tensor(out.ap, in0.ap, a, in1.ap, op0, op1), rd, [out])

    def copy(self, eng, out, in_):
        if eng == "act":
            return self._add("act", lambda e: e.copy(out.ap, in_.ap), [in_], [out])
        return self._add(eng, lambda e: e.tensor_copy(out.ap, in_.ap), [in_], [out])

    def memset(self, eng, out, val):
        return self._add(eng, lambda e: e.memset(out.ap, val), [], [out])

    def recip(self, out, in_):
        return self._add("dve", lambda e: e.reciprocal(out.ap, in_.ap), [in_], [out])

    def reduce(self, eng, out, in_, op, axis=AX.X):
        return self._add(eng, lambda e: e.tensor_reduce(out.ap, in_.ap, axis, op), [in_], [out])

    def cc_allgather(self, out_ap, in_ap, groups):
        if not hasattr(self, "cc_ent"):
            self.cc_ent = [self.nc.alloc_semaphore("ccsem"), 0, "cc"]
            self.dsem_all.append(self.cc_ent)
        return self._add("pool", lambda e: e.collective_compute("AllGather", op=ALU.bypass, replica_groups=groups,
                                                                ins=[in_ap], outs=[out_ap]),
                         [], [], is_dma=True, dgroup=self.cc_ent)

    def dma(self, q, out, in_, **kw):
        cls = "sw" if q == "pool" else "hw"
        if isinstance(out, View):
            grp = self._dsem(out.buf, cls)
            return self._add(q, lambda e: e.dma_start(out.ap, in_, **kw), [], [out], is_dma=True, dgroup=grp)
        grp = self._dsem(in_.buf, cls)
        return self._add(q, lambda e: e.dma_start(out, in_.ap, **kw), [in_], [], is_dma=True, dgroup=grp)

import math
from contextlib import ExitStack
from concourse.bass_utils import run_bass_kernel_spmd

D = 1024
CTX = 256
FF = 2816
EPS = 1e-6
GRID_W = 64
NEG = -30000.0
KINDS = (0, 1, 2, 0)


def lambda_init(i):
    return 0.8 - 0.6 * math.exp(-0.3 * i)


class Ctx:
    pass


def pipeline(n, st_qk, st_mid, st_pv):
    if n == 0:
        return
    st_qk(0)
    for i in range(n):
        if i + 1 < n:
            st_qk(i + 1)
        st_mid(i)
        st_pv(i)


def rsqrt(P, out, in_, mul):
    P.ts("dve", out, in_, mul, ALU.mult, EPS, ALU.add)
    P.act(out, out, AF.Sqrt)
    P.recip(out, out)


_UID = [0]


def sbt(nc, es, name, shape, dtype):
    _UID[0] += 1
    name = "%s_u%d" % (name, _UID[0])
    return Tile(es.enter_context(nc.sbuf_tensor(name, list(shape), dtype)), name)


def build(S, kinds=KINDS):
    NL = len(kinds)
    KINDS_ = kinds
    TOK = CTX + S
    NK = TOK
    ROWS = S // GRID_W
    nc = bass.Bass("TRN2", target_bir_lowering=False)
    g = Ctx()
    g.nc = nc
    g.S, g.TOK, g.NK, g.ROWS, g.NL = S, TOK, NK, ROWS, NL

    def din(name, shape, dtype=F32):
        return nc.dram_tensor(name, list(shape), dtype, kind="ExternalInput").ap()

    def dscr(name, shape, dtype):
        return nc.dram_tensor(name, list(shape), dtype, kind="Internal").ap()

    g.xc = din("xc", [TOK, D])
    g.cc = din("cc", [2, D])
    g.ident_d = din("ident", [128, 128])
    g.ropeA = din("ropeA", [2, 128, TOK])
    g.ropeB = din("ropeB", [2, 128, TOK])
    g.W = []
    for l in range(NL):
        p = "l%d_" % l
        w = {}
        w["w_mod"] = din(p + "w_mod", [D, 6 * D])
        w["b_mod"] = din(p + "b_mod", [6 * D])
        w["g_norm"] = din(p + "g_norm", [4, D])
        w["w_gu"] = din(p + "w_gu", [D, 2 * FF])
        w["w_down"] = din(p + "w_down", [FF, D])
        k = KINDS_[l]
        if k == 0:
            w["w_qkv"] = din(p + "a_w_qkv", [D, 3 * D])
            w["w_o"] = din(p + "a_w_o", [D, D])
            w["lam"] = din(p + "a_lam", [4, 64])
            w["g_sub"] = din(p + "a_g_sub", [128])
        elif k == 1:
            w["w_in"] = din(p + "b_w_in", [D, 544])
            w["g_q"] = din(p + "b_g_q", [256])
            w["g_kv"] = din(p + "b_g_kv", [256])
            w["w_uq"] = din(p + "b_w_uq", [256, 1536])
            w["w_ukv"] = din(p + "b_w_ukv", [256, 2048])
            w["w_o"] = din(p + "b_w_o", [D, D])
        else:
            w["w_qkv"] = din(p + "c_w_qkv", [D, 3 * D])
            w["rpb_tab"] = din(p + "c_rpb_tab", [16, 15, 64, 64])
            w["w_o"] = din(p + "c_w_o", [D, D])
        g.W.append(w)
    g.y = nc.dram_tensor("y", [S, D], F32, kind="ExternalOutput").ap()

    g.xres = dscr("xres", [TOK, D], F32)
    g.hT = dscr("hT", [D, TOK], BF16)
    g.qT = dscr("qT", [1536, TOK], BF16)
    g.kT = dscr("kT", [1056, NK], BF16)
    g.vv = dscr("vv", [NK, 1040], BF16)
    g.oT = dscr("oT", [D, TOK], BF16)

    g.blocks = [(0, CTX, True)] + [(CTX + i * 512, 512, False) for i in range(S // 512)]

    P = Prog(nc)
    g.P = P
    g.ident_f = Tile(nc.alloc_sbuf_tensor("ident_f", [128, 128], F32), "ident_f")
    g.ident_b = Tile(nc.alloc_sbuf_tensor("ident_b", [128, 128], BF16), "ident_b")
    g.ident8 = Tile(nc.alloc_sbuf_tensor("ident8", [128, 128], BF16), "ident8")
    g.ones_f = Tile(nc.alloc_sbuf_tensor("ones_f", [128, 128], F32), "ones_f")
    g.sel65 = Tile(nc.alloc_sbuf_tensor("sel65", [65, 64], F32), "sel65")
    g.MV = [Tile(nc.alloc_sbuf_tensor("MV%d" % l, [128, 48, 2], F32), "MV%d" % l) for l in range(NL)]
    g.DV = [Tile(nc.alloc_sbuf_tensor("DV%d" % l, [128, 4, 8, 2], F32), "DV%d" % l) for l in range(NL)]
    g.gn = [Tile(nc.alloc_sbuf_tensor("gn%d" % l, [128, 4, 8], F32), "gn%d" % l) for l in range(NL)]
    g.psall = nc.alloc_psum_tensor("psall", [128, 8, 512], F32)

    with nc.allow_non_contiguous_dma(reason="small strided parameter loads"):
        phase_init(g)
        phase_mod(g)
        phase_norm(g, 0, first=True)
        for l in range(NL):
            k = KINDS_[l]
            last = (l == NL - 1)
            if k == 0:
                phase_projA(g, l)
                phase_attnA(g, l)
            elif k == 1:
                phase_projB(g, l)
                phase_attnB(g, l)
            else:
                phase_projC(g, l)
                phase_attnC(g, l)
            phase_post(g, l)
            phase_ffn(g, l, final=(l == NL - 1), last=last)
    import sys as _sys
    print("[build] S=%d kinds=%s ops=%d instructions=%d" % (S, str(kinds), P.n_emitted, nc.n_instructions()), file=_sys.stderr)
    return nc


def bank(g, i, name, dtype=None, n=1):
    ap = g.psall[:, i:i + n, :].rearrange("p a b -> p (a b)")
    if dtype is not None:
        ap = ap.bitcast(dtype)
    return Tile(ap, name)


def phase_init(g):
    P, nc = g.P, g.nc
    P.dma("sp", g.ident_f[:], g.ident_d)
    P.copy("dve", g.ident_b[:], g.ident_f[:])
    P.ts("dve", g.ident8[:], g.ident_f[:], 8.0, ALU.mult)
    P.memset("dve", g.ones_f[:], 1.0)
    P.memset("dve", g.sel65[:], 0.0)
    P.memset("dve", g.sel65[64:65, :], 1.0)
    P.flush()


def phase_mod(g):
    P, nc = g.P, g.nc
    with ExitStack() as es:
        cs = sbt(nc, es, "cs", [128, 8, 2], F32)
        sT = sbt(nc, es, "sT", [128, 8, 2], F32)
        wm = [sbt(nc, es, "wm%d" % i, [128, 8, 1024], F32) for i in range(2)]
        bT = [sbt(nc, es, "bT%d" % i, [128, 48], F32) for i in range(2)]
        tmp = sbt(nc, es, "tmpm", [128, 8], F32)
        pst = [bank(g, i, "psm%d" % i) for i in range(4)]
        for r in range(2):
            P.dma("sp", cs[:, :, r], g.cc[r, :].rearrange("(k p) -> p k", p=128))
        P.act(sT[:], cs[:], AF.Silu)
        n = 0
        for l in range(g.NL):
            w = g.W[l]
            b = bT[l % 2]
            P.dma("sp", b[:], w["b_mod"].rearrange("(j p) -> p j", p=128))
            for r in range(4):
                P.dma("sp", g.gn[l][:, r, :], w["g_norm"][r, :].rearrange("(c p) -> p c", p=128))
            for nb in range(6):
                wt = wm[n % 2]
                n += 1
                for k in range(8):
                    P.dma("sp" if k % 2 == 0 else "act", wt[:, k, :],
                          w["w_mod"][k * 128:(k + 1) * 128, nb * 1024:(nb + 1) * 1024])
                for j in range(8):
                    ps = pst[j % 4]
                    for k in range(8):
                        P.mm(ps[:, 0:2], wt[:, k, j * 128:(j + 1) * 128], sT[:, k, :],
                             start=(k == 0), stop=(k == 7))
                    P.ts("dve", g.MV[l][:, nb * 8 + j, :], ps[:, 0:2], b[:, nb * 8 + j:nb * 8 + j + 1], ALU.add)
            MV, DV, gn = g.MV[l], g.DV[l], g.gn[l]
            for r in range(2):
                P.stt("dve", DV[:, 0, :, r], MV[:, 8:16, r], 1.0, gn[:, 0, :], ALU.add, ALU.mult)
                P.stt("dve", DV[:, 1, :, r], MV[:, 32:40, r], 1.0, gn[:, 2, :], ALU.add, ALU.mult)
                P.tt("dve", DV[:, 2, :, r], MV[:, 16:24, r], gn[:, 1, :], ALU.mult)
                P.tt("dve", DV[:, 3, :, r], MV[:, 40:48, r], gn[:, 3, :], ALU.mult)
        P.flush()


class NormT:
    def __init__(self, g, es, bank_ids):
        nc = g.nc
        self.g = g
        self.junk = [sbt(nc, es, "nt_junk%d" % i, [128, 1024], BF16) for i in range(2)]
        self.ss = [sbt(nc, es, "nt_ss%d" % i, [128, 1], F32) for i in range(2)]
        self.rstd = [sbt(nc, es, "nt_rstd%d" % i, [128, 1], F32) for i in range(2)]
        self.xn = [sbt(nc, es, "nt_xn%d" % i, [128, 1024], BF16) for i in range(2)]
        self.pT = [bank(g, b, "nt_pT%d" % b, BF16) for b in bank_ids]
        self.n = 0

    def __call__(self, xt, l, which, r, hblk, col):
        g, P = self.g, self.g.P
        i = self.n % 2
        pT = self.pT[self.n % len(self.pT)]
        self.n += 1
        junk, ss, rstd, xn = self.junk[i], self.ss[i], self.rstd[i], self.xn[i]
        P.memset("dve", ss[:], 0.0)
        P.act(junk[:], xt, AF.Square, accum_out=ss[:])
        rsqrt(P, rstd[:], ss[:], 1.0 / D)
        P.ts("pool", xn[:], xt, rstd[:, 0:1], ALU.mult)
        for c in range(8):
            P.transpose(pT[:, c * 128:(c + 1) * 128], xn[:, c * 128:(c + 1) * 128], g.ident_b[:])
        A = g.DV[l]
        MV = g.MV[l]
        boff = 0 if which == 0 else 24
        for c in range(8):
            P.act(hblk[:, c, col:col + 128], pT[:, c * 128:(c + 1) * 128], AF.Identity,
                  bias=MV[:, boff + c, r:r + 1], scale=A[:, which, c, r:r + 1])


def phase_norm(g, l, first=False):
    P, nc = g.P, g.nc
    with ExitStack() as es:
        nt = NormT(g, es, [0, 1])
        xb = [sbt(nc, es, "pn_x%d" % i, [128, 1024], F32) for i in range(3)]
        hb = [sbt(nc, es, "pn_h%d" % i, [128, 8, 512], BF16) for i in range(2)]
        n = 0
        for bi, (t0, w, isctx) in enumerate(g.blocks):
            h = hb[bi % 2]
            for i in range(w // 128):
                xt = xb[n % 3]
                n += 1
                P.dma("sp", xt[:], g.xc[t0 + i * 128:t0 + (i + 1) * 128, :])
                nt(xt[:], l, 0, 1 if isctx else 0, h, i * 128)
            P.dma("pool", g.hT[:, t0:t0 + w].rearrange("(c p) t -> p c t", p=128), h[:, :, 0:w])
        P.flush()


def load_w_cast(P, tile, src, nk, c0, c1, q="pool"):
    for k in range(nk):
        P.dma(q, tile[:, k, 0:c1 - c0], src[k * 128:(k + 1) * 128, c0:c1], max_dma_last_dim=4096)


def load_w_cast_swapped(P, tile, src, nk, c0, ngroups, half, q="pool"):
    gw = 2 * half
    for k in range(nk):
        dst = tile.h[:, k, 0:ngroups * gw].rearrange("p (g two i) -> p g two i", two=2, i=half)
        s = src[k * 128:(k + 1) * 128, c0:c0 + ngroups * gw].rearrange("p (g two i) -> p g two i", two=2, i=half)
        P.dma(q, tile.v(dst[:, :, 0, :]), s[:, :, 1, :], max_dma_last_dim=4096)
        P.dma(q, tile.v(dst[:, :, 1, :]), s[:, :, 0, :], max_dma_last_dim=4096)


def phase_projA(g, l):
    P, nc, w = g.P, g.nc, g.W[l]
    with ExitStack() as es:
        wq = sbt(nc, es, "wq", [128, 8, 1024], BF16)
        wqp = sbt(nc, es, "wqp", [128, 8, 1024], BF16)
        wk = sbt(nc, es, "wk", [128, 8, 1024], BF16)
        wkp = sbt(nc, es, "wkp", [128, 8, 1024], BF16)
        wv = sbt(nc, es, "wv", [128, 8, 1024], BF16)
        hb = [sbt(nc, es, "hb%d" % i, [128, 8, 512], BF16) for i in range(2)]
        rp = [sbt(nc, es, "rp%d" % i, [128, 2, 512], F32) for i in range(2)]
        t1 = [sbt(nc, es, "t1_%d" % i, [128, 512], F32) for i in range(2)]
        t2 = [sbt(nc, es, "t2_%d" % i, [128, 512], F32) for i in range(2)]
        ro = [sbt(nc, es, "ro%d" % i, [128, 512], BF16) for i in range(2)]
        vt = [sbt(nc, es, "vt%d" % i, [128, 1024], BF16) for i in range(2)]
        psA = [bank(g, i, "psA%d" % i) for i in (0, 1)]
        psB = [bank(g, i, "psB%d" % i) for i in (2, 3)]
        psV = [bank(g, i, "psV%d" % i) for i in (4, 5)]
        load_w_cast(P, wq, w["w_qkv"], 8, 0, 1024)
        load_w_cast_swapped(P, wqp, w["w_qkv"], 8, 0, 16, 32)
        load_w_cast(P, wk, w["w_qkv"], 8, 1024, 2048)
        load_w_cast_swapped(P, wkp, w["w_qkv"], 8, 1024, 16, 32)
        load_w_cast(P, wv, w["w_qkv"], 8, 2048, 3072)
        n = 0
        m = 0
        for bi, (t0, wd, isctx) in enumerate(g.blocks):
            h = hb[bi % 2]
            r = rp[bi % 2]
            P.dma("sp", h[:, :, 0:wd], g.hT[:, t0:t0 + wd].rearrange("(c p) t -> p c t", p=128))
            P.dma("sp", r[:, :, 0:wd], g.ropeA[:, :, t0:t0 + wd].rearrange("a p t -> p a t"))
            for (wt, wtp, dst) in ((wq, wqp, g.qT), (wk, wkp, g.kT)):
                for fc in range(8):
                    pa, pb = psA[n % 2], psB[n % 2]
                    a1, a2, o = t1[n % 2], t2[n % 2], ro[n % 2]
                    n += 1
                    for kc in range(8):
                        P.mm(pa[:, 0:wd], wt[:, kc, fc * 128:(fc + 1) * 128], h[:, kc, 0:wd], start=(kc == 0), stop=(kc == 7))
                    for kc in range(8):
                        P.mm(pb[:, 0:wd], wtp[:, kc, fc * 128:(fc + 1) * 128], h[:, kc, 0:wd], start=(kc == 0), stop=(kc == 7))
                    P.tt("dve", a1[:, 0:wd], pa[:, 0:wd], r[:, 0, 0:wd], ALU.mult)
                    P.tt("dve", a2[:, 0:wd], pb[:, 0:wd], r[:, 1, 0:wd], ALU.mult)
                    P.tt("pool", o[:, 0:wd], a1[:, 0:wd], a2[:, 0:wd], ALU.add)
                    P.dma("pool", dst[fc * 128:(fc + 1) * 128, t0:t0 + wd], o[:, 0:wd])
            for i in range(wd // 128):
                v = vt[m % 2]
                for half in range(2):
                    pv = psV[(2 * m + half) % 2]
                    for kc in range(8):
                        P.mm(pv[:, :], h[:, kc, i * 128:(i + 1) * 128], wv[:, kc, half * 512:(half + 1) * 512],
                             start=(kc == 0), stop=(kc == 7))
                    P.copy("act", v[:, half * 512:(half + 1) * 512], pv[:, :])
                m += 1
                P.dma("pool", g.vv[t0 + i * 128:t0 + (i + 1) * 128, 0:1024], v[:])
        P.flush()


def phase_attnA(g, l):
    P, nc, w = g.P, g.nc, g.W[l]
    NK = g.NK
    NKC = NK // 128
    scale = 64 ** -0.5
    li = lambda_init(l)
    with ExitStack() as es:
        kh = [sbt(nc, es, "kh%d" % i, [128, NK], BF16) for i in range(2)]
        vh = [sbt(nc, es, "vh%d" % i, [128, NKC, 128], BF16) for i in range(2)]
        qh = [sbt(nc, es, "qh%d" % i, [128, 512], BF16) for i in range(2)]
        ee = [sbt(nc, es, "ee_%d" % i, [128, 2, 512], BF16) for i in range(3)]
        accD = [sbt(nc, es, "accD_%d" % i, [128, 2, 512], F32) for i in range(2)]
        accP = [sbt(nc, es, "accP_%d" % i, [128, 2, 512], F32) for i in range(2)]
        rc1 = sbt(nc, es, "rc1", [128, 512], F32)
        rc2 = sbt(nc, es, "rc2", [128, 512], F32)
        u1 = sbt(nc, es, "u1", [128, 512], F32)
        u2 = sbt(nc, es, "u2", [128, 512], F32)
        oo = sbt(nc, es, "oo", [128, 512], F32)
        sq = sbt(nc, es, "sq", [128, 512], F32)
        rs = sbt(nc, es, "rs", [128, 512], F32)
        ob = [sbt(nc, es, "ob%d" % i, [128, 512], BF16) for i in range(2)]
        lt = sbt(nc, es, "lt", [1, 4, 64], F32)
        pr = sbt(nc, es, "pr", [1, 2, 64], F32)
        sm = sbt(nc, es, "sm", [1, 2], F32)
        ex = sbt(nc, es, "ex", [1, 2], F32)
        l1 = sbt(nc, es, "l1", [1, 2], F32)
        neglam = sbt(nc, es, "neglam", [128, 2], F32)
        gs = sbt(nc, es, "gs", [128, 1], F32)
        ps_s = [Tile(g.psall[:, 0:2, :], "ps_sA0"), Tile(g.psall[:, 2:4, :], "ps_sA1")]
        ps_o1 = bank(g, 4, "ps_o1")
        ps_o2 = bank(g, 5, "ps_o2")
        ps_r = [bank(g, 6, "ps_r0"), bank(g, 7, "ps_r1")]
        P.dma("sp", lt[:], w["lam"].rearrange("(o a) d -> o a d", o=1))
        P.dma("sp", gs[:], w["g_sub"].rearrange("(p o) -> p o", o=1))
        P.tt("dve", pr[:, 0, :], lt[:, 0, :], lt[:, 1, :], ALU.mult)
        P.tt("dve", pr[:, 1, :], lt[:, 2, :], lt[:, 3, :], ALU.mult)
        P.reduce("dve", sm[:], pr[:], ALU.add, AX.X)
        P.act(ex[:], sm[:], AF.Exp)
        P.tt("dve", l1[:, 0:1], ex[:, 0:1], ex[:, 1:2], ALU.subtract)
        P.ts("dve", l1[:, 0:1], l1[:, 0:1], li, ALU.add, -1.0, ALU.mult)
        P.copy("dve", l1[:, 1:2], l1[:, 0:1])
        P.mm(ps_r[0][:, 0:2], g.ones_f[0:1, :], l1[0:1, 0:2])
        P.copy("dve", neglam[:], ps_r[0][:, 0:2])
        P.ts("dve", gs[:], gs[:], 1.0 - li, ALU.mult)
        nblk = len(g.blocks)
        items = []
        for h in range(8):
            for bi, (t0, wd, isctx) in enumerate(g.blocks):
                chunks = [0, 1] if isctx else list(range(NKC))
                for ci, kc in enumerate(chunks):
                    items.append((h, bi, t0, wd, ci, kc, len(chunks)))

        def bufs(i):
            h, bi, t0, wd, ci, kc, n = items[i]
            qi = h * nblk + bi
            return kh[h % 2], vh[h % 2], qh[qi % 2], accD[qi % 2], accP[qi % 2], ps_s[i % 2], ee[i % 3], ob[qi % 2]

        def st_qk(i):
            h, bi, t0, wd, ci, kc, n = items[i]
            k_t, v_t, q, aD, aP, s, x, o_b = bufs(i)
            if ci == 0 and bi == 0:
                P.dma("sp", k_t[:, :], g.kT[h * 128:(h + 1) * 128, :])
                P.dma("sp", v_t[:], g.vv[:, h * 128:(h + 1) * 128].rearrange("(c p) d -> p c d", p=128))
            if ci == 0:
                P.dma("sp", q[:, 0:wd], g.qT[h * 128:(h + 1) * 128, t0:t0 + wd])
            P.mm(s[:, 0, 0:wd], k_t[0:64, kc * 128:(kc + 1) * 128], q[0:64, 0:wd])
            P.mm(s[:, 1, 0:wd], k_t[64:128, kc * 128:(kc + 1) * 128], q[64:128, 0:wd])

        def st_exp(i):
            h, bi, t0, wd, ci, kc, n = items[i]
            k_t, v_t, q, aD, aP, s, x, o_b = bufs(i)
            P.act(x[:, :, 0:wd], s[:, :, 0:wd], AF.Exp, scale=scale)

        def st_pv(i):
            h, bi, t0, wd, ci, kc, n = items[i]
            k_t, v_t, q, aD, aP, s, x, o_b = bufs(i)
            first, lastc = (ci == 0), (ci == n - 1)
            P.mm(ps_o1[:, 0:wd], v_t[:, kc, :], x[:, 0, 0:wd], start=first, stop=lastc)
            P.mm(ps_o2[:, 0:wd], v_t[:, kc, :], x[:, 1, 0:wd], start=first, stop=lastc)
            eng, a = ("dve", aD) if ci % 2 == 0 else ("dve", aP)
            if ci < 2:
                P.copy(eng, a[:, :, 0:wd], x[:, :, 0:wd])
            else:
                P.tt(eng, a[:, :, 0:wd], a[:, :, 0:wd], x[:, :, 0:wd], ALU.add)
            if not lastc:
                return
            P.tt("dve", aD[:, :, 0:wd], aD[:, :, 0:wd], aP[:, :, 0:wd], ALU.add)
            a1 = Tile(aD.h[:, 0, :], "a1v", buf=aD.buf)
            a2 = Tile(aD.h[:, 1, :], "a2v", buf=aD.buf)
            P.mm(ps_r[0][:, 0:wd], g.ones_f[:], a1[:, 0:wd])
            P.mm(ps_r[1][:, 0:wd], g.ones_f[:], a2[:, 0:wd])
            P.recip(rc1[:, 0:wd], ps_r[0][:, 0:wd])
            P.recip(rc2[:, 0:wd], ps_r[1][:, 0:wd])
            P.tt("dve", u1[:, 0:wd], ps_o1[:, 0:wd], rc1[:, 0:wd], ALU.mult)
            P.tt("dve", u2[:, 0:wd], ps_o2[:, 0:wd], rc2[:, 0:wd], ALU.mult)
            P.stt("dve", oo[:, 0:wd], u2[:, 0:wd], neglam[:, 0:1], u1[:, 0:wd], ALU.mult, ALU.add)
            P.tt("pool", sq[:, 0:wd], oo[:, 0:wd], oo[:, 0:wd], ALU.mult)
            P.mm(ps_r[0][:, 0:wd], g.ones_f[:], sq[:, 0:wd])
            rsqrt(P, rs[:, 0:wd], ps_r[0][:, 0:wd], 1.0 / 128)
            P.stt("dve", o_b[:, 0:wd], oo[:, 0:wd], gs[:, 0:1], rs[:, 0:wd], ALU.mult, ALU.mult)
            P.dma("pool", g.oT[h * 128:(h + 1) * 128, t0:t0 + wd], o_b[:, 0:wd])

        pipeline(len(items), st_qk, st_exp, st_pv)
        P.flush()


def phase_post(g, l):
    P, nc, w = g.P, g.nc, g.W[l]
    src = g.xc if l == 0 else g.xres
    with ExitStack() as es:
        wo = sbt(nc, es, "wo", [128, 8, 1024], BF16)
        gbc = [sbt(nc, es, "gbc%d" % r, [128, 1024], F32) for r in range(2)]
        dg = sbt(nc, es, "dg", [128, 128], F32)
        ob = [sbt(nc, es, "pob%d" % i, [128, 8, 512], BF16) for i in range(2)]
        hb = [sbt(nc, es, "phb%d" % i, [128, 8, 512], BF16) for i in range(2)]
        xb = [sbt(nc, es, "pxb%d" % i, [128, 1024], F32) for i in range(2)]
        xo = [sbt(nc, es, "pxo%d" % i, [128, 1024], F32) for i in range(2)]
        tm = [sbt(nc, es, "ptm%d" % i, [128, 1024], F32) for i in range(2)]
        junk = sbt(nc, es, "pjunk", [128, 1024], BF16)
        ss = [sbt(nc, es, "pss%d" % i, [128, 1], F32) for i in range(2)]
        rstd = [sbt(nc, es, "prstd%d" % i, [128, 1], F32) for i in range(2)]
        nt = NormT(g, es, [4, 5])
        psy = [bank(g, 0, "psy0", n=2), bank(g, 2, "psy1", n=2)]
        psg = bank(g, 6, "psg")
        load_w_cast(P, wo, w["w_o"], 8, 0, 1024)
        make_gbc(g, l, 2, gbc, dg, psg)
        n = 0
        for bi, (t0, wd, isctx) in enumerate(g.blocks):
            o = ob[bi % 2]
            h = hb[bi % 2]
            r = 1 if isctx else 0
            P.dma("sp", o[:, :, 0:wd], g.oT[:, t0:t0 + wd].rearrange("(c p) t -> p c t", p=128))
            for i in range(wd // 128):
                py = psy[n % 2]
                xt, xn_, t_, s_, r_ = xb[n % 2], xo[n % 2], tm[n % 2], ss[n % 2], rstd[n % 2]
                n += 1
                rows = slice(t0 + i * 128, t0 + (i + 1) * 128)
                P.dma("sp", xt[:], src[rows, :])
                for half in range(2):
                    for kc in range(8):
                        P.mm(py[:, half * 512:(half + 1) * 512], o[:, kc, i * 128:(i + 1) * 128],
                             wo[:, kc, half * 512:(half + 1) * 512], start=(kc == 0), stop=(kc == 7))
                residual_update(P, py, xt, xn_, t_, s_, r_, junk, gbc[r])
                P.dma("pool", g.xres[rows, :], xn_[:])
                nt(xn_[:], l, 1, r, h, i * 128)
            P.dma("pool", g.hT[:, t0:t0 + wd].rearrange("(c p) t -> p c t", p=128), h[:, :, 0:wd])
        P.flush()


def residual_update(P, py, xt, xnew, tmp, ss, rstd, junk, gbc):
    P.memset("dve", ss[:], 0.0)
    P.act(junk[:], py[:, :], AF.Square, accum_out=ss[:])
    rsqrt(P, rstd[:], ss[:], 1.0 / D)
    P.stt("dve", tmp[:], py[:, :], rstd[:, 0:1], gbc[:], ALU.mult, ALU.mult)
    P.tt("pool", xnew[:], tmp[:], xt[:], ALU.add)


def make_gbc(g, l, which, gbc, dg, psg):
    P = g.P
    for r in range(2):
        for c in range(8):
            P.ts("dve", dg[:], g.ident_f[:], g.DV[l][:, which, c, r:r + 1], ALU.mult)
            P.mm(psg[:, 0:128], g.ones_f[:], dg[:])
            P.copy("dve", gbc[r][:, c * 128:(c + 1) * 128], psg[:, 0:128])


def phase_ffn(g, l, final, last):
    P, nc, w = g.P, g.nc, g.W[l]
    TB = 256
    NJ = FF // 128
    with ExitStack() as es:
        wgu = sbt(nc, es, "wgu", [128, 8, 2 * FF], BF16)
        wd_ = sbt(nc, es, "wdn", [128, NJ, 1024], BF16)
        gbc = [sbt(nc, es, "fgbc%d" % r, [128, 1024], F32) for r in range(2)]
        dg = sbt(nc, es, "fdg", [128, 128], F32)
        hb = [sbt(nc, es, "fhb%d" % i, [128, 8, TB], BF16) for i in range(2)]
        ho = [sbt(nc, es, "fho%d" % i, [128, 8, TB], BF16) for i in range(2)]
        at = sbt(nc, es, "fat", [128, NJ, TB], BF16)
        sg = [sbt(nc, es, "fsg%d" % i, [128, TB], F32) for i in range(2)]
        xb = [sbt(nc, es, "fxb%d" % i, [128, 1024], F32) for i in range(2)]
        xo = [sbt(nc, es, "fxo%d" % i, [128, 1024], F32) for i in range(2)]
        tm = sbt(nc, es, "ftm", [128, 1024], F32)
        junk = sbt(nc, es, "fjunk", [128, 1024], BF16)
        ss = [sbt(nc, es, "fss%d" % i, [128, 1], F32) for i in range(2)]
        rstd = [sbt(nc, es, "frstd%d" % i, [128, 1], F32) for i in range(2)]
        nt = NormT(g, es, [6]) if not final else None
        psgu = [bank(g, i, "psgu%d" % i) for i in (0, 1, 2, 3)]
        psf = bank(g, 4, "psf", n=2)
        psg = bank(g, 7, "fpsg")
        for k in range(8):
            for c in range(0, 2 * FF, 1408):
                P.dma("pool", wgu[:, k, c:c + 1408], w["w_gu"][k * 128:(k + 1) * 128, c:c + 1408], max_dma_last_dim=4096)
        for j in range(NJ):
            P.dma("pool", wd_[:, j, :], w["w_down"][j * 128:(j + 1) * 128, :], max_dma_last_dim=4096)
        make_gbc(g, l, 3, gbc, dg, psg)
        n = 0
        ng = 0
        nblk = g.TOK // TB
        for bi in range(nblk):
            t0 = bi * TB
            isctx = t0 < CTX
            r = 1 if isctx else 0
            if last and isctx:
                continue
            h = hb[bi % 2]
            hn = ho[bi % 2]
            P.dma("sp", h[:], g.hT[:, t0:t0 + TB].rearrange("(c p) t -> p c t", p=128))
            for j in range(NJ):
                pg, pu = psgu[(2 * ng) % 4], psgu[(2 * ng + 1) % 4]
                s_ = sg[ng % 2]
                ng += 1
                for kc in range(8):
                    P.mm(pg[:, 0:TB], wgu[:, kc, j * 128:(j + 1) * 128], h[:, kc, :], start=(kc == 0), stop=(kc == 7))
                for kc in range(8):
                    P.mm(pu[:, 0:TB], wgu[:, kc, FF + j * 128:FF + (j + 1) * 128], h[:, kc, :], start=(kc == 0), stop=(kc == 7))
                P.act(s_[:], pg[:, 0:TB], AF.Silu)
                P.tt("dve", at[:, j, :], s_[:], pu[:, 0:TB], ALU.mult)
            for i in range(TB // 128):
                xt, xn_, s2, r2 = xb[n % 2], xo[n % 2], ss[n % 2], rstd[n % 2]
                n += 1
                rows = slice(t0 + i * 128, t0 + (i + 1) * 128)
                P.dma("sp", xt[:], g.xres[rows, :])
                for half in range(2):
                    for j in range(NJ):
                        P.mm(psf[:, half * 512:(half + 1) * 512], at[:, j, i * 128:(i + 1) * 128],
                             wd_[:, j, half * 512:(half + 1) * 512], start=(j == 0), stop=(j == NJ - 1))
                residual_update(P, psf, xt, xn_, tm, s2, r2, junk, gbc[r])
                if final:
                    if not isctx:
                        P.dma("pool", g.y[t0 - CTX + i * 128:t0 - CTX + (i + 1) * 128, :], xn_[:])
                else:
                    P.dma("pool", g.xres[rows, :], xn_[:])
                    nt(xn_[:], l + 1, 0, r, hn, i * 128)
            if not final:
                P.dma("pool", g.hT[:, t0:t0 + TB].rearrange("(c p) t -> p c t", p=128), hn[:])
        P.flush()


def rot_store(P, pa, pb, rt, rows, wd, a1, a2, o, dst_ap):
    P.tt("dve", a1[0:rows, 0:wd], pa[0:rows, 0:wd], rt[0:rows, 0, 0:wd], ALU.mult)
    P.tt("dve", a2[0:rows, 0:wd], pb[0:rows, 0:wd], rt[0:rows, 1, 0:wd], ALU.mult)
    P.tt("pool", o[0:rows, 0:wd], a1[0:rows, 0:wd], a2[0:rows, 0:wd], ALU.add)
    P.dma("pool", dst_ap, o[0:rows, 0:wd])


def phase_projB(g, l):
    P, nc, w = g.P, g.nc, g.W[l]
    with ExitStack() as es:
        win = sbt(nc, es, "win", [128, 8, 544], BF16)
        winp = sbt(nc, es, "winp", [128, 8, 32], BF16)
        wuq = sbt(nc, es, "wuq", [128, 2, 1536], BF16)
        wuqp = sbt(nc, es, "wuqp", [128, 2, 1536], BF16)
        wkk = sbt(nc, es, "wkk", [128, 2, 1024], BF16)
        wkv = sbt(nc, es, "wkv", [128, 2, 1024], BF16)
        gq = sbt(nc, es, "gq", [128, 4], F32)
        hb = [sbt(nc, es, "bhb%d" % i, [128, 8, 512], BF16) for i in range(2)]
        rB = [sbt(nc, es, "rB%d" % i, [128, 2, 512], F32) for i in range(2)]
        rK = [sbt(nc, es, "rK%d" % i, [32, 2, 512], F32) for i in range(2)]
        sqt = [sbt(nc, es, "sqt%d" % i, [128, 512], F32) for i in range(2)]
        rsd = sbt(nc, es, "rsd", [128, 512], F32)
        cn = [sbt(nc, es, "cn%d" % i, [128, 2, 512], BF16) for i in range(2)]
        a1 = [sbt(nc, es, "ba1_%d" % i, [128, 512], F32) for i in range(2)]
        a2 = [sbt(nc, es, "ba2_%d" % i, [128, 512], F32) for i in range(2)]
        ro = [sbt(nc, es, "bro%d" % i, [128, 512], BF16) for i in range(2)]
        va = [sbt(nc, es, "bva%d" % i, [128, 16, 65], BF16) for i in range(2)]
        pz = [bank(g, 0, "pz0"), bank(g, 1, "pz1")]
        pss = bank(g, 2, "pss")
        pzr, pzrp = bank(g, 3, "pzr"), bank(g, 4, "pzrp")
        pq, pqp = bank(g, 5, "pq"), bank(g, 6, "pqp")
        pk = bank(g, 7, "pk")
        load_w_cast(P, win, w["w_in"], 8, 0, 544)
        load_w_cast_swapped(P, winp, w["w_in"], 8, 512, 1, 16)
        load_w_cast(P, wuq, w["w_uq"], 2, 0, 1536)
        for k in range(2):
            dst = wuqp.h[:, k, :].rearrange("p (h c) -> p h c", c=96)
            s = w["w_uq"][k * 128:(k + 1) * 128, :].rearrange("p (h c) -> p h c", c=96)
            P.dma("pool", wuqp.v(dst[:, :, 0:64]), s[:, :, 0:64], max_dma_last_dim=4096)
            P.dma("pool", wuqp.v(dst[:, :, 64:80]), s[:, :, 80:96], max_dma_last_dim=4096)
            P.dma("pool", wuqp.v(dst[:, :, 80:96]), s[:, :, 64:80], max_dma_last_dim=4096)
            s2 = w["w_ukv"][k * 128:(k + 1) * 128, :].rearrange("p (h c) -> p h c", c=128)
            P.dma("pool", wkk.v(wkk.h[:, k, :].rearrange("p (h c) -> p h c", c=64)), s2[:, :, 0:64], max_dma_last_dim=4096)
            P.dma("pool", wkv.v(wkv.h[:, k, :].rearrange("p (h c) -> p h c", c=64)), s2[:, :, 64:128], max_dma_last_dim=4096)
        P.dma("sp", gq[:, 0:2], w["g_q"].rearrange("(c p) -> p c", p=128))
        P.dma("sp", gq[:, 2:4], w["g_kv"].rearrange("(c p) -> p c", p=128))
        for v in va:
            P.memset("dve", v[:], 1.0)
        n = 0
        m = 0
        for bi, (t0, wd, isctx) in enumerate(g.blocks):
            h = hb[bi % 2]
            rb, rk = rB[bi % 2], rK[bi % 2]
            P.dma("sp", h[:, :, 0:wd], g.hT[:, t0:t0 + wd].rearrange("(c p) t -> p c t", p=128))
            P.dma("sp", rb[:, :, 0:wd], g.ropeB[:, :, t0:t0 + wd].rearrange("a p t -> p a t"))
            P.dma("sp", rk[:, :, 0:wd], g.ropeB[:, 64:96, t0:t0 + wd].rearrange("a p t -> p a t"))
            for which in range(2):
                c_ = cn[which]
                for c in range(2):
                    col = which * 256 + c * 128
                    for kc in range(8):
                        P.mm(pz[c][:, 0:wd], win[:, kc, col:col + 128], h[:, kc, 0:wd], start=(kc == 0), stop=(kc == 7))
                    P.act(sqt[c][:, 0:wd], pz[c][:, 0:wd], AF.Square)
                P.mm(pss[:, 0:wd], g.ones_f[:], sqt[0][:, 0:wd], start=True, stop=False)
                P.mm(pss[:, 0:wd], g.ones_f[:], sqt[1][:, 0:wd], start=False, stop=True)
                rsqrt(P, rsd[:, 0:wd], pss[:, 0:wd], 1.0 / 256)
                for c in range(2):
                    P.stt("dve", c_[:, c, 0:wd], pz[c][:, 0:wd], gq[:, which * 2 + c:which * 2 + c + 1], rsd[:, 0:wd],
                          ALU.mult, ALU.mult)
            for kc in range(8):
                P.mm(pzr[0:32, 0:wd], win[:, kc, 512:544], h[:, kc, 0:wd], start=(kc == 0), stop=(kc == 7))
            for kc in range(8):
                P.mm(pzrp[0:32, 0:wd], winp[:, kc, 0:32], h[:, kc, 0:wd], start=(kc == 0), stop=(kc == 7))
            rot_store(P, pzr, pzrp, rk, 32, wd, a1[n % 2], a2[n % 2], ro[n % 2], g.kT[1024:1056, t0:t0 + wd])
            n += 1
            cq, ckv = cn[0], cn[1]
            for hh in range(16):
                for kc in range(2):
                    P.mm(pq[0:96, 0:wd], wuq[:, kc, hh * 96:(hh + 1) * 96], cq[:, kc, 0:wd], start=(kc == 0), stop=(kc == 1))
                for kc in range(2):
                    P.mm(pqp[0:96, 0:wd], wuqp[:, kc, hh * 96:(hh + 1) * 96], cq[:, kc, 0:wd], start=(kc == 0), stop=(kc == 1))
                rot_store(P, pq, pqp, rb, 96, wd, a1[n % 2], a2[n % 2], ro[n % 2], g.qT[hh * 96:(hh + 1) * 96, t0:t0 + wd])
                n += 1
            for fc in range(8):
                for kc in range(2):
                    P.mm(pk[:, 0:wd], wkk[:, kc, fc * 128:(fc + 1) * 128], ckv[:, kc, 0:wd], start=(kc == 0), stop=(kc == 1))
                o = ro[n % 2]
                n += 1
                P.copy("act", o[:, 0:wd], pk[:, 0:wd])
                P.dma("pool", g.kT[fc * 128:(fc + 1) * 128, t0:t0 + wd], o[:, 0:wd])
            for i in range(wd // 128):
                v = va[m % 2]
                m += 1
                for half in range(2):
                    for kc in range(2):
                        P.mm(pk[:, :], ckv[:, kc, i * 128:(i + 1) * 128], wkv[:, kc, half * 512:(half + 1) * 512],
                             start=(kc == 0), stop=(kc == 1))
                    P.copy("act", v[:, half * 8:(half + 1) * 8, 0:64], pk.v(pk.h[:, :].rearrange("p (h d) -> p h d", d=64)))
                P.dma("pool", g.vv[t0 + i * 128:t0 + (i + 1) * 128, 0:1040], v.v(v.h[:, :, :].rearrange("p h d -> p (h d)")))
        P.flush()


def attn_single(g, l, nheads, qrows, krow_loader, scale):
    P, nc = g.P, g.nc
    NK = g.NK
    NKC = NK // 128
    with ExitStack() as es:
        kh = [sbt(nc, es, "bkh%d" % i, [qrows, NK], BF16) for i in range(2)]
        vh = [sbt(nc, es, "bvh%d" % i, [128, NKC, 65], BF16) for i in range(2)]
        qh = [sbt(nc, es, "bqh%d" % i, [qrows, 512], BF16) for i in range(2)]
        ee = [sbt(nc, es, "bee%d" % i, [128, 512], BF16) for i in range(3)]
        osb = [sbt(nc, es, "bosb%d" % i, [65, 512], F32) for i in range(2)]
        rc = sbt(nc, es, "brc", [64, 512], F32)
        ob = [sbt(nc, es, "bob%d" % i, [64, 512], BF16) for i in range(2)]
        ps_s = [bank(g, i, "bps_s%d" % i) for i in (0, 1)]
        ps_o = [bank(g, i, "bps_o%d" % i) for i in (2, 3)]
        ps_b = bank(g, 4, "bps_b")
        nblk = len(g.blocks)
        items = []
        for h in range(nheads):
            for bi, (t0, wd, isctx) in enumerate(g.blocks):
                chunks = [0, 1] if isctx else list(range(NKC))
                for ci, kc in enumerate(chunks):
                    items.append((h, bi, t0, wd, ci, kc, len(chunks)))

        def bufs(i):
            h, bi, t0, wd, ci, kc, n = items[i]
            qi = h * nblk + bi
            return kh[h % 2], vh[h % 2], qh[qi % 2], ps_o[qi % 2], osb[qi % 2], ob[qi % 2], ps_s[i % 2], ee[i % 3]

        def st_qk(i):
            h, bi, t0, wd, ci, kc, n = items[i]
            k_t, v_t, q, po, os_, o_b, s, x = bufs(i)
            if ci == 0 and bi == 0:
                krow_loader(P, k_t, h)
                P.dma("sp", v_t[:], g.vv[:, h * 65:(h + 1) * 65].rearrange("(c p) d -> p c d", p=128))
            if ci == 0:
                P.dma("sp", q[:, 0:wd], g.qT[h * qrows:(h + 1) * qrows, t0:t0 + wd])
            P.mm(s[:, 0:wd], k_t[:, kc * 128:(kc + 1) * 128], q[:, 0:wd])

        def st_exp(i):
            h, bi, t0, wd, ci, kc, n = items[i]
            k_t, v_t, q, po, os_, o_b, s, x = bufs(i)
            P.act(x[:, 0:wd], s[:, 0:wd], AF.Exp, scale=scale)

        def st_pv(i):
            h, bi, t0, wd, ci, kc, n = items[i]
            k_t, v_t, q, po, os_, o_b, s, x = bufs(i)
            P.mm(po[0:65, 0:wd], v_t[:, kc, :], x[:, 0:wd], start=(ci == 0), stop=(ci == n - 1))
            if ci != n - 1:
                return
            P.copy("dve", os_[:, 0:wd], po[0:65, 0:wd])
            P.mm(ps_b[0:64, 0:wd], g.sel65[:, :], os_[:, 0:wd])
            P.recip(rc[:, 0:wd], ps_b[0:64, 0:wd])
            P.tt("dve", o_b[:, 0:wd], os_[0:64, 0:wd], rc[:, 0:wd], ALU.mult)
            P.dma("pool", g.oT[h * 64:(h + 1) * 64, t0:t0 + wd], o_b[:, 0:wd])

        pipeline(len(items), st_qk, st_exp, st_pv)
        P.flush()


def phase_attnB(g, l):
    def loader(P, k_t, h):
        P.dma("sp", k_t[0:64, :], g.kT[h * 64:(h + 1) * 64, :])
        P.dma("sp", k_t[64:96, :], g.kT[1024:1056, :])
    attn_single(g, l, 16, 96, loader, 96 ** -0.5)


def phase_projC(g, l):
    P, nc, w = g.P, g.nc, g.W[l]
    with ExitStack() as es:
        wq = sbt(nc, es, "cwq", [128, 8, 1024], BF16)
        wk = sbt(nc, es, "cwk", [128, 8, 1024], BF16)
        wv = sbt(nc, es, "cwv", [128, 8, 1024], BF16)
        hb = [sbt(nc, es, "chb%d" % i, [128, 8, 512], BF16) for i in range(2)]
        ro = [sbt(nc, es, "cro%d" % i, [128, 512], BF16) for i in range(2)]
        va = [sbt(nc, es, "cva%d" % i, [128, 16, 65], BF16) for i in range(2)]
        psA = [bank(g, i, "cpsA%d" % i) for i in (0, 1)]
        psV = [bank(g, i, "cpsV%d" % i) for i in (2, 3)]
        load_w_cast(P, wq, w["w_qkv"], 8, 0, 1024)
        load_w_cast(P, wk, w["w_qkv"], 8, 1024, 2048)
        load_w_cast(P, wv, w["w_qkv"], 8, 2048, 3072)
        for v in va:
            P.memset("dve", v[:], 1.0)
        n = 0
        m = 0
        for bi, (t0, wd, isctx) in enumerate(g.blocks):
            h = hb[bi % 2]
            P.dma("sp", h[:, :, 0:wd], g.hT[:, t0:t0 + wd].rearrange("(c p) t -> p c t", p=128))
            for (wt, dst) in ((wq, g.qT), (wk, g.kT)):
                for fc in range(8):
                    pa = psA[n % 2]
                    o = ro[n % 2]
                    n += 1
                    for kc in range(8):
                        P.mm(pa[:, 0:wd], wt[:, kc, fc * 128:(fc + 1) * 128], h[:, kc, 0:wd], start=(kc == 0), stop=(kc == 7))
                    if n % 2:
                        P.copy("act", o[:, 0:wd], pa[:, 0:wd])
                    else:
                        P.copy("dve", o[:, 0:wd], pa[:, 0:wd])
                    P.dma("pool", dst[fc * 128:(fc + 1) * 128, t0:t0 + wd], o[:, 0:wd])
            for i in range(wd // 128):
                v = va[m % 2]
                for half in range(2):
                    pv = psV[(2 * m + half) % 2]
                    for kc in range(8):
                        P.mm(pv[:, :], h[:, kc, i * 128:(i + 1) * 128], wv[:, kc, half * 512:(half + 1) * 512],
                             start=(kc == 0), stop=(kc == 7))
                    P.copy("act", v[:, half * 8:(half + 1) * 8, 0:64], pv.v(pv.h[:, :].rearrange("p (h d) -> p h d", d=64)))
                m += 1
                P.dma("pool", g.vv[t0 + i * 128:t0 + (i + 1) * 128, 0:1040], v.v(v.h[:, :, :].rearrange("p h d -> p (h d)")))
        P.flush()


def phase_attnC(g, l):
    P, nc, w = g.P, g.nc, g.W[l]
    ROWS = g.ROWS
    scale = 64 ** -0.5
    with ExitStack() as es:
        tb = sbt(nc, es, "tb", [128, 16, 14, 64], BF16)
        kc_t = sbt(nc, es, "kc_t", [64, 16, 256], BF16)
        vc_t = sbt(nc, es, "vc_t", [128, 2, 1040], BF16)
        kb = [sbt(nc, es, "kb%d" % i, [64, 16, 512], BF16) for i in range(2)]
        vb = [sbt(nc, es, "vb%d" % i, [128, 4, 1040], BF16) for i in range(2)]
        qr = [sbt(nc, es, "qr%d" % i, [64, 16, 512], BF16) for i in range(2)]
        ee = [sbt(nc, es, "cee%d" % i, [128, 512], BF16) for i in range(3)]
        osb = [sbt(nc, es, "cosb%d" % i, [65, 512], F32) for i in range(2)]
        rc = sbt(nc, es, "crc", [64, 512], F32)
        obuf = [sbt(nc, es, "cobuf%d" % i, [64, 16, 512], BF16) for i in range(2)]
        ps_s = [bank(g, i, "cps_s%d" % i) for i in (0, 1)]
        ps_o = [bank(g, i, "cps_o%d" % i) for i in (2, 3)]
        ps_b = bank(g, 4, "cps_b")
        tab = w["rpb_tab"]
        for h in range(16):
            P.dma("pool", tb[0:64, h, :, :], tab[h, 0:14, :, :].rearrange("e k q -> k e q"), max_dma_last_dim=4096)
            P.dma("pool", tb[64:128, h, :, :], tab[h, 1:15, :, :].rearrange("e k q -> k e q"), max_dma_last_dim=4096)
        P.dma("sp", kc_t[:], g.kT[0:1024, 0:CTX].rearrange("(h d) t -> d h t", d=64))
        P.dma("sp", vc_t[:], g.vv[0:CTX, :].rearrange("(c p) d -> p c d", p=128))
        prow = [("c", i) for i in range(CTX // 64)] + [("l", r) for r in range(ROWS)]
        info = []
        for pi, (kind, r) in enumerate(prow):
            if kind == "c":
                gi, tq0, gidx, gw, nch, rs_ = r, 0, 0, CTX, 2, None
                lastg = (r == CTX // 64 - 1)
            else:
                gi, tq0, gidx, gw, nch = r % 8, CTX + (r // 8) * 512, 1 + r // 8, 512, 6
                rs_ = min(max(r - 4, 0), ROWS - 8)
                lastg = (gi == 7 or r == ROWS - 1)
            info.append((kind, r, gi, tq0, gidx, gw, nch, rs_, lastg))
        items = []
        for pi in range(len(prow)):
            for hg in range(2):
                for j in range(info[pi][6]):
                    items.append((pi, hg, j))

        def bufs(i):
            pi, hg, j = items[i]
            kind, r, gi, tq0, gidx, gw, nch, rs_, lastg = info[pi]
            gg = pi * 2 + hg
            return (qr[gidx % 2], obuf[gidx % 2], kb[pi % 2], vb[pi % 2], ps_s[i % 2], ee[i % 3], ps_o[gg % 2], osb[gg % 2])

        def st_qk(i):
            pi, hg, j = items[i]
            kind, r, gi, tq0, gidx, gw, nch, rs_, lastg = info[pi]
            q_t, o_t, k_b, v_b, s, x, po, os_ = bufs(i)
            if hg == 0 and j == 0:
                if gi == 0:
                    P.dma("sp", q_t[:, :, 0:gw], g.qT[0:1024, tq0:tq0 + gw].rearrange("(h d) t -> d h t", d=64))
                if kind == "l":
                    tk0 = CTX + rs_ * 64
                    P.dma("sp", k_b[:], g.kT[0:1024, tk0:tk0 + 512].rearrange("(h d) t -> d h t", d=64))
                    P.dma("sp", v_b[:], g.vv[tk0:tk0 + 512, :].rearrange("(c p) d -> p c d", p=128))
            for hh in range(8):
                h = hg * 8 + hh
                if j < 2:
                    kl = kc_t[:, h, j * 128:(j + 1) * 128]
                else:
                    kl = k_b[:, h, (j - 2) * 128:(j - 1) * 128]
                P.mm(s[:, hh * 64:(hh + 1) * 64], kl, q_t[:, h, gi * 64:(gi + 1) * 64],
                     start=(hh == 0), stop=(hh == 7 and j < 2))
            if j >= 2:
                e = rs_ + 2 * (j - 2) - r + 7
                assert 0 <= e <= 13
                P.mm(s[:, :], g.ident8[:], tb[:, hg * 8:(hg + 1) * 8, e, :], start=False, stop=True)

        def st_exp(i):
            q_t, o_t, k_b, v_b, s, x, po, os_ = bufs(i)
            P.act(x[:], s[:], AF.Exp, scale=scale)

        def st_pv(i):
            pi, hg, j = items[i]
            kind, r, gi, tq0, gidx, gw, nch, rs_, lastg = info[pi]
            q_t, o_t, k_b, v_b, s, x, po, os_ = bufs(i)
            for hh in range(8):
                h = hg * 8 + hh
                if j < 2:
                    vl = vc_t[:, j, h * 65:(h + 1) * 65]
                else:
                    vl = v_b[:, j - 2, h * 65:(h + 1) * 65]
                P.mm(po[0:65, hh * 64:(hh + 1) * 64], vl, x[:, hh * 64:(hh + 1) * 64],
                     start=(j == 0 and hh == 0), stop=(j == nch - 1 and hh == 7))
            if j != nch - 1:
                return
            P.copy("dve", os_[:], po[0:65, :])
            P.mm(ps_b[0:64, :], g.sel65[:, :], os_[:, :])
            P.recip(rc[:], ps_b[0:64, :])
            P.tt("dve", o_t[:, hg * 8:(hg + 1) * 8, gi * 64:(gi + 1) * 64],
                 os_.v(os_.h[0:64, :].rearrange("p (h q) -> p h q", q=64)),
                 rc.v(rc.h[:, :].rearrange("p (h q) -> p h q", q=64)), ALU.mult)
            if hg == 1 and lastg:
                P.dma("pool", g.oT[0:1024, tq0:tq0 + gw].rearrange("(h d) t -> d h t", d=64), o_t[:, :, 0:gw])

        pipeline(len(items), st_qk, st_exp, st_pv)
        P.flush()


def _rope_np(S, rot_dim):
    n_freq = rot_dim // 4
    inv = (np.float32(10000.0) ** (-(np.arange(n_freq, dtype=np.float32) / np.float32(n_freq)))).astype(np.float32)
    t = np.arange(S, dtype=np.int64)
    row = (t // GRID_W).astype(np.float32)
    col = (t % GRID_W).astype(np.float32)
    ang = np.concatenate([row[:, None] * inv, col[:, None] * inv], axis=-1).astype(np.float32)
    return np.cos(ang).astype(np.float32), np.sin(ang).astype(np.float32)


def _tables(S):
    TOK = CTX + S
    ca, sa = _rope_np(S, 64)
    cb, sb_ = _rope_np(S, 32)
    ropeA = np.zeros((2, 128, TOK), np.float32)
    ropeA[0] = 1.0
    ropeB = np.zeros((2, 128, TOK), np.float32)
    ropeB[0] = 1.0
    for p in range(128):
        d = p % 64
        i = d % 32
        ropeA[0, p, CTX:] = ca[:, i]
        ropeA[1, p, CTX:] = -sa[:, i] if d < 32 else sa[:, i]
    for p in range(64, 96):
        d = p - 64
        i = d % 16
        ropeB[0, p, CTX:] = cb[:, i]
        ropeB[1, p, CTX:] = -sb_[:, i] if d < 16 else sb_[:, i]
    return ropeA, ropeB


def _rpb_tab(rpb):
    qc = np.arange(64)
    cstart = np.clip(qc - 8, 0, 64 - 16)
    kc = np.arange(64)
    idx = kc[:, None] - qc[None, :] + 15
    inwin = (kc[:, None] >= cstart[None, :]) & (kc[:, None] < cstart[None, :] + 16)
    idxc = np.clip(idx, 0, 30)
    tab = rpb[:, :, idxc]
    tab = np.where(inwin[None, None], tab, np.float32(NEG)).astype(np.float32)
    return np.ascontiguousarray(tab)


_CACHE = {}


def run_model(inputs, S, kinds, n_cores=8):
    key = (S, tuple(kinds))
    if key not in _CACHE:
        _CACHE[key] = build(S, tuple(kinds))
    nc = _CACHE[key]
    ropeA, ropeB = _tables(S)
    ident = np.eye(128, dtype=np.float32)
    B = inputs["x"].shape[0]
    shared = {"ident": ident, "ropeA": ropeA, "ropeB": ropeB}
    for l, k in enumerate(kinds):
        p = "l%d_" % l
        for nm in ("w_mod", "b_mod", "g_norm", "w_gu", "w_down"):
            shared[p + nm] = np.ascontiguousarray(inputs[p + nm], dtype=np.float32)
        if k == 0:
            for nm in ("a_w_qkv", "a_w_o", "a_lam", "a_g_sub"):
                shared[p + nm] = np.ascontiguousarray(inputs[p + nm], dtype=np.float32)
        elif k == 1:
            for nm in ("b_w_in", "b_g_q", "b_g_kv", "b_w_uq", "b_w_ukv", "b_w_o"):
                shared[p + nm] = np.ascontiguousarray(inputs[p + nm], dtype=np.float32)
        else:
            for nm in ("c_w_qkv", "c_w_o"):
                shared[p + nm] = np.ascontiguousarray(inputs[p + nm], dtype=np.float32)
            shared[p + "c_rpb_tab"] = _rpb_tab(np.asarray(inputs[p + "c_rpb"], dtype=np.float32))
    in_maps = []
    for core in range(n_cores):
        b = core % B
        m = dict(shared)
        m["xc"] = np.ascontiguousarray(np.concatenate([inputs["ctx"][b], inputs["x"][b]], axis=0), dtype=np.float32)
        m["cc"] = np.ascontiguousarray(np.stack([inputs["c"][b], inputs["c_ctx"]], axis=0), dtype=np.float32)
        in_maps.append(m)
    res = run_bass_kernel_spmd(nc, in_maps, core_ids=list(range(n_cores)))
    out = np.stack([np.asarray(res.results[b]["y"]) for b in range(B)], axis=0)
    return out.astype(np.float32)


_INPUT_NAMES = (
    "x", "c", "ctx", "c_ctx",
    "l0_w_mod", "l0_b_mod", "l0_g_norm", "l0_w_gu", "l0_w_down", "l0_a_w_qkv", "l0_a_w_o", "l0_a_lam", "l0_a_g_sub",
    "l1_w_mod", "l1_b_mod", "l1_g_norm", "l1_w_gu", "l1_w_down", "l1_b_w_in", "l1_b_g_q", "l1_b_g_kv", "l1_b_w_uq",
    "l1_b_w_ukv", "l1_b_w_o",
    "l2_w_mod", "l2_b_mod", "l2_g_norm", "l2_w_gu", "l2_w_down", "l2_c_w_qkv", "l2_c_rpb", "l2_c_w_o",
    "l3_w_mod", "l3_b_mod", "l3_g_norm", "l3_w_gu", "l3_w_down", "l3_a_w_qkv", "l3_a_w_o", "l3_a_lam", "l3_a_g_sub",
)


def kernel(**inputs):
    assert all(n in inputs for n in _INPUT_NAMES)
    inputs = {k: np.asarray(v) for k, v in inputs.items()}
    return run_model(inputs, inputs["x"].shape[1], KINDS)
```
